# Optimizing a Trainium2 kernel written in Bass

```python
import math
import jax, jax.numpy as jnp
from jax import lax
import numpy as np

D_MODEL = 1024
BATCH = 8
SEQ = 2048
DEPTH = 2
DEC_BATCH = 128
DEC_SEQ = 8
PAST_LEN = 16384
PAGE_SIZE = 128

MIX_W = D_MODEL
SSM_W = MIX_W // 2
SSM_P = 16
SSM_G = SSM_W // SSM_P
SSM_N = 64
ATTN_W = MIX_W - SSM_W
HEAD_DIM = 64
N_HEADS = ATTN_W // HEAD_DIM
N_KV = 2
GRP = N_HEADS // N_KV
KV_W = N_KV * HEAD_DIM
IN_W = SSM_W + ATTN_W + 2 * KV_W
WINDOW = 128
D_FF = 2816
PLE_DIM = 256
EPS = 1e-6
NEG_INF = -1e30
DT_MIN = 1e-3
DT_MAX = 1e-1

kernel_name = 'hymba_s5_swa_sink_macaron_step'

F32 = jnp.float32


def rms_norm(x, g):
    xf = x.astype(F32)
    y = xf * lax.rsqrt(jnp.mean(xf * xf, axis=-1, keepdims=True) + EPS)
    return (y * g.astype(F32)).astype(x.dtype)


def swiglu(h, w_gate, w_up, w_down):
    return (jax.nn.silu(h @ w_gate) * (h @ w_up)) @ w_down


def _ssm_combine(e1, e2):
    a1, b1 = e1
    a2, b2 = e2
    return a1 * a2, a2[:, None] * b1 + b2


def ssm_scan(u, h0, lam_re, lam_im, log_dt, b_re, b_im, c_re, c_im, d, w_glu):
    bt, l, _ = u.shape
    lam = lax.complex(lam_re.astype(F32), lam_im.astype(F32))
    dt = jnp.exp(log_dt.astype(F32))[:, None]
    a_bar = jnp.exp(lam * dt)
    b = lax.complex(b_re.astype(F32), b_im.astype(F32))
    b_bar = ((a_bar - 1.0) / lam)[..., None] * b
    c = lax.complex(c_re.astype(F32), c_im.astype(F32))
    ug = u.astype(F32).reshape(bt, l, SSM_G, SSM_P).astype(jnp.complex64)
    bu = jnp.einsum('gnp,blgp->lbgn', b_bar, ug)
    if h0 is not None:
        bu = bu.at[0].add(a_bar * h0)
    a = jnp.broadcast_to(a_bar, (l, SSM_G, SSM_N))
    _, h = lax.associative_scan(_ssm_combine, (a, bu), axis=0)
    y = jnp.einsum('gpn,lbgn->blgp', c, h).real.reshape(bt, l, SSM_W)
    y = y + d.astype(F32) * u.astype(F32)
    y = jax.nn.gelu(y)
    y = y * jax.nn.sigmoid(y @ w_glu.astype(F32))
    return y.astype(u.dtype), h[-1]


def sink_softmax(scores, sinks):
    s = jnp.broadcast_to(sinks.astype(F32).reshape(N_KV, GRP, 1, 1), scores.shape[:-1] + (1,))
    return jax.nn.softmax(jnp.concatenate([scores, s], axis=-1), axis=-1)[..., :-1]


def band_mask(n_q, n_k, offset):
    diff = (jnp.arange(n_q)[:, None] + offset) - jnp.arange(n_k)[None, :]
    return (diff >= 0) & (diff < WINDOW)


def attn_prompt(q, k, v, sinks):
    b, l = q.shape[:2]
    nb = l // WINDOW
    qb = q.reshape(b, nb, WINDOW, N_KV, GRP, HEAD_DIM)
    kb = k.reshape(b, nb, WINDOW, N_KV, HEAD_DIM)
    vb = v.reshape(b, nb, WINDOW, N_KV, HEAD_DIM)
    pad = ((0, 0), (1, 0), (0, 0), (0, 0), (0, 0))
    kk = jnp.concatenate([jnp.pad(kb, pad)[:, :-1], kb], axis=2)
    vv = jnp.concatenate([jnp.pad(vb, pad)[:, :-1], vb], axis=2)
    band = band_mask(WINDOW, 2 * WINDOW, WINDOW)
    key_pos = jnp.arange(nb)[:, None, None] * WINDOW - WINDOW + jnp.arange(2 * WINDOW)[None, None, :]
    mask = band[None] & (key_pos >= 0)
    scores = jnp.einsum('bnqkgd,bnskd->bnkgqs', qb, kk).astype(F32) * (HEAD_DIM ** -0.5)
    scores = jnp.where(mask[None, :, None, None], scores, NEG_INF)
    probs = sink_softmax(scores, sinks)
    out = jnp.einsum('bnkgqs,bnskd->bnqkgd', probs.astype(v.dtype), vv)
    return out.reshape(b, l, ATTN_W)


def attn_sample(q, k, v, cache_k, cache_v, sinks):
    bt, s = q.shape[:2]
    kk = jnp.concatenate([cache_k.astype(k.dtype), k], axis=1)
    vv = jnp.concatenate([cache_v.astype(v.dtype), v], axis=1)
    qb = q.reshape(bt, s, N_KV, GRP, HEAD_DIM)
    mask = band_mask(s, WINDOW + s, WINDOW)
    scores = jnp.einsum('bqkgd,bskd->bkgqs', qb, kk).astype(F32) * (HEAD_DIM ** -0.5)
    scores = jnp.where(mask, scores, NEG_INF)
    probs = sink_softmax(scores, sinks)
    out = jnp.einsum('bkgqs,bskd->bqkgd', probs.astype(v.dtype), vv)
    return out.reshape(bt, s, ATTN_W), kk[:, -WINDOW:], vv[:, -WINDOW:]


def token_mixer(h, lw, h0, cache_k, cache_v):
    bt, l, _ = h.shape
    z = h @ lw['w_in']
    u = z[..., :SSM_W]
    q = z[..., SSM_W:SSM_W + ATTN_W].reshape(bt, l, N_HEADS, HEAD_DIM)
    k = z[..., SSM_W + ATTN_W:SSM_W + ATTN_W + KV_W].reshape(bt, l, N_KV, HEAD_DIM)
    v = z[..., SSM_W + ATTN_W + KV_W:].reshape(bt, l, N_KV, HEAD_DIM)
    y_ssm, h_last = ssm_scan(u, h0, lw['ssm_lam_re'], lw['ssm_lam_im'], lw['ssm_log_dt'],
                             lw['ssm_b_re'], lw['ssm_b_im'], lw['ssm_c_re'], lw['ssm_c_im'],
                             lw['ssm_d'], lw['ssm_w_glu'])
    if cache_k is None:
        y_attn = attn_prompt(q, k, v, lw['attn_sinks'])
        new_k, new_v = k[:, -WINDOW:], v[:, -WINDOW:]
    else:
        y_attn, new_k, new_v = attn_sample(q, k, v, cache_k, cache_v, lw['attn_sinks'])
    y = jnp.concatenate([rms_norm(y_ssm, lw['ssm_out_norm']), rms_norm(y_attn, lw['attn_out_norm'])], axis=-1)
    return y @ lw['w_out'], new_k, new_v, h_last


def layer(x, p, lw, h0, cache_k, cache_v):
    x = x + 0.5 * swiglu(rms_norm(x, lw['ffn1_norm']), lw['ffn1_w_gate'], lw['ffn1_w_up'], lw['ffn1_w_down'])
    mix, new_k, new_v, h_last = token_mixer(rms_norm(x, lw['mix_norm']), lw, h0, cache_k, cache_v)
    x = x + mix
    x = x + 0.5 * swiglu(rms_norm(x, lw['ffn2_norm']), lw['ffn2_w_gate'], lw['ffn2_w_up'], lw['ffn2_w_down'])
    gate = jax.nn.sigmoid((rms_norm(x, lw['ple_norm']) @ lw['ple_w_gate']).astype(F32))
    x = x + ((p @ lw['ple_w_proj']).astype(F32) * gate).astype(x.dtype)
    return x, new_k, new_v, h_last


def setup_inputs(seed: int = 0) -> dict:
    key = jax.random.key(seed)
    ks = iter(jax.random.split(key, 48))

    def nrm(shape, scale=1.0):
        return jax.random.normal(next(ks), shape, F32) * scale

    def gain(shape):
        return 1.0 + nrm(shape, 0.05)

    L = DEPTH
    inp = {}
    inp['x_prompt'] = nrm((BATCH, SEQ, D_MODEL))
    inp['x_sample'] = nrm((DEC_BATCH, DEC_SEQ, D_MODEL))
    inp['cache_k'] = nrm((L, DEC_BATCH, WINDOW, N_KV, HEAD_DIM))
    inp['cache_v'] = nrm((L, DEC_BATCH, WINDOW, N_KV, HEAD_DIM))
    inp['state_ssm_re'] = nrm((L, DEC_BATCH, SSM_G, SSM_N), 0.5)
    inp['state_ssm_im'] = nrm((L, DEC_BATCH, SSM_G, SSM_N), 0.5)
    inp['p_prompt'] = nrm((L, BATCH, SEQ, PLE_DIM))
    inp['p_sample'] = nrm((L, DEC_BATCH, DEC_SEQ, PLE_DIM))
    inp['ffn1_norm'] = gain((L, D_MODEL))
    inp['ffn1_w_gate'] = nrm((L, D_MODEL, D_FF), D_MODEL ** -0.5)
    inp['ffn1_w_up'] = nrm((L, D_MODEL, D_FF), D_MODEL ** -0.5)
    inp['ffn1_w_down'] = nrm((L, D_FF, D_MODEL), D_FF ** -0.5)
    inp['mix_norm'] = gain((L, D_MODEL))
    inp['w_in'] = nrm((L, D_MODEL, IN_W), D_MODEL ** -0.5)
    inp['ssm_lam_re'] = -0.5 + nrm((L, SSM_G, SSM_N), 0.01)
    inp['ssm_lam_im'] = jnp.pi * jnp.arange(SSM_N, dtype=F32) + nrm((L, SSM_G, SSM_N), 0.01)
    inp['ssm_log_dt'] = jax.random.uniform(next(ks), (L, SSM_G), F32, math.log(DT_MIN), math.log(DT_MAX))
    inp['ssm_b_re'] = nrm((L, SSM_G, SSM_N, SSM_P), (2 * SSM_P) ** -0.5)
    inp['ssm_b_im'] = nrm((L, SSM_G, SSM_N, SSM_P), (2 * SSM_P) ** -0.5)
    inp['ssm_c_re'] = nrm((L, SSM_G, SSM_P, SSM_N), (2 * SSM_N) ** -0.5)
    inp['ssm_c_im'] = nrm((L, SSM_G, SSM_P, SSM_N), (2 * SSM_N) ** -0.5)
    inp['ssm_d'] = nrm((L, SSM_W))
    inp['ssm_w_glu'] = nrm((L, SSM_W, SSM_W), SSM_W ** -0.5)
    inp['ssm_out_norm'] = gain((L, SSM_W))
    inp['attn_sinks'] = nrm((L, N_HEADS), 0.5)
    inp['attn_out_norm'] = gain((L, ATTN_W))
    inp['w_out'] = nrm((L, MIX_W, D_MODEL), MIX_W ** -0.5)
    inp['ffn2_norm'] = gain((L, D_MODEL))
    inp['ffn2_w_gate'] = nrm((L, D_MODEL, D_FF), D_MODEL ** -0.5)
    inp['ffn2_w_up'] = nrm((L, D_MODEL, D_FF), D_MODEL ** -0.5)
    inp['ffn2_w_down'] = nrm((L, D_FF, D_MODEL), D_FF ** -0.5)
    inp['ple_norm'] = gain((L, D_MODEL))
    inp['ple_w_gate'] = nrm((L, D_MODEL, D_MODEL), D_MODEL ** -0.5)
    inp['ple_w_proj'] = nrm((L, PLE_DIM, D_MODEL), PLE_DIM ** -0.5)
    inp['final_norm'] = gain((D_MODEL,))
    return inp


def reference(x_prompt, x_sample, cache_k, cache_v, state_ssm_re, state_ssm_im, p_prompt, p_sample,
              ffn1_norm, ffn1_w_gate, ffn1_w_up, ffn1_w_down, mix_norm, w_in,
              ssm_lam_re, ssm_lam_im, ssm_log_dt, ssm_b_re, ssm_b_im, ssm_c_re, ssm_c_im,
              ssm_d, ssm_w_glu, ssm_out_norm, attn_sinks, attn_out_norm, w_out,
              ffn2_norm, ffn2_w_gate, ffn2_w_up, ffn2_w_down, ple_norm, ple_w_gate, ple_w_proj,
              final_norm):
    stacked = dict(ffn1_norm=ffn1_norm, ffn1_w_gate=ffn1_w_gate, ffn1_w_up=ffn1_w_up, ffn1_w_down=ffn1_w_down,
                   mix_norm=mix_norm, w_in=w_in, ssm_lam_re=ssm_lam_re, ssm_lam_im=ssm_lam_im,
                   ssm_log_dt=ssm_log_dt, ssm_b_re=ssm_b_re, ssm_b_im=ssm_b_im, ssm_c_re=ssm_c_re,
                   ssm_c_im=ssm_c_im, ssm_d=ssm_d, ssm_w_glu=ssm_w_glu, ssm_out_norm=ssm_out_norm,
                   attn_sinks=attn_sinks, attn_out_norm=attn_out_norm, w_out=w_out,
                   ffn2_norm=ffn2_norm, ffn2_w_gate=ffn2_w_gate, ffn2_w_up=ffn2_w_up, ffn2_w_down=ffn2_w_down,
                   ple_norm=ple_norm, ple_w_gate=ple_w_gate, ple_w_proj=ple_w_proj)
    xp, xs = x_prompt, x_sample
    kp_l, vp_l, hp_l, ks_l, vs_l, hs_l = [], [], [], [], [], []
    for i in range(DEPTH):
        lw = {name: arr[i] for name, arr in stacked.items()}
        xp, kp, vp, hp = layer(xp, p_prompt[i], lw, None, None, None)
        h0 = lax.complex(state_ssm_re[i].astype(F32), state_ssm_im[i].astype(F32))
        xs, ks_, vs_, hs = layer(xs, p_sample[i], lw, h0, cache_k[i], cache_v[i])
        kp_l.append(kp); vp_l.append(vp); hp_l.append(hp)
        ks_l.append(ks_); vs_l.append(vs_); hs_l.append(hs)
    y_prompt = rms_norm(xp, final_norm)
    y_sample = rms_norm(xs, final_norm)
    hp_all = jnp.stack(hp_l)
    hs_all = jnp.stack(hs_l)
    return (y_prompt, y_sample,
            jnp.stack(kp_l), jnp.stack(vp_l), hp_all.real, hp_all.imag,
            jnp.stack(ks_l), jnp.stack(vs_l), hs_all.real, hs_all.imag)
```

```python
import math
from contextlib import ExitStack
import numpy as np
import concourse.bass as bass
import concourse.mybir as mybir
from concourse.bass_utils import run_bass_kernel_spmd

F32 = mybir.dt.float32
BF16 = mybir.dt.bfloat16
AF = mybir.ActivationFunctionType
ALU = mybir.AluOpType
AX = mybir.AxisListType

ENGS = ("pe", "dve", "act", "pool", "sp")
NCORES = 8
D = 1024
DFF = 2816
NTOK = 2176
TT = [(0, 512), (512, 512), (1024, 512), (1536, 512), (2048, 128)]
EPS = 1e-6
NEG = -1e30


class Res:
    __slots__ = ("name", "w", "r", "excl")

    def __init__(self, name):
        self.name = name
        self.w = None
        self.r = {}
        self.excl = name[0] == "ps"


class Sched:
    def __init__(self, nc):
        self.nc = nc
        self.ops = {e: [] for e in ENGS}
        self.trace = {e: [] for e in ENGS}
        self.cnt = {}
        self.sems = {}
        self.waited = {e: {} for e in ENGS}
        self.res = {}

    def sem(self, key):
        if key not in self.cnt:
            self.cnt[key] = 0
        return key

    def R(self, *key):
        r = self.res.get(key)
        if r is None:
            r = Res(key)
            self.res[key] = r
        return r

    def _deps(self, reads, writes):
        deps = {}

        def add(k, v):
            if deps.get(k, 0) < v:
                deps[k] = v
        for r in reads:
            if r.w:
                add(*r.w)
        for r in writes:
            if r.w:
                add(*r.w)
            for k, v in r.r.items():
                add(k, v)
        return deps

    def _emit_waits(self, eng, deps):
        wd = self.waited[eng]
        for k, v in deps.items():
            if wd.get(k, 0) >= v:
                continue
            if k == eng and (eng == "pe" or v > self.cnt.get(eng, 0)):
                continue
            wd[k] = v
            self.sem(k)
            self.ops[eng].append(lambda e, k=k, v=v: e.wait_ge(self.sems[k], v))
            self.trace[eng].append(("w", k, v))

    def _commit(self, tok, reads, writes):
        k, v = tok
        for r in reads:
            if r.r.get(k, 0) < v:
                r.r[k] = v
        for r in writes:
            r.w = tok
            r.r = {}

    @staticmethod
    def _flat(xs):
        out = []
        for x in xs:
            if isinstance(x, (list, tuple)):
                out.extend(Sched._flat(x))
            else:
                out.append(x)
        return out

    def op(self, eng, fn, reads=(), writes=(), sig=True):
        reads, writes = self._flat(reads), self._flat(writes)
        ex = [r for r in reads if r.excl]
        if ex:
            writes = list(writes) + ex
        self._emit_waits(eng, self._deps(reads, writes))
        self.sem(eng)
        if sig:
            self.cnt[eng] += 1
            self.ops[eng].append(lambda e, fn=fn, k=eng: fn(e).then_inc(self.sems[k], 1))
            self.trace[eng].append(("i", eng, 1))
            self._commit((eng, self.cnt[eng]), reads, writes)
        else:
            self.ops[eng].append(lambda e, fn=fn: fn(e))
            self._commit((eng, self.cnt[eng] + 1), reads, writes)

    def dma(self, q, fn, semkey, reads=(), writes=()):
        reads, writes = self._flat(reads), self._flat(writes)
        semkey = ("d",) + tuple((writes[0] if writes else reads[0]).name)
        self._emit_waits(q, self._deps(reads, writes))
        self.sem(semkey)
        self.cnt[semkey] += 16
        self.ops[q].append(lambda e, fn=fn, k=semkey: fn(e).then_inc(self.sems[k], 16))
        self.trace[q].append(("i", semkey, 16))
        self._commit((semkey, self.cnt[semkey]), reads, writes)

    def wait_all(self, eng):
        deps = {}
        for r in self.res.values():
            toks = list(r.r.items()) + ([r.w] if r.w else [])
            for k, v in toks:
                if deps.get(k, 0) < v:
                    deps[k] = v
        self._emit_waits(eng, deps)

    def check_deadlock(self):
        val = {k: 0 for k in self.cnt}
        pos = {e: 0 for e in ENGS}
        progress = True
        while progress:
            progress = False
            for e in ENGS:
                tr = self.trace[e]
                while pos[e] < len(tr):
                    kind, k, v = tr[pos[e]]
                    if kind == "w":
                        if val[k] < v:
                            break
                    else:
                        val[k] += v
                    pos[e] += 1
                    progress = True
        for e in ENGS:
            if pos[e] < len(self.trace[e]):
                raise RuntimeError("DEADLOCK: %s stuck at %s" % (e, self.trace[e][pos[e]],))

    def emit(self, stack):
        ops = self.ops
        self.check_deadlock()
        for i, key in enumerate(self.cnt):
            self.sems[key] = stack.enter_context(self.nc.semaphore("sem%d" % i))
        block = stack.enter_context(self.nc.Block())

        @block.tensor
        def _(e):
            for f in ops["pe"]:
                f(e)

        @block.vector
        def _(e):
            for f in ops["dve"]:
                f(e)

        @block.scalar
        def _(e):
            for f in ops["act"]:
                f(e)

        @block.gpsimd
        def _(e):
            for f in ops["pool"]:
                f(e)

        @block.sync
        def _(e):
            for f in ops["sp"]:
                f(e)


WNAMES = ["ffn1_w_gate", "ffn1_w_up", "ffn1_w_down", "w_in", "ssm_w_glu", "w_out",
          "ffn2_w_gate", "ffn2_w_up", "ffn2_w_down", "ple_w_gate", "ple_w_proj"]
WSHAPES = {"ffn1_w_gate": [2, D, DFF], "ffn1_w_up": [2, D, DFF], "ffn1_w_down": [2, DFF, D], "w_in": [2, D, 1280],
           "ssm_w_glu": [2, 512, 512], "w_out": [2, D, D], "ffn2_w_gate": [2, D, DFF], "ffn2_w_up": [2, D, DFF],
           "ffn2_w_down": [2, DFF, D], "ple_w_gate": [2, D, D], "ple_w_proj": [2, 256, D]}
VNAMES = {"ffn1_norm": [2, D], "mix_norm": [2, D], "ffn2_norm": [2, D], "ple_norm": [2, D], "final_norm": [D],
          "ssm_out_norm": [2, 512], "attn_out_norm": [2, 512], "attn_sinks": [2, 8],
          "ssm_lam_re": [2, 32, 64], "ssm_lam_im": [2, 32, 64], "ssm_log_dt": [2, 32],
          "ssm_b_re": [2, 32, 64, 16], "ssm_b_im": [2, 32, 64, 16], "ssm_c_re": [2, 32, 16, 64],
          "ssm_c_im": [2, 32, 16, 64], "ssm_d": [2, 512]}
CNAMES = {"c_ident": [128, 128], "c_amask": [128, 256], "c_smask": [128, 136], "c_maskT": [128, 128],
          "c_sel": [128, 512]}
INAMES = {"x_p": [2048, D], "x_s": [128, D], "p_p": [2, 2048, 256], "p_s": [2, 128, 256],
          "ck": [2, 16, 128, 128], "cv": [2, 16, 128, 128], "st_re": [2, 16, 2048], "st_im": [2, 16, 2048]}
ONAMES = {"y": [NTOK, D], "kp": [2, 128, 128], "vp": [2, 128, 128], "hp_re": [2, 2048], "hp_im": [2, 2048],
          "ks": [2, 16, 128, 128], "vs": [2, 16, 128, 128], "hs_re": [2, 16, 2048], "hs_im": [2, 16, 2048]}


def build(cfg):
    nc = bass.Bass("TRN2", target_bir_lowering=False)
    dr = {}
    for n, s in list(WSHAPES.items()) + list(VNAMES.items()) + list(CNAMES.items()) + list(INAMES.items()):
        dr[n] = nc.dram_tensor(n, s, F32, kind="ExternalInput").ap()
    for n, s in ONAMES.items():
        dr[n] = nc.dram_tensor(n, s, F32, kind="ExternalOutput").ap()
    st = ExitStack()
    with st:
        S = Sched(nc)
        R = S.R

        def T(name, shape, dt):
            return st.enter_context(nc.sbuf_tensor(name, shape, dt))

        X = T("X", [128, 8, NTOK], F32)
        HT = T("HT", [128, 8, NTOK], BF16)
        R1 = T("R1", [128, 6 * NTOK], BF16)
        WP = T("WP", [128, 3, 4096], BF16)
        RG = T("RG", [128, 4 * NTOK], BF16)
        SPR = T("SPR", [128, 10240], BF16)
        identF = T("identF", [128, 128], F32)
        identB = T("identB", [128, 128], BF16)
        onesB = T("onesB", [128, 128], BF16)
        amask = T("amask", [128, 256], F32)
        amaskB = T("amaskB", [128, 256], BF16)
        smask = T("smask", [128, 136], F32)
        GN = T("GN", [128, 9, 8], F32)
        GN2 = T("GN2", [128, 4, 4], F32)
        SK = T("SK", [128, 16], F32)
        SM = T("SM", [128, 64], F32)
        SV = T("SV", [128, 48, 16], F32)
        E0 = T("E0", [128, 2, 16, 32], F32)
        E1 = T("E1", [128, 2, 16, 16], F32)
        H0 = T("H0", [128, 2, 16, 16], F32)
        SelC = T("SelC", [128, 4, 128], BF16)
        PTMP = T("PTMP", [128, 512], F32)
        SVL = T("SVL", [128, 8, 16], F32)
        maskT = T("maskT", [128, 128], F32)
        PS = [st.enter_context(nc.psum_tensor("ps%d" % i, [128, 512], F32)) for i in range(8)]

        def PR(i):
            return R("ps", i)

        def r1(a, b):
            return [R("R1b", k) for k in range(a // 32, (b - 1) // 32 + 1)]

        def R1r(c, t):
            return r1(c * NTOK + TT[t][0], c * NTOK + TT[t][0] + TT[t][1])

        def rg(a, b):
            return [R("RG", k) for k in range(a // 64, (b - 1) // 64 + 1)]

        def spr(a, b):
            return [R("SPR", k) for k in range(a // 64, (b - 1) // 64 + 1)]

        def XR(c, t):
            return R("X", c, t)

        def HR(c, t):
            return R("HT", c, t)

        wq = {"n": 0}

        def wload(dmas):
            s = wq["n"] % 3
            wq["n"] += 1
            slot = WP[:, s, :]
            for ov, ia in dmas:
                S.dma("pool", lambda e, o=ov(slot), i=ia: e.dma_start(out=o, in_=i), ("dw", s), writes=[R("wp", s)])
            return slot, R("wp", s)

        def wview(l, name):
            return dr[name][l].rearrange("(c p) f -> p c f", p=128)

        def load_consts():
            S.dma("sp", lambda e: e.dma_start(out=identF[:], in_=dr["c_ident"]), "d_c", writes=[R("identF")])
            S.dma("sp", lambda e: e.dma_start(out=amask[:], in_=dr["c_amask"]), "d_c", writes=[R("amask")])
            S.dma("sp", lambda e: e.dma_start(out=smask[:], in_=dr["c_smask"]), "d_c", writes=[R("smask")])
            S.dma("sp", lambda e: e.dma_start(out=maskT[:], in_=dr["c_maskT"]), "d_c", writes=[R("maskT")])
            S.dma("pool", lambda e: e.dma_start(out=SelC[:].rearrange("p a b -> p (a b)"), in_=dr["c_sel"]), "d_sel", writes=[R("SelC")])
            S.op("dve", lambda e: e.tensor_copy(out=identB[:], in_=identF[:]), reads=[R("identF")], writes=[R("identB")])
            S.op("dve", lambda e: e.memset(onesB[:], 1.0), writes=[R("onesB")])
            S.op("dve", lambda e: e.tensor_copy(out=amaskB[:], in_=amask[:]), reads=[R("amask")], writes=[R("amaskB")])
            for l in range(2):
                for k, n in enumerate(["ffn1_norm", "mix_norm", "ffn2_norm", "ple_norm"]):
                    S.dma("sp", lambda e, l=l, k=k, n=n: e.dma_start(
                        out=GN[:, l * 4 + k, :], in_=dr[n][l].rearrange("(c p) -> p c", p=128),
                        allow_slow_non_contiguous=True), "d_c", writes=[R("GN")])
                for k, n in enumerate(["ssm_out_norm", "attn_out_norm"]):
                    S.dma("sp", lambda e, l=l, k=k, n=n: e.dma_start(
                        out=GN2[:, l * 2 + k, :], in_=dr[n][l].rearrange("(c p) -> p c", p=128),
                        allow_slow_non_contiguous=True), "d_c", writes=[R("GN")])
            S.dma("sp", lambda e: e.dma_start(out=GN[:, 8, :], in_=dr["final_norm"].rearrange("(c p) -> p c", p=128),
                                              allow_slow_non_contiguous=True), "d_c", writes=[R("GN")])
            S.dma("sp", lambda e: e.dma_start(out=SK[:], in_=dr["attn_sinks"].rearrange("l h -> (l h)").partition_broadcast(128)),
                  "d_c", writes=[R("SK")])

        def load_x():
            xin = [RG[:, 2048 * b:2048 * b + 2048].bitcast(F32) for b in range(4)]
            for i in range(17):
                b = i % 4
                src = dr["x_p"][128 * i:128 * i + 128, :] if i < 16 else dr["x_s"]
                S.dma("sp", lambda e, b=b, src=src: e.dma_start(out=xin[b], in_=src), ("d_xin", b), writes=rg(2048 * b, 2048 * b + 2048))
                t = i // 4 if i < 16 else 4
                for h in range(2):
                    pb = 6 + h
                    for cc in range(4):
                        c = h * 4 + cc
                        S.op("pe", lambda e, b=b, c=c, cc=cc, pb=pb: e.transpose(PS[pb][:, cc * 128:cc * 128 + 128],
                                                                               xin[b][:, c * 128:c * 128 + 128], identF[:]),
                             reads=rg(2048 * b, 2048 * b + 2048) + [R("identF")], writes=[PR(pb)], sig=(cc == 3))
                    eng = "dve" if h == 0 else "act"
                    dst = X[:, h * 4:h * 4 + 4, 128 * i:128 * i + 128]
                    srcp = PS[pb][:, :].rearrange("p (c n) -> p c n", c=4)
                    if eng == "dve":
                        S.op("dve", lambda e, dst=dst, srcp=srcp: e.tensor_copy(out=dst, in_=srcp),
                             reads=[PR(pb)], writes=[XR(h * 4 + cc, t) for cc in range(4)])
                    else:
                        S.op("act", lambda e, dst=dst, srcp=srcp: e.activation(out=dst, in_=srcp, func=AF.Copy),
                             reads=[PR(pb)], writes=[XR(h * 4 + cc, t) for cc in range(4)])

        def rmsnorm(src, srcR, dst, dstR, nch, gain, tiles=range(5), psb=5):
            SQ = [RG[:, 4096:4608], RG[:, 4608:5120]]
            RS = RG[:, 5120:6144].bitcast(F32)
            for t in tiles:
                lo, n = TT[t]
                for c in range(nch):
                    b = c % 2
                    S.op("act", lambda e, c=c, b=b, lo=lo, n=n: e.activation(out=SQ[b][:, 0:n], in_=src(c, lo, n), func=AF.Square),
                         reads=[srcR(c, t)], writes=rg(4096 + 512 * b, 4608 + 512 * b))
                    S.op("pe", lambda e, c=c, b=b, n=n: e.matmul(PS[psb][:, 0:n], lhsT=onesB[:], rhs=SQ[b][:, 0:n],
                                                                 start=(c == 0), stop=(c == nch - 1)),
                         reads=rg(4096 + 512 * b, 4608 + 512 * b) + [R("onesB")], writes=[PR(psb)], sig=True)
                S.op("act", lambda e, n=n: e.activation(out=RS[:, 0:n], in_=PS[psb][:, 0:n], func=AF.Sqrt,
                                                         scale=1.0 / (128 * nch), bias=EPSB[:, 0:1]),
                     reads=[PR(psb), R("EPSB")], writes=rg(5120, 6144))
                S.op("dve", lambda e, n=n: e.reciprocal(out=RS[:, 0:n], in_=RS[:, 0:n]), reads=rg(5120, 6144), writes=rg(5120, 6144))
                for c in range(nch):
                    S.op("dve", lambda e, c=c, lo=lo, n=n: e.scalar_tensor_tensor(
                        out=dst(c, lo, n), in0=src(c, lo, n), scalar=gain(c), in1=RS[:, 0:n], op0=ALU.mult, op1=ALU.mult),
                        reads=[srcR(c, t), R("GN")] + rg(5120, 6144), writes=[dstR(c, t)])

        EPSB = T("EPSB", [128, 1], F32)
        S.op("dve", lambda e: e.memset(EPSB[:], EPS), writes=[R("EPSB")])

        def norm_X(gi):
            rmsnorm(lambda c, lo, n: X[:, c, lo:lo + n], XR, lambda c, lo, n: HT[:, c, lo:lo + n], HR, 8,
                    lambda c: GN[:, gi, c:c + 1])

        def ffn(l, pre, mid_hook=None):
            norm_X(l * 4 + (0 if pre == "ffn1" else 2))
            A = R1[:, :].rearrange("p (c n) -> p c n", c=6)
            SG = [RG[:, 6144:7168].bitcast(F32), RG[:, 7168:8192].bitcast(F32)]
            quarters = [[0, 1, 2], [3, 4, 5], [6, 7, 8], [9, 10]]
            wg, wu, wd = wview(l, pre + "_w_gate"), wview(l, pre + "_w_up"), dr[pre + "_w_down"][l]
            k = 0
            for q in quarters:
                if q is quarters[1] and mid_hook is not None:
                    mid_hook()
                for pi, jp in enumerate(q):
                    f0 = jp * 256
                    slot, sr = wload([
                        (lambda s: s[:, 0:2048].rearrange("p (c f) -> p c f", c=8), wg[:, :, f0:f0 + 256]),
                        (lambda s: s[:, 2048:4096].rearrange("p (c f) -> p c f", c=8), wu[:, :, f0:f0 + 256])])
                    sv = slot.rearrange("p (g c f) -> p g c f", g=2, c=8)
                    for fc in range(2):
                        ai = pi * 2 + fc
                        for t in range(5):
                            lo, n = TT[t]
                            pg, pu = k % 2, 2 + k % 2
                            for c in range(8):
                                S.op("pe", lambda e, c=c, fc=fc, pg=pg, lo=lo, n=n, sv=sv: e.matmul(
                                    PS[pg][:, 0:n], lhsT=sv[:, 0, c, fc * 128:fc * 128 + 128], rhs=HT[:, c, lo:lo + n],
                                    start=(c == 0), stop=(c == 7)), reads=[sr, HR(c, t)], writes=[PR(pg)], sig=(c == 7))
                            for c in range(8):
                                S.op("pe", lambda e, c=c, fc=fc, pu=pu, lo=lo, n=n, sv=sv: e.matmul(
                                    PS[pu][:, 0:n], lhsT=sv[:, 1, c, fc * 128:fc * 128 + 128], rhs=HT[:, c, lo:lo + n],
                                    start=(c == 0), stop=(c == 7)), reads=[sr, HR(c, t)], writes=[PR(pu)], sig=(c == 7))
                            S.op("act", lambda e, pg=pg, n=n, b=k % 2: e.activation(out=SG[b][:, 0:n], in_=PS[pg][:, 0:n], func=AF.Silu),
                                 reads=[PR(pg)], writes=rg(6144 + 1024 * (k % 2), 7168 + 1024 * (k % 2)))
                            S.op("dve", lambda e, pu=pu, n=n, lo=lo, ai=ai, b=k % 2: e.tensor_tensor(
                                out=A[:, ai, lo:lo + n], in0=PS[pu][:, 0:n], in1=SG[b][:, 0:n], op=ALU.mult),
                                reads=[PR(pu)] + rg(6144 + 1024 * (k % 2), 7168 + 1024 * (k % 2)), writes=[R1r(ai, t)])
                            k += 1
                nj = 2 * len(q)
                r0 = q[0] * 256
                for dh in range(2):
                    slot, sr = wload([(lambda s, nj=nj: s[:, 0:nj * 512].rearrange("p (j d) -> p j d", j=nj),
                                       wd[r0:r0 + nj * 128, dh * 512:dh * 512 + 512].rearrange("(j p) d -> p j d", p=128))])
                    sv = slot[:, 0:nj * 512].rearrange("p (j d) -> p j d", j=nj)
                    for i in range(4):
                        dc = dh * 4 + i
                        for t in range(5):
                            lo, n = TT[t]
                            pd = 4 + k % 2
                            for jj in range(nj):
                                S.op("pe", lambda e, jj=jj, i=i, pd=pd, lo=lo, n=n, sv=sv, nj=nj: e.matmul(
                                    PS[pd][:, 0:n], lhsT=sv[:, jj, i * 128:i * 128 + 128], rhs=A[:, jj, lo:lo + n],
                                    start=(jj == 0), stop=(jj == nj - 1)), reads=[sr, R1r(jj, t)], writes=[PR(pd)], sig=(jj == nj - 1))
                            S.op("dve", lambda e, pd=pd, dc=dc, lo=lo, n=n: e.scalar_tensor_tensor(
                                out=X[:, dc, lo:lo + n], in0=PS[pd][:, 0:n], scalar=0.5, in1=X[:, dc, lo:lo + n],
                                op0=ALU.mult, op1=ALU.add), reads=[PR(pd), XR(dc, t)], writes=[XR(dc, t)])
                            k += 1

        def ple_load(l):
            PT = SPR[:, 0:2 * NTOK].rearrange("p (c n) -> p c n", c=2)
            pin = [RG[:, 512 * b:512 * b + 512].bitcast(F32) for b in range(8)]
            def pdma(i):
                b = i % 8
                src = dr["p_p"][l, 128 * i:128 * i + 128, :] if i < 16 else dr["p_s"][l]
                S.dma("sp", lambda e, b=b, src=src: e.dma_start(out=pin[b], in_=src), None, writes=rg(512 * b, 512 * b + 512))
            for i in range(8):
                pdma(i)
            for i in range(17):
                b = i % 8
                for c in range(2):
                    S.op("pe", lambda e, b=b, c=c: e.transpose(PS[7][:, c * 128:c * 128 + 128], pin[b][:, c * 128:c * 128 + 128], identF[:]),
                         reads=rg(512 * b, 512 * b + 512) + [R("identF")], writes=[PR(7)], sig=(c == 1))
                S.op("act", lambda e, i=i: e.activation(out=PT[:, :, 128 * i:128 * i + 128],
                                                         in_=PS[7][:, 0:256].rearrange("p (c n) -> p c n", c=2), func=AF.Copy),
                     reads=[PR(7)], writes=spr(128 * i, 128 * i + 128) + spr(NTOK + 128 * i, NTOK + 128 * i + 128))
                if i + 8 < 17:
                    pdma(i + 8)

        def ple(l):
            norm_X(l * 4 + 3)
            PT = SPR[:, 0:2 * NTOK].rearrange("p (c n) -> p c n", c=2)
            SG = [RG[:, 6144:7168].bitcast(F32), RG[:, 7168:8192].bitcast(F32)]
            wp_ = dr["ple_w_proj"][l].rearrange("(c p) f -> p c f", p=128)
            wg = wview(l, "ple_w_gate")
            k = 0
            for dh in range(2):
                pslot, psr = wload([(lambda s: s[:, 0:1024].rearrange("p (c f) -> p c f", c=2), wp_[:, :, dh * 512:dh * 512 + 512])])
                pv = pslot[:, 0:1024].rearrange("p (c f) -> p c f", c=2)
                gslot, gsr = wload([(lambda s: s[:, 0:4096].rearrange("p (c f) -> p c f", c=8), wg[:, :, dh * 512:dh * 512 + 512])])
                gv = gslot.rearrange("p (c f) -> p c f", c=8)
                for i in range(4):
                    dc = dh * 4 + i
                    for t in range(5):
                        lo, n = TT[t]
                        pg, pp = k % 2, 2 + k % 2
                        for c in range(8):
                            S.op("pe", lambda e, c=c, i=i, pg=pg, lo=lo, n=n, gv=gv: e.matmul(
                                PS[pg][:, 0:n], lhsT=gv[:, c, i * 128:i * 128 + 128], rhs=HT[:, c, lo:lo + n],
                                start=(c == 0), stop=(c == 7)), reads=[gsr, HR(c, t)], writes=[PR(pg)], sig=(c == 7))
                        for c in range(2):
                            S.op("pe", lambda e, c=c, i=i, pp=pp, lo=lo, n=n, pv=pv: e.matmul(
                                PS[pp][:, 0:n], lhsT=pv[:, c, i * 128:i * 128 + 128], rhs=PT[:, c, lo:lo + n],
                                start=(c == 0), stop=(c == 1)), reads=[psr] + spr(c * NTOK + lo, c * NTOK + lo + n), writes=[PR(pp)], sig=(c == 1))
                        b = k % 2
                        S.op("act", lambda e, pg=pg, n=n, b=b: e.activation(out=SG[b][:, 0:n], in_=PS[pg][:, 0:n], func=AF.Sigmoid),
                             reads=[PR(pg)], writes=rg(6144 + 1024 * b, 7168 + 1024 * b))
                        S.op("dve", lambda e, pp=pp, n=n, b=b: e.tensor_tensor(out=SG[b][:, 0:n], in0=PS[pp][:, 0:n], in1=SG[b][:, 0:n], op=ALU.mult),
                             reads=[PR(pp)] + rg(6144 + 1024 * b, 7168 + 1024 * b), writes=rg(6144 + 1024 * b, 7168 + 1024 * b))
                        S.op("dve", lambda e, dc=dc, lo=lo, n=n, b=b: e.tensor_tensor(out=X[:, dc, lo:lo + n], in0=X[:, dc, lo:lo + n], in1=SG[b][:, 0:n], op=ALU.add),
                             reads=[XR(dc, t)] + rg(6144 + 1024 * b, 7168 + 1024 * b), writes=[XR(dc, t)])
                        k += 1

        MO = HT

        def MR(c, t):
            return R("HT", c, t)

        def mixer(l):
            norm_X(l * 4 + 1)
            QT = R1[:, 0:4 * NTOK].rearrange("p (c n) -> p c n", c=4)
            KD = R1[:, 4 * NTOK:6 * NTOK].rearrange("p (c n) -> p c n", c=2)
            VT = SPR[:, 0:17 * 128].rearrange("p (i f) -> p i f", i=17)
            KVO = SPR[:, 2176:2176 + 1024].bitcast(F32).rearrange("p (i f) -> p i f", i=2)
            win = wview(l, "w_in")
            if cfg.get("ssm", 1):
                ssm(l)
            qslot, qsr = wload([(lambda s: s.rearrange("p (c f) -> p c f", c=8), win[:, :, 512:1024])])
            qv = qslot.rearrange("p (c f) -> p c f", c=8)

            def kvd(s, a, b):
                return s.rearrange("p (c f) -> p c f", c=8)[:, :, a:b]
            kslot, ksr = wload([
                (lambda s: kvd(s, 0, 64), win[:, :, 1024:1088]), (lambda s: kvd(s, 64, 128), win[:, :, 1024:1088]),
                (lambda s: kvd(s, 128, 192), win[:, :, 1088:1152]), (lambda s: kvd(s, 192, 256), win[:, :, 1088:1152]),
                (lambda s: kvd(s, 256, 384), win[:, :, 1152:1280]), (lambda s: kvd(s, 384, 512), win[:, :, 1024:1152])])
            kv_ = kslot.rearrange("p (c f) -> p c f", c=8)
            k = 0
            for qc in range(4):
                for t in range(5):
                    lo, n = TT[t]
                    pb = k % 2
                    for c in range(8):
                        S.op("pe", lambda e, c=c, qc=qc, pb=pb, lo=lo, n=n: e.matmul(
                            PS[pb][:, 0:n], lhsT=qv[:, c, qc * 128:qc * 128 + 128], rhs=HT[:, c, lo:lo + n],
                            start=(c == 0), stop=(c == 7)), reads=[qsr, HR(c, t)], writes=[PR(pb)], sig=(c == 7))
                    S.op("act", lambda e, qc=qc, pb=pb, lo=lo, n=n: e.activation(out=QT[:, qc, lo:lo + n], in_=PS[pb][:, 0:n], func=AF.Copy),
                         reads=[PR(pb)], writes=[R1r(qc, t)])
                    k += 1
            for kc in (range(2) if cfg.get("mixlevel", 9) >= 2 else []):
                for t in range(5):
                    lo, n = TT[t]
                    pb = k % 2
                    for c in range(8):
                        S.op("pe", lambda e, c=c, kc=kc, pb=pb, lo=lo, n=n: e.matmul(
                            PS[pb][:, 0:n], lhsT=kv_[:, c, kc * 128:kc * 128 + 128], rhs=HT[:, c, lo:lo + n],
                            start=(c == 0), stop=(c == 7)), reads=[ksr, HR(c, t)], writes=[PR(pb)], sig=(c == 7))
                    S.op("act", lambda e, kc=kc, pb=pb, lo=lo, n=n: e.activation(out=KD[:, kc, lo:lo + n], in_=PS[pb][:, 0:n], func=AF.Copy),
                         reads=[PR(pb)], writes=[R1r(4 + kc, t)])
                    k += 1
            for i in (range(17) if cfg.get("mixlevel", 9) >= 3 else []):
                t = i // 4 if i < 16 else 4
                pb = k % 2
                for c in range(8):
                    S.op("pe", lambda e, c=c, i=i, pb=pb: e.matmul(
                        PS[pb][:, 0:256], lhsT=HT[:, c, 128 * i:128 * i + 128], rhs=kv_[:, c, 256:512],
                        start=(c == 0), stop=(c == 7)), reads=[ksr, HR(c, t)], writes=[PR(pb)], sig=(c == 7))
                S.op("act", lambda e, i=i, pb=pb: e.activation(out=VT[:, i, :], in_=PS[pb][:, 0:128], func=AF.Copy),
                     reads=[PR(pb)], writes=spr(128 * i, 128 * i + 128))
                if i >= 15:
                    S.op("dve", lambda e, i=i, pb=pb: e.tensor_copy(out=KVO[:, i - 15, :], in_=PS[pb][:, 0:256]),
                         reads=[PR(pb)], writes=spr(2176 + 512 * (i - 15), 2176 + 512 * (i - 15) + 512))
                k += 1
            VNa = [SPR[:, 7296:8320], SPR[:, 9024:10048]]
            rVN = spr(7296, 8320) + spr(9024, 10048)
            for s4 in (range(4) if cfg.get("mixlevel", 9) >= 3 else []):
                for sl in range(4):
                    s = s4 * 4 + sl
                    for c in range(8):
                        S.op("pe", lambda e, c=c, s=s, sl=sl: e.matmul(
                            PS[6][0:8, sl * 128:sl * 128 + 128], lhsT=HT[:, c, 2048 + 8 * s:2048 + 8 * s + 8], rhs=kv_[:, c, 256:384],
                            start=(c == 0), stop=(c == 7)), reads=[ksr, HR(c, 4)], writes=[PR(6)], sig=(c == 7 and sl == 3))
                S.op("act", lambda e, s4=s4: e.activation(out=VNa[s4 // 2][0:8, (s4 % 2) * 512:(s4 % 2) * 512 + 512],
                                                           in_=PS[6][0:8, :], func=AF.Copy),
                     reads=[PR(6)], writes=rVN)
            if cfg.get("mixlevel", 9) >= 4:
                S.dma("sp", lambda e: e.dma_start(out=dr["vp"][l], in_=KVO[:, 0, 0:128]), "d_out", reads=spr(2176, 2688))
                S.dma("sp", lambda e: e.dma_start(out=dr["kp"][l], in_=KVO[:, 0, 128:256]), "d_out", reads=spr(2176, 2688))
            if cfg.get("ssm", 1):
                glu(l)
            else:
                for c in range(4):
                    S.op("dve", lambda e, c=c: e.memset(MO[:, c, :], 0.0), reads=[], writes=[MR(c, t) for t in range(5)])
            if cfg.get("attn", 1):
                if cfg.get("attn_p", 1):
                    attn_prompt(l, QT, KD, VT)
                if cfg.get("attn_s", 1):
                    attn_sample(l, QT, KD, VT, KVO)
                rmsnorm(lambda c, lo, n: MO[:, 4 + c, lo:lo + n], lambda c, t: MR(4 + c, t),
                        lambda c, lo, n: MO[:, 4 + c, lo:lo + n], lambda c, t: MR(4 + c, t), 4,
                        lambda c: GN2[:, l * 2 + 1, c:c + 1])
            else:
                for c in range(4, 8):
                    S.op("dve", lambda e, c=c: e.memset(MO[:, c, :], 0.0), reads=[], writes=[MR(c, t) for t in range(5)])
                zero_kv_sample(l, KVO)
            wo = wview(l, "w_out")
            k = 0
            for dh in range(2):
                slot, sr = wload([(lambda s: s.rearrange("p (c f) -> p c f", c=8), wo[:, :, dh * 512:dh * 512 + 512])])
                sv = slot.rearrange("p (c f) -> p c f", c=8)
                for i in range(4):
                    dc = dh * 4 + i
                    for t in range(5):
                        lo, n = TT[t]
                        pb = 2 + k % 2
                        for c in range(8):
                            S.op("pe", lambda e, c=c, i=i, pb=pb, lo=lo, n=n, sv=sv: e.matmul(
                                PS[pb][:, 0:n], lhsT=sv[:, c, i * 128:i * 128 + 128], rhs=MO[:, c, lo:lo + n],
                                start=(c == 0), stop=(c == 7)), reads=[sr, MR(c, t)], writes=[PR(pb)], sig=(c == 7))
                        S.op("dve", lambda e, pb=pb, dc=dc, lo=lo, n=n: e.tensor_tensor(
                            out=X[:, dc, lo:lo + n], in0=PS[pb][:, 0:n], in1=X[:, dc, lo:lo + n], op=ALU.add),
                            reads=[PR(pb), XR(dc, t)], writes=[XR(dc, t)])
                        k += 1

        def zero_kv_sample(l, KVO):
            pass

        def softmax_block(l, kvh, psA, psB, nk, maskap, Sf, rSf, Pb, rPb, so=0, part="AB", pemask=False):
            mx, ng, sm, es = SM[:, so:so + 4], SM[:, so + 4:so + 8], SM[:, so + 8:so + 12], SM[:, so + 12:so + 16]
            rmx, rng_, rsm, res_ = R("SMx", so), R("SMn", so), R("SMs", so), R("SMe", so)
            for h2, pp in (enumerate((psA, psB)) if ("A" in part and not pemask) else []):
                S.op("dve", lambda e, h2=h2, pp=pp: e.tensor_tensor(
                    out=Sf[:, h2:4:2, 0:nk], in0=PS[pp][:, :].rearrange("p (g k) -> p g k", g=2)[:, :, 0:nk],
                    in1=maskap.unsqueeze(1).to_broadcast([128, 2, nk]), op=ALU.add),
                    reads=[PR(pp), R("amask"), R("smask")], writes=rSf)
            if "A" not in part:
                S.op("dve", lambda e: e.tensor_tensor(out=sm, in0=sm, in1=es, op=ALU.add), reads=[rsm, res_], writes=[rsm])
                S.op("dve", lambda e: e.reciprocal(out=sm, in_=sm), reads=[rsm], writes=[rsm])
                S.op("dve", lambda e: e.tensor_tensor(out=Pb[:, :, 0:nk], in0=Sf[:, :, 0:nk], in1=sm.unsqueeze(2).to_broadcast([128, 4, nk]), op=ALU.mult),
                     reads=rSf + [rsm], writes=rPb)
                return
            if pemask:
                for h2, pp in enumerate((psA, psB)):
                    S.op("dve", lambda e, h2=h2, pp=pp: e.tensor_reduce(
                        out=mx[:, h2:4:2], in_=PS[pp][:, :].rearrange("p (g k) -> p g k", g=2)[:, :, 0:nk], axis=AX.X, op=ALU.max),
                        reads=[PR(pp)], writes=[rmx])
            else:
                S.op("dve", lambda e: e.tensor_reduce(out=mx, in_=Sf[:, :, 0:nk], axis=AX.X, op=ALU.max), reads=rSf, writes=[rmx])
            S.op("dve", lambda e: e.scalar_tensor_tensor(out=mx, in0=mx, scalar=0.125, in1=SK[:, l * 8 + kvh * 4:l * 8 + kvh * 4 + 4],
                                                          op0=ALU.mult, op1=ALU.max), reads=[rmx, R("SK")], writes=[rmx])
            S.op("dve", lambda e: e.tensor_scalar(out=ng, in0=mx, scalar1=-1.0, scalar2=None, op0=ALU.mult), reads=[rmx], writes=[rng_])
            S.op("dve", lambda e: e.tensor_tensor(out=es, in0=SK[:, l * 8 + kvh * 4:l * 8 + kvh * 4 + 4], in1=ng, op=ALU.add),
                 reads=[rng_, R("SK")], writes=[res_])
            S.op("act", lambda e: e.activation(out=es, in_=es, func=AF.Exp), reads=[res_], writes=[res_])
            for g in range(4):
                if pemask:
                    pp = (psA, psB)[g % 2]
                    S.op("act", lambda e, g=g, pp=pp: e.activation(out=Sf[:, g, 0:nk], in_=PS[pp][:, (g // 2) * 256:(g // 2) * 256 + nk], func=AF.Exp,
                                                                    scale=0.125, bias=ng[:, g:g + 1], accum_out=sm[:, g:g + 1]),
                         reads=[PR(pp), rng_], writes=rSf + [rsm])
                else:
                    S.op("act", lambda e, g=g: e.activation(out=Sf[:, g, 0:nk], in_=Sf[:, g, 0:nk], func=AF.Exp, scale=0.125,
                                                             bias=ng[:, g:g + 1], accum_out=sm[:, g:g + 1]),
                         reads=rSf + [rng_], writes=rSf + [rsm])
            if "B" not in part:
                return
            S.op("dve", lambda e: e.tensor_tensor(out=sm, in0=sm, in1=es, op=ALU.add), reads=[rsm, res_], writes=[rsm])
            S.op("dve", lambda e: e.reciprocal(out=sm, in_=sm), reads=[rsm], writes=[rsm])
            S.op("dve", lambda e: e.tensor_tensor(out=Pb[:, :, 0:nk], in0=Sf[:, :, 0:nk], in1=sm.unsqueeze(2).to_broadcast([128, 4, nk]), op=ALU.mult),
                 reads=rSf + [rsm], writes=rPb)

        def attn_prompt(l, QT, KD, VT):
            sets = [
                dict(Sf=SPR[:, 3200:5248].bitcast(F32).rearrange("p (g k) -> p g k", g=4), rSf=spr(3200, 5248),
                     Pb=SPR[:, 5248:6272].rearrange("p (g k) -> p g k", g=4), rPb=spr(5248, 6272),
                     PTs=SPR[:, 6272:7296].rearrange("p (g b q) -> p g b q", g=4, b=2), rPT=spr(6272, 7296), ps=(0, 1, 2, 3), so=0),
                dict(Sf=RG[:, 6144:8192].bitcast(F32).rearrange("p (g k) -> p g k", g=4), rSf=rg(6144, 8192),
                     Pb=RG[:, 0:1024].rearrange("p (g k) -> p g k", g=4), rPb=rg(0, 1024),
                     PTs=RG[:, 1024:2048].rearrange("p (g b q) -> p g b q", g=4, b=2), rPT=rg(1024, 2048), ps=(4, 5, 6, 7), so=16),
            ]

            def geo(it):
                nb, kvh = it // 2, it % 2
                return dict(nb=nb, kvh=kvh, t=nb // 4, tk0=(nb - 1) // 4 if nb > 0 else 0, k0=128 * (nb - 1) if nb > 0 else 0,
                            nk=256 if nb > 0 else 128, mcol=0 if nb > 0 else 128)

            def st_scores(it):
                G, B = geo(it), sets[it % 2]
                nb, kvh, t, tk0, k0, nk = G["nb"], G["kvh"], G["t"], G["tk0"], G["k0"], G["nk"]
                for g in range(4):
                    h = 4 * kvh + g
                    ch, hf = h // 2, h % 2
                    pp = B["ps"][hf]
                    S.op("pe", lambda e, ch=ch, hf=hf, pp=pp, g=g, nb=nb, k0=k0, nk=nk, kvh=kvh: e.matmul(
                        PS[pp][:, (g // 2) * 256:(g // 2) * 256 + nk], lhsT=QT[64 * hf:64 * hf + 64, ch, 128 * nb:128 * nb + 128],
                        rhs=KD[64 * hf:64 * hf + 64, kvh, k0:k0 + nk], start=True, stop=False),
                        reads=[R1r(ch, t), R1r(4 + kvh, t), R1r(4 + kvh, tk0)], writes=[PR(pp)], sig=False)
                    mc = G["mcol"]
                    S.op("pe", lambda e, pp=pp, g=g, nk=nk, mc=mc: e.matmul(
                        PS[pp][:, (g // 2) * 256:(g // 2) * 256 + nk], lhsT=identB[:], rhs=amaskB[:, mc:mc + nk], start=False, stop=True),
                        reads=[R("identB"), R("amaskB")], writes=[PR(pp)], sig=(g >= 2))

            def st_softmax(it, part):
                G, B = geo(it), sets[it % 2]
                softmax_block(l, G["kvh"], B["ps"][0], B["ps"][1], G["nk"], amask[:, G["mcol"]:G["mcol"] + G["nk"]],
                              B["Sf"], B["rSf"], B["Pb"], B["rPb"], B["so"], part, pemask=True)

            def st_pv(it):
                G, B = geo(it), sets[it % 2]
                nb, kvh, t, nk = G["nb"], G["kvh"], G["t"], G["nk"]
                Pb, PTs, pT, pO = B["Pb"], B["PTs"], B["ps"][2], B["ps"][3]
                nkb = nk // 128
                ptp = PS[pT][:, :].bitcast(BF16).rearrange("p (g b q) -> p g b q", g=4, b=2)
                for g in range(4):
                    for kb in range(nkb):
                        S.op("pe", lambda e, g=g, kb=kb, ptp=ptp, Pb=Pb: e.transpose(ptp[:, g, kb, :], Pb[:, g, kb * 128:kb * 128 + 128], identB[:]),
                             reads=B["rPb"] + [R("identB")], writes=[PR(pT)], sig=(g == 3 and kb == nkb - 1))
                S.op("act", lambda e, nkb=nkb, ptp=ptp, PTs=PTs: e.activation(out=PTs[:, :, 0:nkb, :], in_=ptp[:, :, 0:nkb, :], func=AF.Copy),
                     reads=[PR(pT)], writes=B["rPT"])
                for g in range(4):
                    h = 4 * kvh + g
                    hf = h % 2
                    for kb in range(nkb):
                        vi = nb - 1 + kb if nb > 0 else 0
                        S.op("pe", lambda e, g=g, kb=kb, hf=hf, vi=vi, kvh=kvh, nkb=nkb, PTs=PTs, pO=pO: e.matmul(
                            PS[pO][64 * hf:64 * hf + 64, (g // 2) * 128:(g // 2) * 128 + 128],
                            lhsT=VT[:, vi, kvh * 64:kvh * 64 + 64], rhs=PTs[:, g, kb, :],
                            start=(kb == 0), stop=(kb == nkb - 1), tile_position=(0, 64 * hf)),
                            reads=spr(128 * vi, 128 * vi + 128) + B["rPT"], writes=[PR(pO)], sig=(g == 3 and kb == nkb - 1))
                S.op("act", lambda e, kvh=kvh, nb=nb, pO=pO: e.activation(
                    out=MO[:, 4 + 2 * kvh:4 + 2 * kvh + 2, 128 * nb:128 * nb + 128],
                    in_=PS[pO][:, 0:256].rearrange("p (c q) -> p c q", c=2), func=AF.Copy),
                    reads=[PR(pO)], writes=[MR(4 + 2 * kvh, t), MR(5 + 2 * kvh, t)])

            st_scores(0)
            st_scores(1)
            st_softmax(0, "A")
            for it in range(32):
                if it + 1 < 32:
                    st_softmax(it + 1, "A")
                if it + 2 < 32:
                    st_scores(it + 2)
                st_softmax(it, "B")
                st_pv(it)

        def attn_sample(l, QT, KD, VT, KVO):
            Sf = SPR[:, 3200:3200 + 2048].bitcast(F32).rearrange("p (g k) -> p g k", g=4)
            Pb = SPR[:, 5248:5248 + 1024].rearrange("p (g k) -> p g k", g=4)
            PTs = SPR[:, 6272:6272 + 512].rearrange("p (g q) -> p g q", g=4)
            PTn = SPR[:, 6784:6784 + 512].rearrange("p (g q) -> p g q", g=4)
            KTs = SPR[:, 8320:8320 + 4 * 136].rearrange("p (s k) -> p s k", s=4)
            KCb = SPR[:, 8896:8896 + 128]
            VNa = [SPR[:, 7296:8320], SPR[:, 9024:10048]]
            rVN = spr(7296, 8320) + spr(9024, 10048)
            KC4 = RG[:, 2048:3072].bitcast(F32).rearrange("p (s f) -> p s f", s=4)
            VC4 = RG[:, 3072:4096].bitcast(F32).rearrange("p (s f) -> p s f", s=4)
            VCs = RG[:, 8192:8704].rearrange("p (s f) -> p s f", s=4)
            for s in range(16):
                S.dma("sp", lambda e, s=s: e.dma_start(out=dr["vs"][l, s, 120:128, :], in_=KVO[8 * s:8 * s + 8, 1, 0:128]), "d_out", reads=spr(2688, 3200))
                S.dma("sp", lambda e, s=s: e.dma_start(out=dr["ks"][l, s, 120:128, :], in_=KVO[8 * s:8 * s + 8, 1, 128:256]), "d_out", reads=spr(2688, 3200))
            for s4 in range(4):
                S.dma("sp", lambda e, s4=s4: e.dma_start(out=KC4, in_=dr["ck"][l, 4 * s4:4 * s4 + 4].rearrange("s k f -> k s f")), "d_kc", writes=rg(2048, 3072))
                S.dma("sp", lambda e, s4=s4: e.dma_start(out=VC4, in_=dr["cv"][l, 4 * s4:4 * s4 + 4].rearrange("s k f -> k s f")), "d_vc", writes=rg(3072, 4096))
                S.op("dve", lambda e: e.tensor_copy(out=VCs, in_=VC4), reads=rg(3072, 4096), writes=rg(8192, 8704))
                S.dma("sp", lambda e, s4=s4: e.dma_start(out=dr["ks"][l, 4 * s4:4 * s4 + 4, 0:120, :].rearrange("s k f -> k s f"), in_=KC4[8:128, :, :]), "d_out", reads=rg(2048, 3072))
                S.dma("sp", lambda e, s4=s4: e.dma_start(out=dr["vs"][l, 4 * s4:4 * s4 + 4, 0:120, :].rearrange("s k f -> k s f"), in_=VC4[8:128, :, :]), "d_out", reads=rg(3072, 4096))
                for kvh in range(2):
                    po = 3 + kvh
                    for sl in range(4):
                        s = s4 * 4 + sl
                        S.op("dve", lambda e, sl=sl, kvh=kvh: e.tensor_copy(
                            out=KCb[:, :].rearrange("p (d f) -> p d f", d=2),
                            in_=KC4[:, sl, kvh * 64:kvh * 64 + 64].unsqueeze(1).to_broadcast([128, 2, 64])),
                            reads=rg(2048, 3072), writes=spr(8896, 9024))
                        ptk = PS[5][:, :].bitcast(BF16)
                        S.op("pe", lambda e, ptk=ptk: e.transpose(ptk[:, 0:128], KCb[:, 0:128], identB[:]),
                             reads=spr(8896, 9024) + [R("identB")], writes=[PR(5)])
                        S.op("act", lambda e, sl=sl, ptk=ptk: e.activation(out=KTs[:, sl, 0:128], in_=ptk[:, 0:128], func=AF.Copy),
                             reads=[PR(5)], writes=spr(8320, 8864))
                        S.op("act", lambda e, sl=sl, s=s, kvh=kvh: e.activation(out=KTs[:, sl, 128:136], in_=KD[:, kvh, 2048 + 8 * s:2048 + 8 * s + 8], func=AF.Copy),
                             reads=[R1r(4 + kvh, 4)], writes=spr(8320, 8864))
                    for sl in range(4):
                        s = s4 * 4 + sl
                        for g in range(4):
                            h = 4 * kvh + g
                            ch, hf = h // 2, h % 2
                            pp = hf
                            S.op("pe", lambda e, ch=ch, hf=hf, pp=pp, g=g, s=s, sl=sl: e.matmul(
                                PS[pp][32 * sl:32 * sl + 8, (g // 2) * 256:(g // 2) * 256 + 136],
                                lhsT=QT[64 * hf:64 * hf + 64, ch, 2048 + 8 * s:2048 + 8 * s + 8],
                                rhs=KTs[64 * hf:64 * hf + 64, sl, :], start=True, stop=True, tile_position=(64 * hf, 32 * sl)),
                                reads=[R1r(ch, 4)] + spr(8320, 8864), writes=[PR(pp)], sig=(sl == 3 and g >= 2))
                    softmax_block(l, kvh, 0, 1, 136, smask[:, :], Sf, spr(3200, 5248), Pb, spr(5248, 6272), 0)
                    ptp = PS[2][:, :].bitcast(BF16).rearrange("p (g q) -> p g q", g=8)
                    for g in range(4):
                        S.op("pe", lambda e, g=g, ptp=ptp: e.transpose(ptp[:, g, :], Pb[:, g, 0:128], identB[:]),
                             reads=spr(5248, 6272) + [R("identB")], writes=[PR(2)], sig=False)
                        S.op("pe", lambda e, g=g, ptp=ptp: e.transpose(ptp[0:8, 4 + g, :], Pb[:, g, 128:136], identB[:]),
                             reads=spr(5248, 6272) + [R("identB")], writes=[PR(2)], sig=(g == 3))
                    S.op("act", lambda e, ptp=ptp: e.activation(out=PTs, in_=ptp[:, 0:4, :], func=AF.Copy), reads=[PR(2)], writes=spr(6272, 7296))
                    S.op("act", lambda e, ptp=ptp: e.activation(out=PTn[0:8, :, :], in_=ptp[0:8, 4:8, :], func=AF.Copy), reads=[PR(2)], writes=spr(6272, 7296))
                    for sl in range(4):
                        s = s4 * 4 + sl
                        for g in range(4):
                            h = 4 * kvh + g
                            hf = h % 2
                            dst = PS[po][64 * hf:64 * hf + 64, (g // 2) * 128 + 8 * s:(g // 2) * 128 + 8 * s + 8]
                            S.op("pe", lambda e, dst=dst, s=s, sl=sl, g=g, kvh=kvh, hf=hf: e.matmul(
                                dst, lhsT=VCs[:, sl, kvh * 64:kvh * 64 + 64], rhs=PTs[:, g, 32 * sl:32 * sl + 8],
                                start=True, stop=False, tile_position=(0, 64 * hf)),
                                reads=rg(8192, 8704) + spr(6272, 7296), writes=[PR(po)], sig=False)
                            S.op("pe", lambda e, dst=dst, s=s, sl=sl, g=g, kvh=kvh, hf=hf: e.matmul(
                                dst, lhsT=VNa[s // 8][0:8, (s % 8) * 128 + kvh * 64:(s % 8) * 128 + kvh * 64 + 64], rhs=PTn[0:8, g, 32 * sl:32 * sl + 8],
                                start=False, stop=True, tile_position=(0, 64 * hf)),
                                reads=rVN + spr(6272, 7296), writes=[PR(po)], sig=(sl == 3 and g == 3))
            for kvh in range(2):
                S.op("act", lambda e, kvh=kvh: e.activation(
                    out=MO[:, 4 + 2 * kvh:4 + 2 * kvh + 2, 2048:2176],
                    in_=PS[3 + kvh][:, 0:256].rearrange("p (c q) -> p c q", c=2), func=AF.Copy),
                    reads=[PR(3 + kvh)], writes=[MR(4 + 2 * kvh, 4), MR(5 + 2 * kvh, 4)])

        NSV = {}

        def sv(name):
            if name not in NSV:
                NSV[name] = len(NSV)
                assert len(NSV) <= 48, "SV overflow"
            i = NSV[name]
            return SV[:, i, :], R("sv", i)

        def tt(eng, out, a, b, op, reads, writes):
            S.op(eng, lambda e: e.tensor_tensor(out=out, in0=a, in1=b, op=op), reads=reads, writes=writes)

        def svop(out, a, b, op):
            (oa, orr), (aa, ar_), (ba, br_) = sv(out), sv(a), sv(b)
            tt("dve", oa, aa, ba, op, [ar_, br_], [orr])

        def svts(out, a, s1, op0, s2=None, op1=None):
            (oa, orr), (aa, ar_) = sv(out), sv(a)
            if op1 is None:
                S.op("dve", lambda e: e.tensor_scalar(out=oa, in0=aa, scalar1=s1, scalar2=None, op0=op0), reads=[ar_], writes=[orr])
            else:
                S.op("dve", lambda e: e.tensor_scalar(out=oa, in0=aa, scalar1=s1, scalar2=s2, op0=op0, op1=op1), reads=[ar_], writes=[orr])

        def svact(out, a, func, scale=1.0):
            (oa, orr), (aa, ar_) = sv(out), sv(a)
            S.op("act", lambda e: e.activation(out=oa, in_=aa, func=func, scale=scale), reads=[ar_], writes=[orr])

        def svcmul(outr, outi, ar_, ai_, br_, bi_):
            svop("t0", ar_, br_, ALU.mult); svop("t1", ai_, bi_, ALU.mult); svop(outr, "t0", "t1", ALU.subtract)
            svop("t0", ar_, bi_, ALU.mult); svop("t1", ai_, br_, ALU.mult); svop(outi, "t0", "t1", ALU.add)

        def load_ssm_vecs():
            for l in range(2):
                for k, src in enumerate(("ssm_lam_re", "ssm_lam_im")):
                    S.dma("sp", lambda e, l=l, k=k, src=src: e.dma_start(out=SVL[:, l * 4 + k, :], in_=dr[src][l].rearrange("(j g) n -> (g n) j", g=2),
                                                                      allow_slow_non_contiguous=True), None, writes=[R("SVL", l * 4 + k)])
                for g2 in range(2):
                    S.dma("sp", lambda e, l=l, g2=g2: e.dma_start(
                        out=SVL[64 * g2:64 * g2 + 64, l * 4 + 2, :], in_=dr["ssm_log_dt"][l].rearrange("(j g) -> g j", g=2)[g2].partition_broadcast(64),
                        allow_slow_non_contiguous=True), None, writes=[R("SVL", l * 4 + 2)])
                for sg in range(4):
                    S.dma("sp", lambda e, l=l, sg=sg: e.dma_start(out=SVL[32 * sg:32 * sg + 32, l * 4 + 3, :], in_=dr["ssm_d"][l].rearrange("(j r) -> r j", r=32),
                                                                  allow_slow_non_contiguous=True), None, writes=[R("SVL", l * 4 + 3)])

        def ssm_prefetch(l):
            Bre = RG[:, 0:512].bitcast(F32).rearrange("p (j q) -> p j q", j=16)
            Bim = RG[:, 512:1024].bitcast(F32).rearrange("p (j q) -> p j q", j=16)
            S.dma("sp", lambda e: e.dma_start(out=Bre, in_=dr["ssm_b_re"][l].rearrange("(j g) n q -> (g n) j q", g=2)), None, writes=rg(0, 512))
            S.dma("sp", lambda e: e.dma_start(out=Bim, in_=dr["ssm_b_im"][l].rearrange("(j g) n q -> (g n) j q", g=2)), None, writes=rg(512, 1024))
            for ci, src in enumerate(("ssm_c_re", "ssm_c_im")):
                for hh in range(2):
                    kk = ci * 2 + hh
                    cin = RG[:, 1024 + 256 * kk:1024 + 256 * kk + 256].bitcast(F32)
                    for j8 in range(8):
                        j = hh * 8 + j8
                        S.dma("sp", lambda e, src=src, j=j, j8=j8, cin=cin: e.dma_start(
                            out=cin[16 * j8:16 * j8 + 16, :].rearrange("p (g n) -> p g n", g=2),
                            in_=dr[src][l, 2 * j:2 * j + 2].rearrange("g p n -> p g n")), None, writes=rg(1024 + 256 * kk, 1280 + 256 * kk))

        def ssm_params(l):
            for k, nm in enumerate(("lamr", "lami", "ldt", "dcol")):
                ap, rr = sv(nm)
                S.op("dve", lambda e, ap=ap, k=k: e.tensor_copy(out=ap, in_=SVL[:, l * 4 + k, :]), reads=[R("SVL", l * 4 + k)], writes=[rr])
            Xre = R1[:, 0:2048]; Xim = R1[:, 2048:4096]; Yre = R1[:, 4096:6144]; Yim = R1[:, 6144:8192]
            Bre = R1[:, 8192:8704].bitcast(F32).rearrange("p (j q) -> p j q", j=16)
            Bim = R1[:, 8704:9216].bitcast(F32).rearrange("p (j q) -> p j q", j=16)
            Cre = R1[:, 9216:9728].bitcast(F32).rearrange("p (j q) -> p j q", j=16)
            Cim = R1[:, 9728:10240].bitcast(F32).rearrange("p (j q) -> p j q", j=16)
            Bbr = R1[:, 10240:10752].bitcast(F32).rearrange("p (j q) -> p j q", j=16)
            Bbi = R1[:, 10752:11264].bitcast(F32).rearrange("p (j q) -> p j q", j=16)
            TA = R1[:, 11264:11776].bitcast(F32).rearrange("p (j q) -> p j q", j=16)
            TB = R1[:, 11776:12288].bitcast(F32).rearrange("p (j q) -> p j q", j=16)
            CIN = R1[:, 12288:12544].bitcast(F32)
            rXre, rXim, rYre, rYim = r1(0, 2048), r1(2048, 4096), r1(4096, 6144), r1(6144, 8192)
            rB, rC, rBb, rTA, rTB, rCIN = r1(8192, 9216), r1(9216, 10240), r1(10240, 11264), r1(11264, 11776), r1(11776, 12288), r1(12288, 12544)
            STi = RG[:, 2048:6144].bitcast(F32)
            rSTi = rg(2048, 6144)
            Bre = RG[:, 0:512].bitcast(F32).rearrange("p (j q) -> p j q", j=16)
            Bim = RG[:, 512:1024].bitcast(F32).rearrange("p (j q) -> p j q", j=16)
            rB = rg(0, 1024)
            for ci, dst in enumerate((Cre, Cim)):
                for hh in range(2):
                    kk = ci * 2 + hh
                    cin = RG[:, 1024 + 256 * kk:1024 + 256 * kk + 256].bitcast(F32)
                    S.op("pe", lambda e, cin=cin: e.transpose(PS[7][:, 0:128], cin, identF[:]), reads=rg(1024 + 256 * kk, 1280 + 256 * kk) + [R("identF")], writes=[PR(7)])
                    S.op("dve", lambda e, dst=dst, hh=hh: e.tensor_copy(out=dst[:, 8 * hh:8 * hh + 8, :], in_=PS[7][:, 0:128].rearrange("p (j q) -> p j q", j=8)),
                         reads=[PR(7)], writes=rC)
            for ri, src in enumerate(("st_re", "st_im")):
                S.dma("sp", lambda e, src=src: e.dma_start(out=STi[0:16, :], in_=dr[src][l]), "d_sti", writes=rSTi)
                for j in range(16):
                    S.op("pe", lambda e, j=j: e.transpose(PS[6][:, 16 * j:16 * j + 16], STi[0:16, 128 * j:128 * j + 128], identF[0:16, 0:16]),
                         reads=rSTi + [R("identF")], writes=[PR(6)], sig=(j == 15))
                S.op("dve", lambda e, ri=ri: e.tensor_copy(out=H0[:, ri, :, :], in_=PS[6][:, 0:256].rearrange("p (j s) -> p j s", j=16)),
                     reads=[PR(6)], writes=[R("H0")])
            svact("dt", "ldt", AF.Exp)
            svop("x", "lamr", "dt", ALU.mult)
            svop("phi", "lami", "dt", ALU.mult)
            svact("mag", "x", AF.Exp)
            for _ in range(5):
                svts("t0", "phi", math.pi, ALU.is_gt, 2 * math.pi, ALU.mult)
                svop("phi", "phi", "t0", ALU.subtract)
            svts("phc", "phi", math.pi / 2, ALU.add)
            svts("t0", "phc", math.pi, ALU.is_gt, 2 * math.pi, ALU.mult)
            svop("phc", "phc", "t0", ALU.subtract)
            svact("s1", "phi", AF.Sin)
            svact("c1", "phc", AF.Sin)
            svop("a1r", "mag", "c1", ALU.mult)
            svop("a1i", "mag", "s1", ALU.mult)
            svcmul("a2r", "a2i", "a1r", "a1i", "a1r", "a1i")
            svcmul("a3r", "a3i", "a2r", "a2i", "a1r", "a1i")
            svcmul("a4r", "a4i", "a2r", "a2i", "a2r", "a2i")
            svts("na4i", "a4i", -1.0, ALU.mult)
            for k in (1, 2, 3):
                svact("t2", "x", AF.Exp, scale=-2.0 * k)
                svop("i%dr" % k, "a%dr" % k, "t2", ALU.mult)
                svop("t3", "a%di" % k, "t2", ALU.mult)
                svts("i%di" % k, "t3", -1.0, ALU.mult)
            S.op("dve", lambda e: e.memset(sv("one")[0], 1.0), writes=[sv("one")[1]])
            S.op("dve", lambda e: e.memset(sv("zero")[0], 0.0), writes=[sv("zero")[1]])
            svts("nr", "a1r", -1.0, ALU.add)
            svop("t0", "lamr", "lamr", ALU.mult); svop("t1", "lami", "lami", ALU.mult); svop("den", "t0", "t1", ALU.add)
            S.op("dve", lambda e: e.reciprocal(out=sv("den")[0], in_=sv("den")[0]), reads=[sv("den")[1]], writes=[sv("den")[1]])
            svop("t0", "nr", "lamr", ALU.mult); svop("t1", "a1i", "lami", ALU.mult); svop("t2", "t0", "t1", ALU.add); svop("cfr", "t2", "den", ALU.mult)
            svop("t0", "a1i", "lamr", ALU.mult); svop("t1", "nr", "lami", ALU.mult); svop("t2", "t0", "t1", ALU.subtract); svop("cfi", "t2", "den", ALU.mult)
            svact("r4", "x", AF.Exp, scale=4.0)
            svact("t2", "x", AF.Exp, scale=-4.0)
            svop("wr", "a4r", "t2", ALU.mult); svop("wi", "a4i", "t2", ALU.mult)

            def table(Et, n, wr, wi):
                rE = R("E", id(Et))
                S.op("dve", lambda e: e.memset(Et[:, 0, :, 0:1], 1.0), writes=[rE])
                S.op("dve", lambda e: e.memset(Et[:, 1, :, 0:1], 0.0), writes=[rE])
                m = 1
                cr, ci = wr, wi
                while m < n:
                    (cra, crr), (cia, cir) = sv(cr), sv(ci)
                    bcr = cra.unsqueeze(2).to_broadcast([128, 16, m]); bci = cia.unsqueeze(2).to_broadcast([128, 16, m])
                    tA, tBv = TA[:, :, 0:m], TB[:, :, 0:m]
                    src_c, src_s = Et[:, 0, :, 0:m], Et[:, 1, :, 0:m]
                    tt("dve", tA, src_c, bcr, ALU.mult, [rE, crr], rTA); tt("dve", tBv, src_s, bci, ALU.mult, [rE, cir], rTB)
                    tt("dve", Et[:, 0, :, m:2 * m], tA, tBv, ALU.subtract, rTA + rTB, [rE])
                    tt("dve", tA, src_c, bci, ALU.mult, [rE, cir], rTA); tt("dve", tBv, src_s, bcr, ALU.mult, [rE, crr], rTB)
                    tt("dve", Et[:, 1, :, m:2 * m], tA, tBv, ALU.add, rTA + rTB, [rE])
                    nr_, ni_ = ("pAr", "pAi") if cr != "pAr" else ("pBr", "pBi")
                    svcmul(nr_, ni_, cr, ci, cr, ci)
                    cr, ci = nr_, ni_
                    m *= 2
                return cr, ci
            w32r, w32i = table(E0, 32, "wr", "wi")
            table(E1, 16, w32r, w32i)

            def bc(name, m=16):
                a, r = sv(name)
                return a.unsqueeze(2).to_broadcast([128, 16, m]), r
            (cfr, rcfr), (cfi, rcfi) = bc("cfr"), bc("cfi")
            tt("dve", TA, Bre, cfr, ALU.mult, rB + [rcfr], rTA); tt("dve", TB, Bim, cfi, ALU.mult, rB + [rcfi], rTB)
            tt("dve", Bbr, TA, TB, ALU.subtract, rTA + rTB, rBb)
            tt("dve", TA, Bim, cfr, ALU.mult, rB + [rcfr], rTA); tt("dve", TB, Bre, cfi, ALU.mult, rB + [rcfi], rTB)
            tt("dve", Bbi, TA, TB, ALU.add, rTA + rTB, rBb)
            MRv = SPR[:, 6144:8192]; MIv = SPR[:, 8192:10240]
            for buf, rr in ((Xre, rXre), (Xim, rXim), (Yre, rYre), (Yim, rYim), (MRv, spr(6144, 8192)), (MIv, spr(8192, 10240))):
                S.op("dve", lambda e, buf=buf: e.memset(buf, 0.0), writes=rr)

            def place(dst, rdst, sr, si, rsrc, pwr, pwi, neg_im):
                dre, dim_ = dst
                rdre, rdim = rdst
                for sg in range(4):
                    (pr, rpr), (pi, rpi) = bc(pwr[sg]), bc(pwi[sg])
                    dr5 = dre.rearrange("p (j s g q) -> p j s g q", j=16, s=4, g=2)
                    di5 = dim_.rearrange("p (j s g q) -> p j s g q", j=16, s=4, g=2)
                    tt("dve", TA, sr, pr, ALU.mult, rsrc + [rpr], rTA); tt("dve", TB, si, pi, ALU.mult, rsrc + [rpi], rTB)
                    for g2 in range(2):
                        ps_ = slice(64 * g2, 64 * g2 + 64)
                        tt("dve", dr5[ps_, :, sg, g2, :], TA[ps_], TB[ps_], ALU.subtract, rTA + rTB, rdre)
                    tt("dve", TA, sr, pi, ALU.mult, rsrc + [rpi], rTA); tt("dve", TB, si, pr, ALU.mult, rsrc + [rpr], rTB)
                    for g2 in range(2):
                        ps_ = slice(64 * g2, 64 * g2 + 64)
                        if neg_im:
                            S.op("dve", lambda e, ps_=ps_, sg=sg, g2=g2, di5=di5: e.scalar_tensor_tensor(
                                out=di5[ps_, :, sg, g2, :], in0=TA[ps_], scalar=-1.0, in1=TB[ps_], op0=ALU.mult, op1=ALU.subtract),
                                reads=rTA + rTB, writes=rdim)
                        else:
                            tt("dve", di5[ps_, :, sg, g2, :], TA[ps_], TB[ps_], ALU.add, rTA + rTB, rdim)
            place((Xre, Xim), (rXre, rXim), Bbr, Bbi, rBb, ["a3r", "a2r", "a1r", "one"], ["a3i", "a2i", "a1i", "zero"], False)
            place((Yre, Yim), (rYre, rYim), Cre, Cim, rC, ["i3r", "i2r", "i1r", "one"], ["i3i", "i2i", "i1i", "zero"], True)
            place((MRv, MIv), (spr(6144, 8192), spr(8192, 10240)), Cre, Cim, rC, ["a1r", "a2r", "a3r", "a4r"], ["a1i", "a2i", "a3i", "a4i"], True)
            Tv = SPR[:, 0:2048].rearrange("p (j c) -> p j c", j=16)
            WRv = SPR[:, 2048:4096].rearrange("p (j c) -> p j c", j=16)
            WIv = SPR[:, 4096:6144].rearrange("p (j c) -> p j c", j=16)
            X3r = Xre.rearrange("p (j c) -> p j c", j=16); X3i = Xim.rearrange("p (j c) -> p j c", j=16)
            Y3r = Yre.rearrange("p (j c) -> p j c", j=16); Y3i = Yim.rearrange("p (j c) -> p j c", j=16)
            TMs = [(R1[:, 12544:12800].bitcast(F32), r1(12544, 12800)), (R1[:, 12800:13056].bitcast(F32), r1(12800, 13056))]
            dcol, rdcol = sv("dcol")
            for j in range(16):
                TM, rTM = TMs[j % 2]
                pT, pW = (7, 6) if j % 2 == 0 else (5, 4)
                S.op("pe", lambda e, j=j, pT=pT: e.matmul(PS[pT][:, 0:128], lhsT=X3r[:, j, :], rhs=Y3r[:, j, :], start=True, stop=False),
                     reads=rXre + rYre, writes=[PR(pT)], sig=False)
                S.op("pe", lambda e, j=j, pT=pT: e.matmul(PS[pT][:, 0:128], lhsT=X3i[:, j, :], rhs=Y3i[:, j, :], start=False, stop=True),
                     reads=rXim + rYim, writes=[PR(pT)])
                tt("dve", TM, PS[pT][:, 0:128], maskT[:], ALU.mult, [PR(pT), R("maskT")], rTM)
                S.op("dve", lambda e, j=j, TM=TM: e.scalar_tensor_tensor(out=Tv[:, j, :], in0=identF[:], scalar=dcol[:, j:j + 1], in1=TM,
                                                                         op0=ALU.mult, op1=ALU.add), reads=rTM + [R("identF"), rdcol], writes=spr(128 * j, 128 * j + 128))
                ptb = PS[pW][:, :].bitcast(BF16)
                S.op("pe", lambda e, j=j, ptb=ptb: e.transpose(ptb[:, 0:128], X3r[:, j, :], identB[:]), reads=rXre + [R("identB")], writes=[PR(pW)], sig=False)
                S.op("pe", lambda e, j=j, ptb=ptb: e.transpose(ptb[:, 128:256], X3i[:, j, :], identB[:]), reads=rXim + [R("identB")], writes=[PR(pW)])
                S.op("act", lambda e, j=j, ptb=ptb: e.activation(out=WRv[:, j, :], in_=ptb[:, 0:128], func=AF.Copy), reads=[PR(pW)], writes=spr(2048 + 128 * j, 2048 + 128 * j + 128))
                S.op("act", lambda e, j=j, ptb=ptb: e.activation(out=WIv[:, j, :], in_=ptb[:, 128:256], func=AF.Copy), reads=[PR(pW)], writes=spr(4096 + 128 * j, 4096 + 128 * j + 128))

        def ssm(l):
            ssm_params(l)
            Tv = SPR[:, 0:2048].rearrange("p (j c) -> p j c", j=16)
            WRv = SPR[:, 2048:4096].rearrange("p (j c) -> p j c", j=16)
            WIv = SPR[:, 4096:6144].rearrange("p (j c) -> p j c", j=16)
            MRv = SPR[:, 6144:8192].rearrange("p (j c) -> p j c", j=16)
            MIv = SPR[:, 8192:10240].rearrange("p (j c) -> p j c", j=16)
            rT, rWR, rWI, rMR, rMI = spr(0, 2048), spr(2048, 4096), spr(4096, 6144), spr(6144, 8192), spr(8192, 10240)
            NC_ = 544
            Ut = R1[:, 0:544]; rUt = r1(0, 544)
            def f32v(a, n):
                return R1[:, a:a + 2 * n].bitcast(F32), r1(a, a + 2 * n)
            ZtR, rZtR = f32v(544, 512); ZtI, rZtI = f32v(1568, 512)
            GR, rGR = f32v(2592, 512); GI, rGI = f32v(3616, 512)
            ECS = [f32v(4640, 512) + f32v(5664, 512),
                   (RG[:, 6528:7552].bitcast(F32), rg(6528, 7552), RG[:, 7552:8576].bitcast(F32), rg(7552, 8576))]
            rPT = [R("PTMP")]
            TA, rTA = f32v(6688, 512); TB, rTB = f32v(7712, 512)
            HsR = R1[:, 8736:8736 + 544]; rHsR = r1(8736, 9280)
            HsI = R1[:, 9280:9280 + 544]; rHsI = r1(9280, 9824)
            Y4 = R1[:, 9824:9824 + 2176].rearrange("p (j c) -> p j c", j=4); rY4 = r1(9824, 12000)
            SA, rSA = f32v(12000, 64)
            YG = RG[:, :].rearrange("p (c n) -> p c n", c=4)
            win = wview(l, "w_in")
            uslot, usr = wload([(lambda s: s.rearrange("p (c f) -> p c f", c=8), win[:, :, 0:512])])
            uw = uslot.rearrange("p (c f) -> p c f", c=8)
            r4, rr4 = sv("r4")
            a4r, ra4r = sv("a4r"); a4i, ra4i = sv("a4i"); na4i, rna4i = sv("na4i")
            hp_r, rhp_r = sv("hp_r"); hp_i, rhp_i = sv("hp_i")
            E0c, E0s, E1c, E1s = E0[:, 0], E0[:, 1], E1[:, 0], E1[:, 1]
            rE0, rE1 = R("E", id(E0)), R("E", id(E1))
            HTp = HT[:, :, 0:2048].rearrange("p c (n t) -> p c t n", t=4)
            HTs = HT[:, :, 2048:2176].rearrange("p c (n t) -> p c t n", t=4)
            Uts = [(R1[:, 0:544], r1(0, 544)), (R1[:, 12192:12736], r1(12192, 12736))]

            def stage_U(j):
                Ut, rUt = Uts[j % 2]
                for c in range(8):
                    for tau in range(4):
                        S.op("pe", lambda e, tau=tau, c=c, j=j: e.matmul(
                            PS[0][32 * tau:32 * tau + 32, 0:512], lhsT=uw[:, c, 32 * j:32 * j + 32], rhs=HTp[:, c, tau, :],
                            start=(c == 0), stop=(c == 7), tile_position=(0, 32 * tau)),
                            reads=[usr] + [HR(c, t) for t in range(4)], writes=[PR(0)], sig=(c == 7 and tau == 3))
                for c in range(8):
                    for tau in range(4):
                        S.op("pe", lambda e, tau=tau, c=c, j=j: e.matmul(
                            PS[7][32 * tau:32 * tau + 32, 0:32], lhsT=uw[:, c, 32 * j:32 * j + 32], rhs=HTs[:, c, tau, :],
                            start=(c == 0), stop=(c == 7), tile_position=(0, 32 * tau)),
                            reads=[usr, HR(c, 4)], writes=[PR(7)], sig=(c == 7 and tau == 3))
                S.op("act", lambda e, Ut=Ut: e.activation(out=Ut[:, 0:512], in_=PS[0][:, :], func=AF.Copy), reads=[PR(0)], writes=rUt)
                S.op("act", lambda e, Ut=Ut: e.activation(out=Ut[:, 512:544], in_=PS[7][:, 0:32], func=AF.Copy), reads=[PR(7)], writes=rUt)

            def stage_Z(j):
                Ut, rUt = Uts[j % 2]
                for (W_, rW, pb, so) in ((WRv, rWR, 1, 32), (WIv, rWI, 2, 64)):
                    S.op("pe", lambda e, W_=W_, pb=pb, j=j, Ut=Ut: e.matmul(PS[pb][:, 0:512], lhsT=W_[:, j, :], rhs=Ut[:, 0:512], start=True, stop=True),
                         reads=rW + rUt, writes=[PR(pb)])
                    S.op("pe", lambda e, W_=W_, so=so, j=j, Ut=Ut: e.matmul(PS[4][:, so:so + 32], lhsT=W_[:, j, :], rhs=Ut[:, 512:544], start=True, stop=True),
                         reads=rW + rUt, writes=[PR(4)])

            def stage_B(j):
                zs_r = PS[4][:, 32:64].rearrange("p (s h) -> p s h", h=2); zs_i = PS[4][:, 64:96].rearrange("p (s h) -> p s h", h=2)
                hsr3 = HsR[:, 512:544].rearrange("p (s h) -> p s h", h=2); hsi3 = HsI[:, 512:544].rearrange("p (s h) -> p s h", h=2)
                h0r, h0i = H0[:, 0, j, :], H0[:, 1, j, :]
                sa = [SA[:, 16 * k:16 * k + 16] for k in range(4)]
                car, cai, cnai = a4r[:, j:j + 1], a4i[:, j:j + 1], na4i[:, j:j + 1]
                rH0 = R("H0")

                def cstep(inr, ini, zr, zi, outr, outi, rin, rout, car=car, cai=cai, cnai=cnai):
                    S.op("dve", lambda e: e.tensor_scalar(out=sa[0], in0=inr, scalar1=car, scalar2=None, op0=ALU.mult), reads=rin + [ra4r], writes=rSA)
                    S.op("dve", lambda e: e.scalar_tensor_tensor(out=sa[1], in0=ini, scalar=cnai, in1=sa[0], op0=ALU.mult, op1=ALU.add), reads=rin + rSA + [rna4i], writes=rSA)
                    S.op("dve", lambda e: e.tensor_scalar(out=sa[2], in0=ini, scalar1=car, scalar2=None, op0=ALU.mult), reads=rin + [ra4r], writes=rSA)
                    S.op("dve", lambda e: e.scalar_tensor_tensor(out=sa[3], in0=inr, scalar=cai, in1=sa[2], op0=ALU.mult, op1=ALU.add), reads=rin + rSA + [ra4i], writes=rSA)
                    tt("dve", outr, sa[1], zr, ALU.add, rSA + [PR(4)], rout)
                    tt("dve", outi, sa[3], zi, ALU.add, rSA + [PR(4)], rout)
                S.op("act", lambda e, h0r=h0r: e.activation(out=hsr3[:, :, 0], in_=h0r, func=AF.Copy), reads=[rH0], writes=rHsR)
                S.op("act", lambda e, h0i=h0i: e.activation(out=hsi3[:, :, 0], in_=h0i, func=AF.Copy), reads=[rH0], writes=rHsI)
                HA, rHA = f32v(12128, 32)
                har, hai = HA[:, 0:16], HA[:, 16:32]
                cstep(h0r, h0i, zs_r[:, :, 0], zs_i[:, :, 0], har, hai, [rH0], rHA)
                S.op("act", lambda e: e.activation(out=hsr3[:, :, 1], in_=har, func=AF.Copy), reads=rHA, writes=rHsR)
                S.op("act", lambda e: e.activation(out=hsi3[:, :, 1], in_=hai, func=AF.Copy), reads=rHA, writes=rHsI)
                if not cfg.get('dbg_h0', 0):
                    cstep(har, hai, zs_r[:, :, 1], zs_i[:, :, 1], h0r, h0i, rHA, [rH0])
                EC, rEC, ES, rES = ECS[j % 2]
                tt("dve", TA, PS[1][:, 0:512], EC, ALU.mult, [PR(1)] + rEC, rTA); tt("dve", TB, PS[2][:, 0:512], ES, ALU.mult, [PR(2)] + rES, rTB)
                tt("dve", ZtR, TA, TB, ALU.add, rTA + rTB, rZtR)
                tt("dve", TA, PS[2][:, 0:512], EC, ALU.mult, [PR(2)] + rEC, rTA); tt("dve", TB, PS[1][:, 0:512], ES, ALU.mult, [PR(1)] + rES, rTB)
                tt("dve", ZtI, TA, TB, ALU.subtract, rTA + rTB, rZtI)
                r4b = r4[:, j:j + 1].to_broadcast([128, 512])
                S.op("dve", lambda e, r4b=r4b: e.tensor_tensor_scan(out=GR, data0=r4b, data1=ZtR, initial=0.0, op0=ALU.mult, op1=ALU.add),
                     reads=rZtR + [rr4], writes=rGR)
                S.op("dve", lambda e, r4b=r4b: e.tensor_tensor_scan(out=GI, data0=r4b, data1=ZtI, initial=0.0, op0=ALU.mult, op1=ALU.add),
                     reads=rZtI + [rr4], writes=rGI)
                tt("dve", TA, GR, EC, ALU.mult, rGR + rEC, rTA); tt("dve", TB, GI, ES, ALU.mult, rGI + rES, rTB)
                tt("dve", ZtR, TA, TB, ALU.subtract, rTA + rTB, rZtR)
                tt("pool", ZtI, GI, EC, ALU.mult, rGI + rEC, rZtI); tt("pool", PTMP[:, :], GR, ES, ALU.mult, rGR + rES, rPT)
                tt("pool", ZtI, ZtI, PTMP[:, :], ALU.add, rZtI + rPT, rZtI)
                S.op("act", lambda e: e.activation(out=HsR[:, 1:512], in_=ZtR[:, 0:511], func=AF.Copy), reads=rZtR, writes=rHsR)
                S.op("act", lambda e: e.activation(out=HsI[:, 1:512], in_=ZtI[:, 0:511], func=AF.Copy), reads=rZtI, writes=rHsI)
                S.op("dve", lambda e: e.memset(HsR[:, 0:1], 0.0), writes=rHsR)
                S.op("dve", lambda e: e.memset(HsI[:, 0:1], 0.0), writes=rHsI)
                S.op("act", lambda e, j=j: e.activation(out=hp_r[:, j:j + 1], in_=ZtR[:, 511:512], func=AF.Copy), reads=rZtR, writes=[rhp_r])
                S.op("act", lambda e, j=j: e.activation(out=hp_i[:, j:j + 1], in_=ZtI[:, 511:512], func=AF.Copy), reads=rZtI, writes=[rhp_i])

            def stage_E(j):
                EC, rEC, ES, rES = ECS[j % 2]
                e0c = E0c[:, j, :].unsqueeze(1).to_broadcast([128, 16, 32]); e0s = E0s[:, j, :].unsqueeze(1).to_broadcast([128, 16, 32])
                e1c = E1c[:, j, :].unsqueeze(2).to_broadcast([128, 16, 32]); e1s = E1s[:, j, :].unsqueeze(2).to_broadcast([128, 16, 32])
                v3 = lambda a: a.rearrange("p (a b) -> p a b", a=16)
                tt("pool", v3(EC), e0c, e1c, ALU.mult, [rE0, rE1], rEC); tt("pool", v3(PTMP[:, :]), e0s, e1s, ALU.mult, [rE0, rE1], rPT)
                tt("pool", EC, EC, PTMP[:, :], ALU.subtract, rEC + rPT, rEC)
                tt("pool", v3(ES), e0s, e1c, ALU.mult, [rE0, rE1], rES); tt("pool", v3(PTMP[:, :]), e0c, e1s, ALU.mult, [rE0, rE1], rPT)
                tt("pool", ES, ES, PTMP[:, :], ALU.add, rES + rPT, rES)

            def stage_C(j):
                jj = j % 4
                Ut, rUt = Uts[j % 2]
                for (lo_, n_, ob) in ((0, 512, PS[3][:, 0:512]), (512, 32, PS[7][:, 64:96])):
                    pr_ = PR(3) if lo_ == 0 else PR(7)
                    S.op("pe", lambda e, lo_=lo_, n_=n_, ob=ob, j=j, Ut=Ut: e.matmul(ob, lhsT=Tv[:, j, :], rhs=Ut[:, lo_:lo_ + n_], start=True, stop=False),
                         reads=rT + rUt, writes=[pr_], sig=False)
                    S.op("pe", lambda e, lo_=lo_, n_=n_, ob=ob, j=j: e.matmul(ob, lhsT=MRv[:, j, :], rhs=HsR[:, lo_:lo_ + n_], start=False, stop=False),
                         reads=rMR + rHsR, writes=[pr_], sig=False)
                    S.op("pe", lambda e, lo_=lo_, n_=n_, ob=ob, j=j: e.matmul(ob, lhsT=MIv[:, j, :], rhs=HsI[:, lo_:lo_ + n_], start=False, stop=True),
                         reads=rMI + rHsI, writes=[pr_])
                S.op("act", lambda e, jj=jj: e.activation(out=Y4[:, jj, 0:512], in_=PS[3][:, 0:512], func=AF.Copy), reads=[PR(3)], writes=rY4)
                S.op("act", lambda e, jj=jj: e.activation(out=Y4[:, jj, 512:544], in_=PS[7][:, 64:96], func=AF.Copy), reads=[PR(7)], writes=rY4)
                if jj == 3:
                    oc = j // 4
                    for (lo_, n_, tok0) in ((0, 512, 0), (512, 32, 2048)):
                        for tau in range(4):
                            pb = 5 + tau % 2
                            for q in range(4):
                                S.op("pe", lambda e, tau=tau, q=q, pb=pb, lo_=lo_, n_=n_: e.matmul(
                                    PS[pb][:, 0:n_], lhsT=SelC[32 * tau:32 * tau + 32, q, :], rhs=Y4[32 * tau:32 * tau + 32, q, lo_:lo_ + n_],
                                    start=(q == 0), stop=(q == 3), tile_position=(32 * tau, 0)),
                                    reads=rY4 + [R("SelC")], writes=[PR(pb)], sig=(q == 3))
                            yv, tv = TA[:, 0:n_], TB[:, 0:n_]
                            S.op("act", lambda e, pb=pb, n_=n_, yv=yv: e.activation(out=yv, in_=PS[pb][:, 0:n_], func=AF.Copy), reads=[PR(pb)], writes=rTA)
                            S.op("act", lambda e, pb=pb, n_=n_, tv=tv: e.activation(out=tv, in_=PS[pb][:, 0:n_], func=AF.Square), reads=[PR(pb)], writes=rTB)
                            S.op("act", lambda e, tv=tv: e.activation(out=tv, in_=tv, func=AF.Copy, scale=0.044715, bias=1.0), reads=rTB, writes=rTB)
                            tt("dve", tv, tv, yv, ALU.mult, rTA + rTB, rTB)
                            S.op("act", lambda e, tv=tv: e.activation(out=tv, in_=tv, func=AF.Sigmoid, scale=1.5957691216057308), reads=rTB, writes=rTB)
                            nn = n_ * 4
                            dst = YG[:, oc, tok0:tok0 + nn].rearrange("p (n t) -> p t n", t=4)[:, tau, :]
                            a_ = oc * NTOK + tok0
                            tt("dve", dst, yv, tv, ALU.mult, rTA + rTB, rg(a_, a_ + nn))

            stage_U(0)
            stage_Z(0)
            stage_E(0)
            for j in range(16):
                if j + 1 < 16:
                    stage_U(j + 1)
                    stage_E(j + 1)
                stage_B(j)
                if j + 1 < 16:
                    stage_Z(j + 1)
                stage_C(j)
            STo = R1[:, 0:4096].bitcast(F32)
            rSTo = r1(0, 4096)
            for ri, nm in enumerate(("hs_re", "hs_im")):
                for j4 in range(4):
                    for q in range(4):
                        j = j4 * 4 + q
                        S.op("pe", lambda e, ri=ri, j=j, q=q: e.transpose(PS[6][0:16, 128 * q:128 * q + 128], H0[:, ri, j, :], identF[:]),
                             reads=[R("H0"), R("identF")], writes=[PR(6)], sig=(q == 3))
                    S.op("dve", lambda e, j4=j4: e.tensor_copy(out=STo[0:16, 512 * j4:512 * j4 + 512], in_=PS[6][0:16, :]), reads=[PR(6)], writes=rSTo)
                S.dma("sp", lambda e, nm=nm: e.dma_start(out=dr[nm][l], in_=STo[0:16, :]), "d_out", reads=rSTo)
            for (nm, (hp, rhp)) in (("hp_re", (hp_r, rhp_r)), ("hp_im", (hp_i, rhp_i))):
                S.op("pe", lambda e, hp=hp: e.transpose(PS[6][0:16, 0:128], hp, identF[:]), reads=[rhp, R("identF")], writes=[PR(6)])
                S.op("dve", lambda e: e.tensor_copy(out=STo[0:16, 0:128], in_=PS[6][0:16, 0:128]), reads=[PR(6)], writes=rSTo)
                S.dma("sp", lambda e, nm=nm: e.dma_start(out=dr[nm][l].rearrange("(j c) -> j c", j=16), in_=STo[0:16, 0:128]), "d_out", reads=rSTo)

        def glu(l):
            YG = RG[:, :].rearrange("p (c n) -> p c n", c=4)
            SGt = SPR[:, 3200:4224].bitcast(F32)
            rSGt = spr(3200, 4224)
            gslot, gsr = wload([(lambda s: s[:, 0:2048].rearrange("p (c f) -> p c f", c=4),
                                 dr["ssm_w_glu"][l].rearrange("(c p) f -> p c f", p=128))])
            gv = gslot[:, 0:2048].rearrange("p (c f) -> p c f", c=4)
            k = 0
            for oc in range(4):
                for t in range(5):
                    lo, n = TT[t]
                    pb = k % 2
                    for c in range(4):
                        S.op("pe", lambda e, c=c, oc=oc, pb=pb, lo=lo, n=n: e.matmul(
                            PS[pb][:, 0:n], lhsT=gv[:, c, oc * 128:oc * 128 + 128], rhs=YG[:, c, lo:lo + n], start=(c == 0), stop=(c == 3)),
                            reads=[gsr] + rg(c * NTOK + lo, c * NTOK + lo + n), writes=[PR(pb)], sig=(c == 3))
                    S.op("act", lambda e, pb=pb, n=n: e.activation(out=SGt[:, 0:n], in_=PS[pb][:, 0:n], func=AF.Sigmoid), reads=[PR(pb)], writes=rSGt)
                    S.op("dve", lambda e, oc=oc, lo=lo, n=n: e.tensor_tensor(out=MO[:, oc, lo:lo + n], in0=YG[:, oc, lo:lo + n], in1=SGt[:, 0:n], op=ALU.mult),
                         reads=rSGt + rg(oc * NTOK + lo, oc * NTOK + lo + n), writes=[MR(oc, t)])
                    k += 1
            rmsnorm(lambda c, lo, n: MO[:, c, lo:lo + n], lambda c, t: MR(c, t), lambda c, lo, n: MO[:, c, lo:lo + n], lambda c, t: MR(c, t), 4,
                    lambda c: GN2[:, l * 2 + 0, c:c + 1])

        def final_out():
            yo = [RG[:, 0:2048].bitcast(F32), RG[:, 2048:4096].bitcast(F32)]
            SQ = [RG[:, 4096:4608], RG[:, 4608:5120]]
            RS = RG[:, 5120:6144].bitcast(F32)
            k = 0
            for t in range(5):
                lo, n = TT[t]
                for c in range(8):
                    b = c % 2
                    S.op("act", lambda e, c=c, b=b, lo=lo, n=n: e.activation(out=SQ[b][:, 0:n], in_=X[:, c, lo:lo + n], func=AF.Square),
                         reads=[XR(c, t)], writes=rg(4096 + 512 * b, 4608 + 512 * b))
                    S.op("pe", lambda e, c=c, b=b, n=n: e.matmul(PS[5][:, 0:n], lhsT=onesB[:], rhs=SQ[b][:, 0:n], start=(c == 0), stop=(c == 7)),
                         reads=rg(4096 + 512 * b, 4608 + 512 * b) + [R("onesB")], writes=[PR(5)])
                S.op("act", lambda e, n=n: e.activation(out=RS[:, 0:n], in_=PS[5][:, 0:n], func=AF.Sqrt, scale=1.0 / D, bias=EPSB[:, 0:1]),
                     reads=[PR(5), R("EPSB")], writes=rg(5120, 6144))
                S.op("dve", lambda e, n=n: e.reciprocal(out=RS[:, 0:n], in_=RS[:, 0:n]), reads=rg(5120, 6144), writes=rg(5120, 6144))
                for c in range(8):
                    S.op("dve", lambda e, c=c, lo=lo, n=n: e.scalar_tensor_tensor(
                        out=X[:, c, lo:lo + n], in0=X[:, c, lo:lo + n], scalar=GN[:, 8, c:c + 1], in1=RS[:, 0:n], op0=ALU.mult, op1=ALU.mult),
                        reads=[XR(c, t), R("GN")] + rg(5120, 6144), writes=[XR(c, t)])
                for sub in range(n // 128):
                    tok0 = lo + 128 * sub
                    b = k % 2
                    for h in range(2):
                        pb = 6 + h
                        for cc in range(4):
                            c = h * 4 + cc
                            S.op("pe", lambda e, c=c, cc=cc, pb=pb, tok0=tok0: e.transpose(
                                PS[pb][:, cc * 128:cc * 128 + 128], X[:, c, tok0:tok0 + 128], identF[:]),
                                reads=[XR(c, t), R("identF")], writes=[PR(pb)], sig=(cc == 3))
                        if h == 0:
                            S.op("dve", lambda e, b=b, pb=pb: e.tensor_copy(out=yo[b][:, 0:512], in_=PS[pb][:, :]), reads=[PR(pb)], writes=rg(2048 * b, 2048 * b + 2048))
                        else:
                            S.op("act", lambda e, b=b, pb=pb: e.activation(out=yo[b][:, 512:1024], in_=PS[pb][:, :], func=AF.Copy), reads=[PR(pb)], writes=rg(2048 * b, 2048 * b + 2048))
                    S.dma("sp", lambda e, b=b, tok0=tok0: e.dma_start(out=dr["y"][tok0:tok0 + 128, :], in_=yo[b]), ("d_yo", b), reads=rg(2048 * b, 2048 * b + 2048))
                    k += 1

        load_consts()
        load_x()
        if cfg.get("ssm", 1) and cfg.get("mix", 1):
            load_ssm_vecs()
        for l in range(2):
            if cfg.get("ssm", 1) and cfg.get("mix", 1):
                ssm_prefetch(l)
            if cfg.get("ffn", 1):
                ffn(l, "ffn1")
            if cfg.get("mix", 1):
                mixer(l)
            if cfg.get("ffn", 1):
                ffn(l, "ffn2", (lambda l=l: ple_load(l)) if cfg.get("ple", 1) else None)
            elif cfg.get("ple", 1):
                ple_load(l)
            if cfg.get("ple", 1):
                ple(l)
        final_out()
        S.wait_all("sp")
        S.emit(st)
        info = {k: len(v) for k, v in S.ops.items()}
        print("instructions per engine:", info, "sems:", len(S.cnt), flush=True)
    return nc


def make_consts():
    c = {}
    c["c_ident"] = np.eye(128, dtype=np.float32)
    i = np.arange(128)[:, None]
    j = np.arange(256)[None, :]
    valid = ((j < 128) & (j > i)) | ((j >= 128) & (j - 128 <= i))
    c["c_amask"] = np.where(valid, 0.0, NEG).astype(np.float32)
    r = np.arange(128)[:, None] % 32
    j = np.arange(136)[None, :]
    valid = (r < 8) & (j > r) & (j <= r + 128)
    c["c_smask"] = np.where(valid | (r >= 8), 0.0, NEG).astype(np.float32)
    row_tau = (np.arange(128) // 32)[:, None]
    col_tau = (np.arange(128) // 32)[None, :]
    c["c_maskT"] = (col_tau >= row_tau).astype(np.float32)
    sel = np.zeros((128, 4, 128), np.float32)
    for p in range(128):
        for jj in range(4):
            sel[p, jj, 32 * jj + (p % 32)] = 1.0
    c["c_sel"] = sel.reshape(128, 512)
    return c


CFG = dict(ffn=1, mix=1, ple=1, ssm=1, attn=1)
_NC_CACHE = {}


def kernel(**inputs):
    cfg = dict(CFG)
    key = tuple(sorted(cfg.items()))
    if key not in _NC_CACHE:
        _NC_CACHE[key] = build(cfg)
    nc = _NC_CACHE[key]
    f = lambda a: np.ascontiguousarray(np.asarray(a, dtype=np.float32))
    consts = make_consts()
    shared = {n: f(inputs[n]) for n in list(WSHAPES) + list(VNAMES)}
    shared.update(consts)
    in_maps = []
    for b in range(NCORES):
        m = dict(shared)
        sl = slice(16 * b, 16 * b + 16)
        m["x_p"] = f(inputs["x_prompt"][b])
        m["x_s"] = f(inputs["x_sample"][sl]).reshape(128, D)
        m["p_p"] = f(inputs["p_prompt"][:, b])
        m["p_s"] = f(inputs["p_sample"][:, sl]).reshape(2, 128, 256)
        m["ck"] = f(inputs["cache_k"][:, sl]).reshape(2, 16, 128, 128)
        m["cv"] = f(inputs["cache_v"][:, sl]).reshape(2, 16, 128, 128)
        m["st_re"] = f(inputs["state_ssm_re"][:, sl]).reshape(2, 16, 2048)
        m["st_im"] = f(inputs["state_ssm_im"][:, sl]).reshape(2, 16, 2048)
        in_maps.append(m)
    res = run_bass_kernel_spmd(nc, in_maps, core_ids=list(range(NCORES)))
    rs = res.results
    y = np.stack([r["y"] for r in rs])
    y_prompt = np.ascontiguousarray(y[:, :2048, :])
    y_sample = np.ascontiguousarray(y[:, 2048:, :].reshape(128, 8, D))
    kp = np.stack([r["kp"] for r in rs], axis=1).reshape(2, 8, 128, 2, 64)
    vp = np.stack([r["vp"] for r in rs], axis=1).reshape(2, 8, 128, 2, 64)
    hp_re = np.stack([r["hp_re"] for r in rs], axis=1).reshape(2, 8, 32, 64)
    hp_im = np.stack([r["hp_im"] for r in rs], axis=1).reshape(2, 8, 32, 64)
    ks = np.concatenate([r["ks"] for r in rs], axis=1).reshape(2, 128, 128, 2, 64)
    vs = np.concatenate([r["vs"] for r in rs], axis=1).reshape(2, 128, 128, 2, 64)
    hs_re = np.concatenate([r["hs_re"] for r in rs], axis=1).reshape(2, 128, 32, 64)
    hs_im = np.concatenate([r["hs_im"] for r in rs], axis=1).reshape(2, 128, 32, 64)
    return (y_prompt, y_sample, kp, vp, hp_re, hp_im, ks, vs, hs_re, hs_im)
```

```python
import math
from contextlib import ExitStack
import numpy as np
import concourse.bass as bass
import concourse.mybir as mybir
from concourse.bass_utils import run_bass_kernel_spmd

F32 = mybir.dt.float32
BF16 = mybir.dt.bfloat16
AF = mybir.ActivationFunctionType
ALU = mybir.AluOpType
AX = mybir.AxisListType

ENGS = ("pe", "dve", "act", "pool", "sp")
NCORES = 8
D = 1024
DFF = 2816
NTOK = 2176
TT = [(0, 512), (512, 512), (1024, 512), (1536, 512), (2048, 128)]
EPS = 1e-6
NEG = -1e30


class Res:
    __slots__ = ("name", "w", "r", "excl")

    def __init__(self, name):
        self.name = name
        self.w = None
        self.r = {}
        self.excl = name[0] == "ps"


class Sched:
    def __init__(self, nc):
        self.nc = nc
        self.ops = {e: [] for e in ENGS}
        self.trace = {e: [] for e in ENGS}
        self.cnt = {}
        self.sems = {}
        self.waited = {e: {} for e in ENGS}
        self.res = {}

    def sem(self, key):
        if key not in self.cnt:
            self.cnt[key] = 0
        return key

    def R(self, *key):
        r = self.res.get(key)
        if r is None:
            r = Res(key)
            self.res[key] = r
        return r

    def _deps(self, reads, writes):
        deps = {}

        def add(k, v):
            if deps.get(k, 0) < v:
                deps[k] = v
        for r in reads:
            if r.w:
                add(*r.w)
        for r in writes:
            if r.w:
                add(*r.w)
            for k, v in r.r.items():
                add(k, v)
        return deps

    def _emit_waits(self, eng, deps):
        wd = self.waited[eng]
        for k, v in deps.items():
            if wd.get(k, 0) >= v:
                continue
            if k == eng and (eng == "pe" or v > self.cnt.get(eng, 0)):
                continue
            wd[k] = v
            self.sem(k)
            self.ops[eng].append(lambda e, k=k, v=v: e.wait_ge(self.sems[k], v))
            self.trace[eng].append(("w", k, v))

    def _commit(self, tok, reads, writes):
        k, v = tok
        for r in reads:
            if r.r.get(k, 0) < v:
                r.r[k] = v
        for r in writes:
            r.w = tok
            r.r = {}

    @staticmethod
    def _flat(xs):
        out = []
        for x in xs:
            if isinstance(x, (list, tuple)):
                out.extend(Sched._flat(x))
            else:
                out.append(x)
        return out

    def op(self, eng, fn, reads=(), writes=(), sig=True):
        reads, writes = self._flat(reads), self._flat(writes)
        ex = [r for r in reads if r.excl]
        if ex:
            writes = list(writes) + ex
        self._emit_waits(eng, self._deps(reads, writes))
        self.sem(eng)
        if sig:
            self.cnt[eng] += 1
            self.ops[eng].append(lambda e, fn=fn, k=eng: fn(e).then_inc(self.sems[k], 1))
            self.trace[eng].append(("i", eng, 1))
            self._commit((eng, self.cnt[eng]), reads, writes)
        else:
            self.ops[eng].append(lambda e, fn=fn: fn(e))
            self._commit((eng, self.cnt[eng] + 1), reads, writes)

    def dma(self, q, fn, semkey, reads=(), writes=()):
        reads, writes = self._flat(reads), self._flat(writes)
        semkey = ("d",) + tuple((writes[0] if writes else reads[0]).name)
        self._emit_waits(q, self._deps(reads, writes))
        self.sem(semkey)
        self.cnt[semkey] += 16
        self.ops[q].append(lambda e, fn=fn, k=semkey: fn(e).then_inc(self.sems[k], 16))
        self.trace[q].append(("i", semkey, 16))
        self._commit((semkey, self.cnt[semkey]), reads, writes)

    def wait_all(self, eng):
        deps = {}
        for r in self.res.values():
            toks = list(r.r.items()) + ([r.w] if r.w else [])
            for k, v in toks:
                if deps.get(k, 0) < v:
                    deps[k] = v
        self._emit_waits(eng, deps)

    def check_deadlock(self):
        val = {k: 0 for k in self.cnt}
        pos = {e: 0 for e in ENGS}
        progress = True
        while progress:
            progress = False
            for e in ENGS:
                tr = self.trace[e]
                while pos[e] < len(tr):
                    kind, k, v = tr[pos[e]]
                    if kind == "w":
                        if val[k] < v:
                            break
                    else:
                        val[k] += v
                    pos[e] += 1
                    progress = True
        for e in ENGS:
            if pos[e] < len(self.trace[e]):
                raise RuntimeError("DEADLOCK: %s stuck at %s" % (e, self.trace[e][pos[e]],))

    def emit(self, stack):
        ops = self.ops
        self.check_deadlock()
        for i, key in enumerate(self.cnt):
            self.sems[key] = stack.enter_context(self.nc.semaphore("sem%d" % i))
        block = stack.enter_context(self.nc.Block())

        @block.tensor
        def _(e):
            for f in ops["pe"]:
                f(e)

        @block.vector
        def _(e):
            for f in ops["dve"]:
                f(e)

        @block.scalar
        def _(e):
            for f in ops["act"]:
                f(e)

        @block.gpsimd
        def _(e):
            for f in ops["pool"]:
                f(e)

        @block.sync
        def _(e):
            for f in ops["sp"]:
                f(e)


WNAMES = ["ffn1_w_gate", "ffn1_w_up", "ffn1_w_down", "w_in", "ssm_w_glu", "w_out",
          "ffn2_w_gate", "ffn2_w_up", "ffn2_w_down", "ple_w_gate", "ple_w_proj"]
WSHAPES = {"ffn1_w_gate": [2, D, DFF], "ffn1_w_up": [2, D, DFF], "ffn1_w_down": [2, DFF, D], "w_in": [2, D, 1280],
           "ssm_w_glu": [2, 512, 512], "w_out": [2, D, D], "ffn2_w_gate": [2, D, DFF], "ffn2_w_up": [2, D, DFF],
           "ffn2_w_down": [2, DFF, D], "ple_w_gate": [2, D, D], "ple_w_proj": [2, 256, D]}
VNAMES = {"ffn1_norm": [2, D], "mix_norm": [2, D], "ffn2_norm": [2, D], "ple_norm": [2, D], "final_norm": [D],
          "ssm_out_norm": [2, 512], "attn_out_norm": [2, 512], "attn_sinks": [2, 8],
          "ssm_lam_re": [2, 32, 64], "ssm_lam_im": [2, 32, 64], "ssm_log_dt": [2, 32],
          "ssm_b_re": [2, 32, 64, 16], "ssm_b_im": [2, 32, 64, 16], "ssm_c_re": [2, 32, 16, 64],
          "ssm_c_im": [2, 32, 16, 64], "ssm_d": [2, 512]}
CNAMES = {"c_ident": [128, 128], "c_amask": [128, 256], "c_smask": [128, 136], "c_maskT": [128, 128],
          "c_sel": [128, 512]}
INAMES = {"x_p": [2048, D], "x_s": [128, D], "p_p": [2, 2048, 256], "p_s": [2, 128, 256],
          "ck": [2, 16, 128, 128], "cv": [2, 16, 128, 128], "st_re": [2, 16, 2048], "st_im": [2, 16, 2048]}
ONAMES = {"y": [NTOK, D], "kp": [2, 128, 128], "vp": [2, 128, 128], "hp_re": [2, 2048], "hp_im": [2, 2048],
          "ks": [2, 16, 128, 128], "vs": [2, 16, 128, 128], "hs_re": [2, 16, 2048], "hs_im": [2, 16, 2048]}


def build(cfg):
    nc = bass.Bass("TRN2", target_bir_lowering=False)
    dr = {}
    for n, s in list(WSHAPES.items()) + list(VNAMES.items()) + list(CNAMES.items()) + list(INAMES.items()):
        dr[n] = nc.dram_tensor(n, s, F32, kind="ExternalInput").ap()
    for n, s in ONAMES.items():
        dr[n] = nc.dram_tensor(n, s, F32, kind="ExternalOutput").ap()
    st = ExitStack()
    with st:
        S = Sched(nc)
        R = S.R

        def T(name, shape, dt):
            return st.enter_context(nc.sbuf_tensor(name, shape, dt))

        X = T("X", [128, 8, NTOK], F32)
        HT = T("HT", [128, 8, NTOK], BF16)
        R1 = T("R1", [128, 6 * NTOK], BF16)
        WP = T("WP", [128, 3, 4096], BF16)
        RG = T("RG", [128, 4 * NTOK], BF16)
        SPR = T("SPR", [128, 10240], BF16)
        identF = T("identF", [128, 128], F32)
        identB = T("identB", [128, 128], BF16)
        onesB = T("onesB", [128, 128], BF16)
        amask = T("amask", [128, 256], F32)
        amaskB = T("amaskB", [128, 256], BF16)
        smask = T("smask", [128, 136], F32)
        GN = T("GN", [128, 9, 8], F32)
        GN2 = T("GN2", [128, 4, 4], F32)
        SK = T("SK", [128, 16], F32)
        SM = T("SM", [128, 64], F32)
        SV = T("SV", [128, 48, 16], F32)
        E0 = T("E0", [128, 2, 16, 32], F32)
        E1 = T("E1", [128, 2, 16, 16], F32)
        H0 = T("H0", [128, 2, 16, 16], F32)
        SelC = T("SelC", [128, 4, 128], BF16)
        PTMP = T("PTMP", [128, 512], F32)
        SVL = T("SVL", [128, 8, 16], F32)
        maskT = T("maskT", [128, 128], F32)
        PS = [st.enter_context(nc.psum_tensor("ps%d" % i, [128, 512], F32)) for i in range(8)]

        def PR(i):
            return R("ps", i)

        def r1(a, b):
            return [R("R1b", k) for k in range(a // 32, (b - 1) // 32 + 1)]

        def R1r(c, t):
            return r1(c * NTOK + TT[t][0], c * NTOK + TT[t][0] + TT[t][1])

        def rg(a, b):
            return [R("RG", k) for k in range(a // 64, (b - 1) // 64 + 1)]

        def spr(a, b):
            return [R("SPR", k) for k in range(a // 64, (b - 1) // 64 + 1)]

        def XR(c, t):
            return R("X", c, t)

        def HR(c, t):
            return R("HT", c, t)

        wq = {"n": 0}

        def wload(dmas):
            s = wq["n"] % 3
            wq["n"] += 1
            slot = WP[:, s, :]
            for ov, ia in dmas:
                S.dma("pool", lambda e, o=ov(slot), i=ia: e.dma_start(out=o, in_=i), ("dw", s), writes=[R("wp", s)])
            return slot, R("wp", s)

        def wview(l, name):
            return dr[name][l].rearrange("(c p) f -> p c f", p=128)

        def load_consts():
            S.dma("sp", lambda e: e.dma_start(out=identF[:], in_=dr["c_ident"]), "d_c", writes=[R("identF")])
            S.dma("sp", lambda e: e.dma_start(out=amask[:], in_=dr["c_amask"]), "d_c", writes=[R("amask")])
            S.dma("sp", lambda e: e.dma_start(out=smask[:], in_=dr["c_smask"]), "d_c", writes=[R("smask")])
            S.dma("sp", lambda e: e.dma_start(out=maskT[:], in_=dr["c_maskT"]), "d_c", writes=[R("maskT")])
            S.dma("pool", lambda e: e.dma_start(out=SelC[:].rearrange("p a b -> p (a b)"), in_=dr["c_sel"]), "d_sel", writes=[R("SelC")])
            S.op("dve", lambda e: e.tensor_copy(out=identB[:], in_=identF[:]), reads=[R("identF")], writes=[R("identB")])
            S.op("dve", lambda e: e.memset(onesB[:], 1.0), writes=[R("onesB")])
            S.op("dve", lambda e: e.tensor_copy(out=amaskB[:], in_=amask[:]), reads=[R("amask")], writes=[R("amaskB")])
            for l in range(2):
                for k, n in enumerate(["ffn1_norm", "mix_norm", "ffn2_norm", "ple_norm"]):
                    S.dma("sp", lambda e, l=l, k=k, n=n: e.dma_start(
                        out=GN[:, l * 4 + k, :], in_=dr[n][l].rearrange("(c p) -> p c", p=128),
                        allow_slow_non_contiguous=True), "d_c", writes=[R("GN")])
                for k, n in enumerate(["ssm_out_norm", "attn_out_norm"]):
                    S.dma("sp", lambda e, l=l, k=k, n=n: e.dma_start(
                        out=GN2[:, l * 2 + k, :], in_=dr[n][l].rearrange("(c p) -> p c", p=128),
                        allow_slow_non_contiguous=True), "d_c", writes=[R("GN")])
            S.dma("sp", lambda e: e.dma_start(out=GN[:, 8, :], in_=dr["final_norm"].rearrange("(c p) -> p c", p=128),
                                              allow_slow_non_contiguous=True), "d_c", writes=[R("GN")])
            S.dma("sp", lambda e: e.dma_start(out=SK[:], in_=dr["attn_sinks"].rearrange("l h -> (l h)").partition_broadcast(128)),
                  "d_c", writes=[R("SK")])

        def load_x():
            xin = [RG[:, 2048 * b:2048 * b + 2048].bitcast(F32) for b in range(4)]
            for i in range(17):
                b = i % 4
                src = dr["x_p"][128 * i:128 * i + 128, :] if i < 16 else dr["x_s"]
                S.dma("sp", lambda e, b=b, src=src: e.dma_start(out=xin[b], in_=src), ("d_xin", b), writes=rg(2048 * b, 2048 * b + 2048))
                t = i // 4 if i < 16 else 4
                for h in range(2):
                    pb = 6 + h
                    for cc in range(4):
                        c = h * 4 + cc
                        S.op("pe", lambda e, b=b, c=c, cc=cc, pb=pb: e.transpose(PS[pb][:, cc * 128:cc * 128 + 128],
                                                                               xin[b][:, c * 128:c * 128 + 128], identF[:]),
                             reads=rg(2048 * b, 2048 * b + 2048) + [R("identF")], writes=[PR(pb)], sig=(cc == 3))
                    eng = "dve" if h == 0 else "act"
                    dst = X[:, h * 4:h * 4 + 4, 128 * i:128 * i + 128]
                    srcp = PS[pb][:, :].rearrange("p (c n) -> p c n", c=4)
                    if eng == "dve":
                        S.op("dve", lambda e, dst=dst, srcp=srcp: e.tensor_copy(out=dst, in_=srcp),
                             reads=[PR(pb)], writes=[XR(h * 4 + cc, t) for cc in range(4)])
                    else:
                        S.op("act", lambda e, dst=dst, srcp=srcp: e.activation(out=dst, in_=srcp, func=AF.Copy),
                             reads=[PR(pb)], writes=[XR(h * 4 + cc, t) for cc in range(4)])

        def rmsnorm(src, srcR, dst, dstR, nch, gain, tiles=range(5), psb=5):
            SQ = [RG[:, 4096:4608], RG[:, 4608:5120]]
            RS = RG[:, 5120:6144].bitcast(F32)
            for t in tiles:
                lo, n = TT[t]
                for c in range(nch):
                    b = c % 2
                    S.op("act", lambda e, c=c, b=b, lo=lo, n=n: e.activation(out=SQ[b][:, 0:n], in_=src(c, lo, n), func=AF.Square),
                         reads=[srcR(c, t)], writes=rg(4096 + 512 * b, 4608 + 512 * b))
                    S.op("pe", lambda e, c=c, b=b, n=n: e.matmul(PS[psb][:, 0:n], lhsT=onesB[:], rhs=SQ[b][:, 0:n],
                                                                 start=(c == 0), stop=(c == nch - 1)),
                         reads=rg(4096 + 512 * b, 4608 + 512 * b) + [R("onesB")], writes=[PR(psb)], sig=True)
                S.op("act", lambda e, n=n: e.activation(out=RS[:, 0:n], in_=PS[psb][:, 0:n], func=AF.Sqrt,
                                                         scale=1.0 / (128 * nch), bias=EPSB[:, 0:1]),
                     reads=[PR(psb), R("EPSB")], writes=rg(5120, 6144))
                S.op("dve", lambda e, n=n: e.reciprocal(out=RS[:, 0:n], in_=RS[:, 0:n]), reads=rg(5120, 6144), writes=rg(5120, 6144))
                for c in range(nch):
                    S.op("dve", lambda e, c=c, lo=lo, n=n: e.scalar_tensor_tensor(
                        out=dst(c, lo, n), in0=src(c, lo, n), scalar=gain(c), in1=RS[:, 0:n], op0=ALU.mult, op1=ALU.mult),
                        reads=[srcR(c, t), R("GN")] + rg(5120, 6144), writes=[dstR(c, t)])

        EPSB = T("EPSB", [128, 1], F32)
        S.op("dve", lambda e: e.memset(EPSB[:], EPS), writes=[R("EPSB")])

        def norm_X(gi):
            rmsnorm(lambda c, lo, n: X[:, c, lo:lo + n], XR, lambda c, lo, n: HT[:, c, lo:lo + n], HR, 8,
                    lambda c: GN[:, gi, c:c + 1])

        def ffn(l, pre, mid_hook=None):
            norm_X(l * 4 + (0 if pre == "ffn1" else 2))
            A = R1[:, :].rearrange("p (c n) -> p c n", c=6)
            SG = [RG[:, 6144:7168].bitcast(F32), RG[:, 7168:8192].bitcast(F32)]
            quarters = [[0, 1, 2], [3, 4, 5], [6, 7, 8], [9, 10]]
            wg, wu, wd = wview(l, pre + "_w_gate"), wview(l, pre + "_w_up"), dr[pre + "_w_down"][l]
            k = 0
            for q in quarters:
                if q is quarters[1] and mid_hook is not None:
                    mid_hook()
                for pi, jp in enumerate(q):
                    f0 = jp * 256
                    slot, sr = wload([
                        (lambda s: s[:, 0:2048].rearrange("p (c f) -> p c f", c=8), wg[:, :, f0:f0 + 256]),
                        (lambda s: s[:, 2048:4096].rearrange("p (c f) -> p c f", c=8), wu[:, :, f0:f0 + 256])])
                    sv = slot.rearrange("p (g c f) -> p g c f", g=2, c=8)
                    for fc in range(2):
                        ai = pi * 2 + fc
                        for t in range(5):
                            lo, n = TT[t]
                            pg, pu = k % 2, 2 + k % 2
                            for c in range(8):
                                S.op("pe", lambda e, c=c, fc=fc, pg=pg, lo=lo, n=n, sv=sv: e.matmul(
                                    PS[pg][:, 0:n], lhsT=sv[:, 0, c, fc * 128:fc * 128 + 128], rhs=HT[:, c, lo:lo + n],
                                    start=(c == 0), stop=(c == 7)), reads=[sr, HR(c, t)], writes=[PR(pg)], sig=(c == 7))
                            for c in range(8):
                                S.op("pe", lambda e, c=c, fc=fc, pu=pu, lo=lo, n=n, sv=sv: e.matmul(
                                    PS[pu][:, 0:n], lhsT=sv[:, 1, c, fc * 128:fc * 128 + 128], rhs=HT[:, c, lo:lo + n],
                                    start=(c == 0), stop=(c == 7)), reads=[sr, HR(c, t)], writes=[PR(pu)], sig=(c == 7))
                            S.op("act", lambda e, pg=pg, n=n, b=k % 2: e.activation(out=SG[b][:, 0:n], in_=PS[pg][:, 0:n], func=AF.Silu),
                                 reads=[PR(pg)], writes=rg(6144 + 1024 * (k % 2), 7168 + 1024 * (k % 2)))
                            S.op("dve", lambda e, pu=pu, n=n, lo=lo, ai=ai, b=k % 2: e.tensor_tensor(
                                out=A[:, ai, lo:lo + n], in0=PS[pu][:, 0:n], in1=SG[b][:, 0:n], op=ALU.mult),
                                reads=[PR(pu)] + rg(6144 + 1024 * (k % 2), 7168 + 1024 * (k % 2)), writes=[R1r(ai, t)])
                            k += 1
                nj = 2 * len(q)
                r0 = q[0] * 256
                for dh in range(2):
                    slot, sr = wload([(lambda s, nj=nj: s[:, 0:nj * 512].rearrange("p (j d) -> p j d", j=nj),
                                       wd[r0:r0 + nj * 128, dh * 512:dh * 512 + 512].rearrange("(j p) d -> p j d", p=128))])
                    sv = slot[:, 0:nj * 512].rearrange("p (j d) -> p j d", j=nj)
                    for i in range(4):
                        dc = dh * 4 + i
                        for t in range(5):
                            lo, n = TT[t]
                            pd = 4 + k % 2
                            for jj in range(nj):
                                S.op("pe", lambda e, jj=jj, i=i, pd=pd, lo=lo, n=n, sv=sv, nj=nj: e.matmul(
                                    PS[pd][:, 0:n], lhsT=sv[:, jj, i * 128:i * 128 + 128], rhs=A[:, jj, lo:lo + n],
                                    start=(jj == 0), stop=(jj == nj - 1)), reads=[sr, R1r(jj, t)], writes=[PR(pd)], sig=(jj == nj - 1))
                            S.op("dve", lambda e, pd=pd, dc=dc, lo=lo, n=n: e.scalar_tensor_tensor(
                                out=X[:, dc, lo:lo + n], in0=PS[pd][:, 0:n], scalar=0.5, in1=X[:, dc, lo:lo + n],
                                op0=ALU.mult, op1=ALU.add), reads=[PR(pd), XR(dc, t)], writes=[XR(dc, t)])
                            k += 1

        def ple_load(l):
            PT = SPR[:, 0:2 * NTOK].rearrange("p (c n) -> p c n", c=2)
            pin = [RG[:, 512 * b:512 * b + 512].bitcast(F32) for b in range(8)]
            def pdma(i):
                b = i % 8
                src = dr["p_p"][l, 128 * i:128 * i + 128, :] if i < 16 else dr["p_s"][l]
                S.dma("sp", lambda e, b=b, src=src: e.dma_start(out=pin[b], in_=src), None, writes=rg(512 * b, 512 * b + 512))
            for i in range(8):
                pdma(i)
            for i in range(17):
                b = i % 8
                for c in range(2):
                    S.op("pe", lambda e, b=b, c=c: e.transpose(PS[7][:, c * 128:c * 128 + 128], pin[b][:, c * 128:c * 128 + 128], identF[:]),
                         reads=rg(512 * b, 512 * b + 512) + [R("identF")], writes=[PR(7)], sig=(c == 1))
                S.op("act", lambda e, i=i: e.activation(out=PT[:, :, 128 * i:128 * i + 128],
                                                         in_=PS[7][:, 0:256].rearrange("p (c n) -> p c n", c=2), func=AF.Copy),
                     reads=[PR(7)], writes=spr(128 * i, 128 * i + 128) + spr(NTOK + 128 * i, NTOK + 128 * i + 128))
                if i + 8 < 17:
                    pdma(i + 8)

        def ple(l):
            norm_X(l * 4 + 3)
            PT = SPR[:, 0:2 * NTOK].rearrange("p (c n) -> p c n", c=2)
            SG = [RG[:, 6144:7168].bitcast(F32), RG[:, 7168:8192].bitcast(F32)]
            wp_ = dr["ple_w_proj"][l].rearrange("(c p) f -> p c f", p=128)
            wg = wview(l, "ple_w_gate")
            k = 0
            for dh in range(2):
                pslot, psr = wload([(lambda s: s[:, 0:1024].rearrange("p (c f) -> p c f", c=2), wp_[:, :, dh * 512:dh * 512 + 512])])
                pv = pslot[:, 0:1024].rearrange("p (c f) -> p c f", c=2)
                gslot, gsr = wload([(lambda s: s[:, 0:4096].rearrange("p (c f) -> p c f", c=8), wg[:, :, dh * 512:dh * 512 + 512])])
                gv = gslot.rearrange("p (c f) -> p c f", c=8)
                for i in range(4):
                    dc = dh * 4 + i
                    for t in range(5):
                        lo, n = TT[t]
                        pg, pp = k % 2, 2 + k % 2
                        for c in range(8):
                            S.op("pe", lambda e, c=c, i=i, pg=pg, lo=lo, n=n, gv=gv: e.matmul(
                                PS[pg][:, 0:n], lhsT=gv[:, c, i * 128:i * 128 + 128], rhs=HT[:, c, lo:lo + n],
                                start=(c == 0), stop=(c == 7)), reads=[gsr, HR(c, t)], writes=[PR(pg)], sig=(c == 7))
                        for c in range(2):
                            S.op("pe", lambda e, c=c, i=i, pp=pp, lo=lo, n=n, pv=pv: e.matmul(
                                PS[pp][:, 0:n], lhsT=pv[:, c, i * 128:i * 128 + 128], rhs=PT[:, c, lo:lo + n],
                                start=(c == 0), stop=(c == 1)), reads=[psr] + spr(c * NTOK + lo, c * NTOK + lo + n), writes=[PR(pp)], sig=(c == 1))
                        b = k % 2
                        S.op("act", lambda e, pg=pg, n=n, b=b: e.activation(out=SG[b][:, 0:n], in_=PS[pg][:, 0:n], func=AF.Sigmoid),
                             reads=[PR(pg)], writes=rg(6144 + 1024 * b, 7168 + 1024 * b))
                        S.op("dve", lambda e, pp=pp, n=n, b=b: e.tensor_tensor(out=SG[b][:, 0:n], in0=PS[pp][:, 0:n], in1=SG[b][:, 0:n], op=ALU.mult),
                             reads=[PR(pp)] + rg(6144 + 1024 * b, 7168 + 1024 * b), writes=rg(6144 + 1024 * b, 7168 + 1024 * b))
                        S.op("dve", lambda e, dc=dc, lo=lo, n=n, b=b: e.tensor_tensor(out=X[:, dc, lo:lo + n], in0=X[:, dc, lo:lo + n], in1=SG[b][:, 0:n], op=ALU.add),
                             reads=[XR(dc, t)] + rg(6144 + 1024 * b, 7168 + 1024 * b), writes=[XR(dc, t)])
                        k += 1

        MO = HT

        def MR(c, t):
            return R("HT", c, t)

        def mixer(l):
            norm_X(l * 4 + 1)
            QT = R1[:, 0:4 * NTOK].rearrange("p (c n) -> p c n", c=4)
            KD = R1[:, 4 * NTOK:6 * NTOK].rearrange("p (c n) -> p c n", c=2)
            VT = SPR[:, 0:17 * 128].rearrange("p (i f) -> p i f", i=17)
            KVO = SPR[:, 2176:2176 + 1024].bitcast(F32).rearrange("p (i f) -> p i f", i=2)
            win = wview(l, "w_in")
            if cfg.get("ssm", 1):
                ssm(l)
            qslot, qsr = wload([(lambda s: s.rearrange("p (c f) -> p c f", c=8), win[:, :, 512:1024])])
            qv = qslot.rearrange("p (c f) -> p c f", c=8)

            def kvd(s, a, b):
                return s.rearrange("p (c f) -> p c f", c=8)[:, :, a:b]
            kslot, ksr = wload([
                (lambda s: kvd(s, 0, 64), win[:, :, 1024:1088]), (lambda s: kvd(s, 64, 128), win[:, :, 1024:1088]),
                (lambda s: kvd(s, 128, 192), win[:, :, 1088:1152]), (lambda s: kvd(s, 192, 256), win[:, :, 1088:1152]),
                (lambda s: kvd(s, 256, 384), win[:, :, 1152:1280]), (lambda s: kvd(s, 384, 512), win[:, :, 1024:1152])])
            kv_ = kslot.rearrange("p (c f) -> p c f", c=8)
            k = 0
            for qc in range(4):
                for t in range(5):
                    lo, n = TT[t]
                    pb = k % 2
                    for c in range(8):
                        S.op("pe", lambda e, c=c, qc=qc, pb=pb, lo=lo, n=n: e.matmul(
                            PS[pb][:, 0:n], lhsT=qv[:, c, qc * 128:qc * 128 + 128], rhs=HT[:, c, lo:lo + n],
                            start=(c == 0), stop=(c == 7)), reads=[qsr, HR(c, t)], writes=[PR(pb)], sig=(c == 7))
                    S.op("act", lambda e, qc=qc, pb=pb, lo=lo, n=n: e.activation(out=QT[:, qc, lo:lo + n], in_=PS[pb][:, 0:n], func=AF.Copy),
                         reads=[PR(pb)], writes=[R1r(qc, t)])
                    k += 1
            for kc in (range(2) if cfg.get("mixlevel", 9) >= 2 else []):
                for t in range(5):
                    lo, n = TT[t]
                    pb = k % 2
                    for c in range(8):
                        S.op("pe", lambda e, c=c, kc=kc, pb=pb, lo=lo, n=n: e.matmul(
                            PS[pb][:, 0:n], lhsT=kv_[:, c, kc * 128:kc * 128 + 128], rhs=HT[:, c, lo:lo + n],
                            start=(c == 0), stop=(c == 7)), reads=[ksr, HR(c, t)], writes=[PR(pb)], sig=(c == 7))
                    S.op("act", lambda e, kc=kc, pb=pb, lo=lo, n=n: e.activation(out=KD[:, kc, lo:lo + n], in_=PS[pb][:, 0:n], func=AF.Copy),
                         reads=[PR(pb)], writes=[R1r(4 + kc, t)])
                    k += 1
            for i in (range(17) if cfg.get("mixlevel", 9) >= 3 else []):
                t = i // 4 if i < 16 else 4
                pb = k % 2
                for c in range(8):
                    S.op("pe", lambda e, c=c, i=i, pb=pb: e.matmul(
                        PS[pb][:, 0:256], lhsT=HT[:, c, 128 * i:128 * i + 128], rhs=kv_[:, c, 256:512],
                        start=(c == 0), stop=(c == 7)), reads=[ksr, HR(c, t)], writes=[PR(pb)], sig=(c == 7))
                S.op("act", lambda e, i=i, pb=pb: e.activation(out=VT[:, i, :], in_=PS[pb][:, 0:128], func=AF.Copy),
                     reads=[PR(pb)], writes=spr(128 * i, 128 * i + 128))
                if i >= 15:
                    S.op("dve", lambda e, i=i, pb=pb: e.tensor_copy(out=KVO[:, i - 15, :], in_=PS[pb][:, 0:256]),
                         reads=[PR(pb)], writes=spr(2176 + 512 * (i - 15), 2176 + 512 * (i - 15) + 512))
                k += 1
            VNa = [SPR[:, 7296:8320], SPR[:, 9024:10048]]
            rVN = spr(7296, 8320) + spr(9024, 10048)
            for s4 in (range(4) if cfg.get("mixlevel", 9) >= 3 else []):
                for sl in range(4):
                    s = s4 * 4 + sl
                    for c in range(8):
                        S.op("pe", lambda e, c=c, s=s, sl=sl: e.matmul(
                            PS[6][0:8, sl * 128:sl * 128 + 128], lhsT=HT[:, c, 2048 + 8 * s:2048 + 8 * s + 8], rhs=kv_[:, c, 256:384],
                            start=(c == 0), stop=(c == 7)), reads=[ksr, HR(c, 4)], writes=[PR(6)], sig=(c == 7 and sl == 3))
                S.op("act", lambda e, s4=s4: e.activation(out=VNa[s4 // 2][0:8, (s4 % 2) * 512:(s4 % 2) * 512 + 512],
                                                           in_=PS[6][0:8, :], func=AF.Copy),
                     reads=[PR(6)], writes=rVN)
            if cfg.get("mixlevel", 9) >= 4:
                S.dma("sp", lambda e: e.dma_start(out=dr["vp"][l], in_=KVO[:, 0, 0:128]), "d_out", reads=spr(2176, 2688))
                S.dma("sp", lambda e: e.dma_start(out=dr["kp"][l], in_=KVO[:, 0, 128:256]), "d_out", reads=spr(2176, 2688))
            if cfg.get("ssm", 1):
                glu(l)
            else:
                for c in range(4):
                    S.op("dve", lambda e, c=c: e.memset(MO[:, c, :], 0.0), reads=[], writes=[MR(c, t) for t in range(5)])
            if cfg.get("attn", 1):
                if cfg.get("attn_p", 1):
                    attn_prompt(l, QT, KD, VT)
                if cfg.get("attn_s", 1):
                    attn_sample(l, QT, KD, VT, KVO)
                rmsnorm(lambda c, lo, n: MO[:, 4 + c, lo:lo + n], lambda c, t: MR(4 + c, t),
                        lambda c, lo, n: MO[:, 4 + c, lo:lo + n], lambda c, t: MR(4 + c, t), 4,
                        lambda c: GN2[:, l * 2 + 1, c:c + 1])
            else:
                for c in range(4, 8):
                    S.op("dve", lambda e, c=c: e.memset(MO[:, c, :], 0.0), reads=[], writes=[MR(c, t) for t in range(5)])
                zero_kv_sample(l, KVO)
            wo = wview(l, "w_out")
            k = 0
            for dh in range(2):
                slot, sr = wload([(lambda s: s.rearrange("p (c f) -> p c f", c=8), wo[:, :, dh * 512:dh * 512 + 512])])
                sv = slot.rearrange("p (c f) -> p c f", c=8)
                for i in range(4):
                    dc = dh * 4 + i
                    for t in range(5):
                        lo, n = TT[t]
                        pb = 2 + k % 2
                        for c in range(8):
                            S.op("pe", lambda e, c=c, i=i, pb=pb, lo=lo, n=n, sv=sv: e.matmul(
                                PS[pb][:, 0:n], lhsT=sv[:, c, i * 128:i * 128 + 128], rhs=MO[:, c, lo:lo + n],
                                start=(c == 0), stop=(c == 7)), reads=[sr, MR(c, t)], writes=[PR(pb)], sig=(c == 7))
                        S.op("dve", lambda e, pb=pb, dc=dc, lo=lo, n=n: e.tensor_tensor(
                            out=X[:, dc, lo:lo + n], in0=PS[pb][:, 0:n], in1=X[:, dc, lo:lo + n], op=ALU.add),
                            reads=[PR(pb), XR(dc, t)], writes=[XR(dc, t)])
                        k += 1

        def zero_kv_sample(l, KVO):
            pass

        def softmax_block(l, kvh, psA, psB, nk, maskap, Sf, rSf, Pb, rPb, so=0, part="AB", pemask=False):
            mx, ng, sm, es = SM[:, so:so + 4], SM[:, so + 4:so + 8], SM[:, so + 8:so + 12], SM[:, so + 12:so + 16]
            rmx, rng_, rsm, res_ = R("SMx", so), R("SMn", so), R("SMs", so), R("SMe", so)
            for h2, pp in (enumerate((psA, psB)) if ("A" in part and not pemask) else []):
                S.op("dve", lambda e, h2=h2, pp=pp: e.tensor_tensor(
                    out=Sf[:, h2:4:2, 0:nk], in0=PS[pp][:, :].rearrange("p (g k) -> p g k", g=2)[:, :, 0:nk],
                    in1=maskap.unsqueeze(1).to_broadcast([128, 2, nk]), op=ALU.add),
                    reads=[PR(pp), R("amask"), R("smask")], writes=rSf)
            if "A" not in part:
                S.op("dve", lambda e: e.tensor_tensor(out=sm, in0=sm, in1=es, op=ALU.add), reads=[rsm, res_], writes=[rsm])
                S.op("dve", lambda e: e.reciprocal(out=sm, in_=sm), reads=[rsm], writes=[rsm])
                S.op("dve", lambda e: e.tensor_tensor(out=Pb[:, :, 0:nk], in0=Sf[:, :, 0:nk], in1=sm.unsqueeze(2).to_broadcast([128, 4, nk]), op=ALU.mult),
                     reads=rSf + [rsm], writes=rPb)
                return
            if pemask:
                for h2, pp in enumerate((psA, psB)):
                    S.op("dve", lambda e, h2=h2, pp=pp: e.tensor_reduce(
                        out=mx[:, h2:4:2], in_=PS[pp][:, :].rearrange("p (g k) -> p g k", g=2)[:, :, 0:nk], axis=AX.X, op=ALU.max),
                        reads=[PR(pp)], writes=[rmx])
            else:
                S.op("dve", lambda e: e.tensor_reduce(out=mx, in_=Sf[:, :, 0:nk], axis=AX.X, op=ALU.max), reads=rSf, writes=[rmx])
            S.op("dve", lambda e: e.scalar_tensor_tensor(out=mx, in0=mx, scalar=0.125, in1=SK[:, l * 8 + kvh * 4:l * 8 + kvh * 4 + 4],
                                                          op0=ALU.mult, op1=ALU.max), reads=[rmx, R("SK")], writes=[rmx])
            S.op("dve", lambda e: e.tensor_scalar(out=ng, in0=mx, scalar1=-1.0, scalar2=None, op0=ALU.mult), reads=[rmx], writes=[rng_])
            S.op("dve", lambda e: e.tensor_tensor(out=es, in0=SK[:, l * 8 + kvh * 4:l * 8 + kvh * 4 + 4], in1=ng, op=ALU.add),
                 reads=[rng_, R("SK")], writes=[res_])
            S.op("act", lambda e: e.activation(out=es, in_=es, func=AF.Exp), reads=[res_], writes=[res_])
            for g in range(4):
                if pemask:
                    pp = (psA, psB)[g % 2]
                    S.op("act", lambda e, g=g, pp=pp: e.activation(out=Sf[:, g, 0:nk], in_=PS[pp][:, (g // 2) * 256:(g // 2) * 256 + nk], func=AF.Exp,
                                                                    scale=0.125, bias=ng[:, g:g + 1], accum_out=sm[:, g:g + 1]),
                         reads=[PR(pp), rng_], writes=rSf + [rsm])
                else:
                    S.op("act", lambda e, g=g: e.activation(out=Sf[:, g, 0:nk], in_=Sf[:, g, 0:nk], func=AF.Exp, scale=0.125,
                                                             bias=ng[:, g:g + 1], accum_out=sm[:, g:g + 1]),
                         reads=rSf + [rng_], writes=rSf + [rsm])
            if "B" not in part:
                return
            S.op("dve", lambda e: e.tensor_tensor(out=sm, in0=sm, in1=es, op=ALU.add), reads=[rsm, res_], writes=[rsm])
            S.op("dve", lambda e: e.reciprocal(out=sm, in_=sm), reads=[rsm], writes=[rsm])
            S.op("dve", lambda e: e.tensor_tensor(out=Pb[:, :, 0:nk], in0=Sf[:, :, 0:nk], in1=sm.unsqueeze(2).to_broadcast([128, 4, nk]), op=ALU.mult),
                 reads=rSf + [rsm], writes=rPb)

        def attn_prompt(l, QT, KD, VT):
            sets = [
                dict(Sf=SPR[:, 3200:5248].bitcast(F32).rearrange("p (g k) -> p g k", g=4), rSf=spr(3200, 5248),
                     Pb=SPR[:, 5248:6272].rearrange("p (g k) -> p g k", g=4), rPb=spr(5248, 6272),
                     PTs=SPR[:, 6272:7296].rearrange("p (g b q) -> p g b q", g=4, b=2), rPT=spr(6272, 7296), ps=(0, 1, 2, 3), so=0),
                dict(Sf=RG[:, 6144:8192].bitcast(F32).rearrange("p (g k) -> p g k", g=4), rSf=rg(6144, 8192),
                     Pb=RG[:, 0:1024].rearrange("p (g k) -> p g k", g=4), rPb=rg(0, 1024),
                     PTs=RG[:, 1024:2048].rearrange("p (g b q) -> p g b q", g=4, b=2), rPT=rg(1024, 2048), ps=(4, 5, 6, 7), so=16),
            ]

            def geo(it):
                nb, kvh = it // 2, it % 2
                return dict(nb=nb, kvh=kvh, t=nb // 4, tk0=(nb - 1) // 4 if nb > 0 else 0, k0=128 * (nb - 1) if nb > 0 else 0,
                            nk=256 if nb > 0 else 128, mcol=0 if nb > 0 else 128)

            def st_scores(it):
                G, B = geo(it), sets[it % 2]
                nb, kvh, t, tk0, k0, nk = G["nb"], G["kvh"], G["t"], G["tk0"], G["k0"], G["nk"]
                for g in range(4):
                    h = 4 * kvh + g
                    ch, hf = h // 2, h % 2
                    pp = B["ps"][hf]
                    S.op("pe", lambda e, ch=ch, hf=hf, pp=pp, g=g, nb=nb, k0=k0, nk=nk, kvh=kvh: e.matmul(
                        PS[pp][:, (g // 2) * 256:(g // 2) * 256 + nk], lhsT=QT[64 * hf:64 * hf + 64, ch, 128 * nb:128 * nb + 128],
                        rhs=KD[64 * hf:64 * hf + 64, kvh, k0:k0 + nk], start=True, stop=False),
                        reads=[R1r(ch, t), R1r(4 + kvh, t), R1r(4 + kvh, tk0)], writes=[PR(pp)], sig=False)
                    mc = G["mcol"]
                    S.op("pe", lambda e, pp=pp, g=g, nk=nk, mc=mc: e.matmul(
                        PS[pp][:, (g // 2) * 256:(g // 2) * 256 + nk], lhsT=identB[:], rhs=amaskB[:, mc:mc + nk], start=False, stop=True),
                        reads=[R("identB"), R("amaskB")], writes=[PR(pp)], sig=(g >= 2))

            def st_softmax(it, part):
                G, B = geo(it), sets[it % 2]
                softmax_block(l, G["kvh"], B["ps"][0], B["ps"][1], G["nk"], amask[:, G["mcol"]:G["mcol"] + G["nk"]],
                              B["Sf"], B["rSf"], B["Pb"], B["rPb"], B["so"], part, pemask=True)

            def st_pv(it):
                G, B = geo(it), sets[it % 2]
                nb, kvh, t, nk = G["nb"], G["kvh"], G["t"], G["nk"]
                Pb, PTs, pT, pO = B["Pb"], B["PTs"], B["ps"][2], B["ps"][3]
                nkb = nk // 128
                ptp = PS[pT][:, :].bitcast(BF16).rearrange("p (g b q) -> p g b q", g=4, b=2)
                for g in range(4):
                    for kb in range(nkb):
                        S.op("pe", lambda e, g=g, kb=kb, ptp=ptp, Pb=Pb: e.transpose(ptp[:, g, kb, :], Pb[:, g, kb * 128:kb * 128 + 128], identB[:]),
                             reads=B["rPb"] + [R("identB")], writes=[PR(pT)], sig=(g == 3 and kb == nkb - 1))
                S.op("act", lambda e, nkb=nkb, ptp=ptp, PTs=PTs: e.activation(out=PTs[:, :, 0:nkb, :], in_=ptp[:, :, 0:nkb, :], func=AF.Copy),
                     reads=[PR(pT)], writes=B["rPT"])
                for g in range(4):
                    h = 4 * kvh + g
                    hf = h % 2
                    for kb in range(nkb):
                        vi = nb - 1 + kb if nb > 0 else 0
                        S.op("pe", lambda e, g=g, kb=kb, hf=hf, vi=vi, kvh=kvh, nkb=nkb, PTs=PTs, pO=pO: e.matmul(
                            PS[pO][64 * hf:64 * hf + 64, (g // 2) * 128:(g // 2) * 128 + 128],
                            lhsT=VT[:, vi, kvh * 64:kvh * 64 + 64], rhs=PTs[:, g, kb, :],
                            start=(kb == 0), stop=(kb == nkb - 1), tile_position=(0, 64 * hf)),
                            reads=spr(128 * vi, 128 * vi + 128) + B["rPT"], writes=[PR(pO)], sig=(g == 3 and kb == nkb - 1))
                S.op("act", lambda e, kvh=kvh, nb=nb, pO=pO: e.activation(
                    out=MO[:, 4 + 2 * kvh:4 + 2 * kvh + 2, 128 * nb:128 * nb + 128],
                    in_=PS[pO][:, 0:256].rearrange("p (c q) -> p c q", c=2), func=AF.Copy),
                    reads=[PR(pO)], writes=[MR(4 + 2 * kvh, t), MR(5 + 2 * kvh, t)])

            st_scores(0)
            st_scores(1)
            st_softmax(0, "A")
            for it in range(32):
                if it + 1 < 32:
                    st_softmax(it + 1, "A")
                if it + 2 < 32:
                    st_scores(it + 2)
                st_softmax(it, "B")
                st_pv(it)

        def attn_sample(l, QT, KD, VT, KVO):
            Sf = SPR[:, 3200:3200 + 2048].bitcast(F32).rearrange("p (g k) -> p g k", g=4)
            Pb = SPR[:, 5248:5248 + 1024].rearrange("p (g k) -> p g k", g=4)
            PTs = SPR[:, 6272:6272 + 512].rearrange("p (g q) -> p g q", g=4)
            PTn = SPR[:, 6784:6784 + 512].rearrange("p (g q) -> p g q", g=4)
            KTs = SPR[:, 8320:8320 + 4 * 136].rearrange("p (s k) -> p s k", s=4)
            KCb = SPR[:, 8896:8896 + 128]
            VNa = [SPR[:, 7296:8320], SPR[:, 9024:10048]]
            rVN = spr(7296, 8320) + spr(9024, 10048)
            for s in range(16):
                S.dma("sp", lambda e, s=s: e.dma_start(out=dr["vs"][l, s, 120:128, :], in_=KVO[8 * s:8 * s + 8, 1, 0:128]), "d_out", reads=spr(2688, 3200))
                S.dma("sp", lambda e, s=s: e.dma_start(out=dr["ks"][l, s, 120:128, :], in_=KVO[8 * s:8 * s + 8, 1, 128:256]), "d_out", reads=spr(2688, 3200))
            for s4 in range(4):
                klo = 2048 if s4 % 2 == 0 else 4096
                vlo = klo + 1024
                slo = 8192 if s4 % 2 == 0 else 6144
                KC4 = RG[:, klo:klo + 1024].bitcast(F32).rearrange("p (s f) -> p s f", s=4)
                VC4 = RG[:, vlo:vlo + 1024].bitcast(F32).rearrange("p (s f) -> p s f", s=4)
                VCs = RG[:, slo:slo + 512].rearrange("p (s f) -> p s f", s=4)
                rKC, rVC, rVS = rg(klo, klo + 1024), rg(vlo, vlo + 1024), rg(slo, slo + 512)
                S.dma("sp", lambda e, s4=s4, KC4=KC4: e.dma_start(out=KC4, in_=dr["ck"][l, 4 * s4:4 * s4 + 4].rearrange("s k f -> k s f")), "d_kc", writes=rKC)
                S.dma("sp", lambda e, s4=s4, VC4=VC4: e.dma_start(out=VC4, in_=dr["cv"][l, 4 * s4:4 * s4 + 4].rearrange("s k f -> k s f")), "d_vc", writes=rVC)
                S.op("dve", lambda e, VCs=VCs, VC4=VC4: e.tensor_copy(out=VCs, in_=VC4), reads=rVC, writes=rVS)
                S.dma("sp", lambda e, s4=s4, KC4=KC4: e.dma_start(out=dr["ks"][l, 4 * s4:4 * s4 + 4, 0:120, :].rearrange("s k f -> k s f"), in_=KC4[8:128, :, :]), "d_out", reads=rKC)
                S.dma("sp", lambda e, s4=s4, VC4=VC4: e.dma_start(out=dr["vs"][l, 4 * s4:4 * s4 + 4, 0:120, :].rearrange("s k f -> k s f"), in_=VC4[8:128, :, :]), "d_out", reads=rVC)
                for kvh in range(2):
                    po = 3 + kvh
                    for sl in range(4):
                        s = s4 * 4 + sl
                        S.op("dve", lambda e, sl=sl, kvh=kvh, KC4=KC4: e.tensor_copy(
                            out=KCb[:, :].rearrange("p (d f) -> p d f", d=2),
                            in_=KC4[:, sl, kvh * 64:kvh * 64 + 64].unsqueeze(1).to_broadcast([128, 2, 64])),
                            reads=rKC, writes=spr(8896, 9024))
                        ptk = PS[5][:, :].bitcast(BF16)
                        S.op("pe", lambda e, ptk=ptk: e.transpose(ptk[:, 0:128], KCb[:, 0:128], identB[:]),
                             reads=spr(8896, 9024) + [R("identB")], writes=[PR(5)])
                        S.op("act", lambda e, sl=sl, ptk=ptk: e.activation(out=KTs[:, sl, 0:128], in_=ptk[:, 0:128], func=AF.Copy),
                             reads=[PR(5)], writes=spr(8320, 8864))
                        S.op("act", lambda e, sl=sl, s=s, kvh=kvh: e.activation(out=KTs[:, sl, 128:136], in_=KD[:, kvh, 2048 + 8 * s:2048 + 8 * s + 8], func=AF.Copy),
                             reads=[R1r(4 + kvh, 4)], writes=spr(8320, 8864))
                    for sl in range(4):
                        s = s4 * 4 + sl
                        for g in range(4):
                            h = 4 * kvh + g
                            ch, hf = h // 2, h % 2
                            pp = hf
                            S.op("pe", lambda e, ch=ch, hf=hf, pp=pp, g=g, s=s, sl=sl: e.matmul(
                                PS[pp][32 * sl:32 * sl + 8, (g // 2) * 256:(g // 2) * 256 + 136],
                                lhsT=QT[64 * hf:64 * hf + 64, ch, 2048 + 8 * s:2048 + 8 * s + 8],
                                rhs=KTs[64 * hf:64 * hf + 64, sl, :], start=True, stop=True, tile_position=(64 * hf, 32 * sl)),
                                reads=[R1r(ch, 4)] + spr(8320, 8864), writes=[PR(pp)], sig=(sl == 3 and g >= 2))
                    softmax_block(l, kvh, 0, 1, 136, smask[:, :], Sf, spr(3200, 5248), Pb, spr(5248, 6272), 0)
                    ptp = PS[2][:, :].bitcast(BF16).rearrange("p (g q) -> p g q", g=8)
                    for g in range(4):
                        S.op("pe", lambda e, g=g, ptp=ptp: e.transpose(ptp[:, g, :], Pb[:, g, 0:128], identB[:]),
                             reads=spr(5248, 6272) + [R("identB")], writes=[PR(2)], sig=False)
                        S.op("pe", lambda e, g=g, ptp=ptp: e.transpose(ptp[0:8, 4 + g, :], Pb[:, g, 128:136], identB[:]),
                             reads=spr(5248, 6272) + [R("identB")], writes=[PR(2)], sig=(g == 3))
                    S.op("act", lambda e, ptp=ptp: e.activation(out=PTs, in_=ptp[:, 0:4, :], func=AF.Copy), reads=[PR(2)], writes=spr(6272, 7296))
                    S.op("act", lambda e, ptp=ptp: e.activation(out=PTn[0:8, :, :], in_=ptp[0:8, 4:8, :], func=AF.Copy), reads=[PR(2)], writes=spr(6272, 7296))
                    for sl in range(4):
                        s = s4 * 4 + sl
                        for g in range(4):
                            h = 4 * kvh + g
                            hf = h % 2
                            dst = PS[po][64 * hf:64 * hf + 64, (g // 2) * 128 + 8 * s:(g // 2) * 128 + 8 * s + 8]
                            S.op("pe", lambda e, dst=dst, s=s, sl=sl, g=g, kvh=kvh, hf=hf, VCs=VCs: e.matmul(
                                dst, lhsT=VCs[:, sl, kvh * 64:kvh * 64 + 64], rhs=PTs[:, g, 32 * sl:32 * sl + 8],
                                start=True, stop=False, tile_position=(0, 64 * hf)),
                                reads=rVS + spr(6272, 7296), writes=[PR(po)], sig=False)
                            S.op("pe", lambda e, dst=dst, s=s, sl=sl, g=g, kvh=kvh, hf=hf: e.matmul(
                                dst, lhsT=VNa[s // 8][0:8, (s % 8) * 128 + kvh * 64:(s % 8) * 128 + kvh * 64 + 64], rhs=PTn[0:8, g, 32 * sl:32 * sl + 8],
                                start=False, stop=True, tile_position=(0, 64 * hf)),
                                reads=rVN + spr(6272, 7296), writes=[PR(po)], sig=(sl == 3 and g == 3))
            for kvh in range(2):
                S.op("act", lambda e, kvh=kvh: e.activation(
                    out=MO[:, 4 + 2 * kvh:4 + 2 * kvh + 2, 2048:2176],
                    in_=PS[3 + kvh][:, 0:256].rearrange("p (c q) -> p c q", c=2), func=AF.Copy),
                    reads=[PR(3 + kvh)], writes=[MR(4 + 2 * kvh, 4), MR(5 + 2 * kvh, 4)])

        NSV = {}

        def sv(name):
            if name not in NSV:
                NSV[name] = len(NSV)
                assert len(NSV) <= 48, "SV overflow"
            i = NSV[name]
            return SV[:, i, :], R("sv", i)

        def tt(eng, out, a, b, op, reads, writes):
            S.op(eng, lambda e: e.tensor_tensor(out=out, in0=a, in1=b, op=op), reads=reads, writes=writes)

        def svop(out, a, b, op):
            (oa, orr), (aa, ar_), (ba, br_) = sv(out), sv(a), sv(b)
            tt("dve", oa, aa, ba, op, [ar_, br_], [orr])

        def svts(out, a, s1, op0, s2=None, op1=None):
            (oa, orr), (aa, ar_) = sv(out), sv(a)
            if op1 is None:
                S.op("dve", lambda e: e.tensor_scalar(out=oa, in0=aa, scalar1=s1, scalar2=None, op0=op0), reads=[ar_], writes=[orr])
            else:
                S.op("dve", lambda e: e.tensor_scalar(out=oa, in0=aa, scalar1=s1, scalar2=s2, op0=op0, op1=op1), reads=[ar_], writes=[orr])

        def svact(out, a, func, scale=1.0):
            (oa, orr), (aa, ar_) = sv(out), sv(a)
            S.op("act", lambda e: e.activation(out=oa, in_=aa, func=func, scale=scale), reads=[ar_], writes=[orr])

        def svcmul(outr, outi, ar_, ai_, br_, bi_):
            svop("t0", ar_, br_, ALU.mult); svop("t1", ai_, bi_, ALU.mult); svop(outr, "t0", "t1", ALU.subtract)
            svop("t0", ar_, bi_, ALU.mult); svop("t1", ai_, br_, ALU.mult); svop(outi, "t0", "t1", ALU.add)

        def load_ssm_vecs():
            for l in range(2):
                for k, src in enumerate(("ssm_lam_re", "ssm_lam_im")):
                    S.dma("sp", lambda e, l=l, k=k, src=src: e.dma_start(out=SVL[:, l * 4 + k, :], in_=dr[src][l].rearrange("(j g) n -> (g n) j", g=2),
                                                                      allow_slow_non_contiguous=True), None, writes=[R("SVL", l * 4 + k)])
                for g2 in range(2):
                    S.dma("sp", lambda e, l=l, g2=g2: e.dma_start(
                        out=SVL[64 * g2:64 * g2 + 64, l * 4 + 2, :], in_=dr["ssm_log_dt"][l].rearrange("(j g) -> g j", g=2)[g2].partition_broadcast(64),
                        allow_slow_non_contiguous=True), None, writes=[R("SVL", l * 4 + 2)])
                for sg in range(4):
                    S.dma("sp", lambda e, l=l, sg=sg: e.dma_start(out=SVL[32 * sg:32 * sg + 32, l * 4 + 3, :], in_=dr["ssm_d"][l].rearrange("(j r) -> r j", r=32),
                                                                  allow_slow_non_contiguous=True), None, writes=[R("SVL", l * 4 + 3)])

        def ssm_prefetch(l):
            Bre = RG[:, 0:512].bitcast(F32).rearrange("p (j q) -> p j q", j=16)
            Bim = RG[:, 512:1024].bitcast(F32).rearrange("p (j q) -> p j q", j=16)
            S.dma("sp", lambda e: e.dma_start(out=Bre, in_=dr["ssm_b_re"][l].rearrange("(j g) n q -> (g n) j q", g=2)), None, writes=rg(0, 512))
            S.dma("sp", lambda e: e.dma_start(out=Bim, in_=dr["ssm_b_im"][l].rearrange("(j g) n q -> (g n) j q", g=2)), None, writes=rg(512, 1024))
            for ci, src in enumerate(("ssm_c_re", "ssm_c_im")):
                for hh in range(2):
                    kk = ci * 2 + hh
                    cin = RG[:, 1024 + 256 * kk:1024 + 256 * kk + 256].bitcast(F32)
                    for j8 in range(8):
                        j = hh * 8 + j8
                        S.dma("sp", lambda e, src=src, j=j, j8=j8, cin=cin: e.dma_start(
                            out=cin[16 * j8:16 * j8 + 16, :].rearrange("p (g n) -> p g n", g=2),
                            in_=dr[src][l, 2 * j:2 * j + 2].rearrange("g p n -> p g n")), None, writes=rg(1024 + 256 * kk, 1280 + 256 * kk))

        def ssm_params(l):
            for k, nm in enumerate(("lamr", "lami", "ldt", "dcol")):
                ap, rr = sv(nm)
                S.op("dve", lambda e, ap=ap, k=k: e.tensor_copy(out=ap, in_=SVL[:, l * 4 + k, :]), reads=[R("SVL", l * 4 + k)], writes=[rr])
            Xre = R1[:, 0:2048]; Xim = R1[:, 2048:4096]; Yre = R1[:, 4096:6144]; Yim = R1[:, 6144:8192]
            Bre = R1[:, 8192:8704].bitcast(F32).rearrange("p (j q) -> p j q", j=16)
            Bim = R1[:, 8704:9216].bitcast(F32).rearrange("p (j q) -> p j q", j=16)
            Cre = R1[:, 9216:9728].bitcast(F32).rearrange("p (j q) -> p j q", j=16)
            Cim = R1[:, 9728:10240].bitcast(F32).rearrange("p (j q) -> p j q", j=16)
            Bbr = R1[:, 10240:10752].bitcast(F32).rearrange("p (j q) -> p j q", j=16)
            Bbi = R1[:, 10752:11264].bitcast(F32).rearrange("p (j q) -> p j q", j=16)
            TA = R1[:, 11264:11776].bitcast(F32).rearrange("p (j q) -> p j q", j=16)
            TB = R1[:, 11776:12288].bitcast(F32).rearrange("p (j q) -> p j q", j=16)
            CIN = R1[:, 12288:12544].bitcast(F32)
            rXre, rXim, rYre, rYim = r1(0, 2048), r1(2048, 4096), r1(4096, 6144), r1(6144, 8192)
            rB, rC, rBb, rTA, rTB, rCIN = r1(8192, 9216), r1(9216, 10240), r1(10240, 11264), r1(11264, 11776), r1(11776, 12288), r1(12288, 12544)
            STi = RG[:, 2048:6144].bitcast(F32)
            rSTi = rg(2048, 6144)
            Bre = RG[:, 0:512].bitcast(F32).rearrange("p (j q) -> p j q", j=16)
            Bim = RG[:, 512:1024].bitcast(F32).rearrange("p (j q) -> p j q", j=16)
            rB = rg(0, 1024)
            for ci, dst in enumerate((Cre, Cim)):
                for hh in range(2):
                    kk = ci * 2 + hh
                    cin = RG[:, 1024 + 256 * kk:1024 + 256 * kk + 256].bitcast(F32)
                    S.op("pe", lambda e, cin=cin: e.transpose(PS[7][:, 0:128], cin, identF[:]), reads=rg(1024 + 256 * kk, 1280 + 256 * kk) + [R("identF")], writes=[PR(7)])
                    S.op("dve", lambda e, dst=dst, hh=hh: e.tensor_copy(out=dst[:, 8 * hh:8 * hh + 8, :], in_=PS[7][:, 0:128].rearrange("p (j q) -> p j q", j=8)),
                         reads=[PR(7)], writes=rC)
            for ri, src in enumerate(("st_re", "st_im")):
                S.dma("sp", lambda e, src=src: e.dma_start(out=STi[0:16, :], in_=dr[src][l]), "d_sti", writes=rSTi)
                for j in range(16):
                    S.op("pe", lambda e, j=j: e.transpose(PS[6][:, 16 * j:16 * j + 16], STi[0:16, 128 * j:128 * j + 128], identF[0:16, 0:16]),
                         reads=rSTi + [R("identF")], writes=[PR(6)], sig=(j == 15))
                S.op("dve", lambda e, ri=ri: e.tensor_copy(out=H0[:, ri, :, :], in_=PS[6][:, 0:256].rearrange("p (j s) -> p j s", j=16)),
                     reads=[PR(6)], writes=[R("H0")])
            svact("dt", "ldt", AF.Exp)
            svop("x", "lamr", "dt", ALU.mult)
            svop("phi", "lami", "dt", ALU.mult)
            svact("mag", "x", AF.Exp)
            for _ in range(5):
                svts("t0", "phi", math.pi, ALU.is_gt, 2 * math.pi, ALU.mult)
                svop("phi", "phi", "t0", ALU.subtract)
            svts("phc", "phi", math.pi / 2, ALU.add)
            svts("t0", "phc", math.pi, ALU.is_gt, 2 * math.pi, ALU.mult)
            svop("phc", "phc", "t0", ALU.subtract)
            svact("s1", "phi", AF.Sin)
            svact("c1", "phc", AF.Sin)
            svop("a1r", "mag", "c1", ALU.mult)
            svop("a1i", "mag", "s1", ALU.mult)
            svcmul("a2r", "a2i", "a1r", "a1i", "a1r", "a1i")
            svcmul("a3r", "a3i", "a2r", "a2i", "a1r", "a1i")
            svcmul("a4r", "a4i", "a2r", "a2i", "a2r", "a2i")
            svts("na4i", "a4i", -1.0, ALU.mult)
            for k in (1, 2, 3):
                svact("t2", "x", AF.Exp, scale=-2.0 * k)
                svop("i%dr" % k, "a%dr" % k, "t2", ALU.mult)
                svop("t3", "a%di" % k, "t2", ALU.mult)
                svts("i%di" % k, "t3", -1.0, ALU.mult)
            S.op("dve", lambda e: e.memset(sv("one")[0], 1.0), writes=[sv("one")[1]])
            S.op("dve", lambda e: e.memset(sv("zero")[0], 0.0), writes=[sv("zero")[1]])
            svts("nr", "a1r", -1.0, ALU.add)
            svop("t0", "lamr", "lamr", ALU.mult); svop("t1", "lami", "lami", ALU.mult); svop("den", "t0", "t1", ALU.add)
            S.op("dve", lambda e: e.reciprocal(out=sv("den")[0], in_=sv("den")[0]), reads=[sv("den")[1]], writes=[sv("den")[1]])
            svop("t0", "nr", "lamr", ALU.mult); svop("t1", "a1i", "lami", ALU.mult); svop("t2", "t0", "t1", ALU.add); svop("cfr", "t2", "den", ALU.mult)
            svop("t0", "a1i", "lamr", ALU.mult); svop("t1", "nr", "lami", ALU.mult); svop("t2", "t0", "t1", ALU.subtract); svop("cfi", "t2", "den", ALU.mult)
            svact("r4", "x", AF.Exp, scale=4.0)
            svact("t2", "x", AF.Exp, scale=-4.0)
            svop("wr", "a4r", "t2", ALU.mult); svop("wi", "a4i", "t2", ALU.mult)

            def table(Et, n, wr, wi):
                rE = R("E", id(Et))
                S.op("dve", lambda e: e.memset(Et[:, 0, :, 0:1], 1.0), writes=[rE])
                S.op("dve", lambda e: e.memset(Et[:, 1, :, 0:1], 0.0), writes=[rE])
                m = 1
                cr, ci = wr, wi
                while m < n:
                    (cra, crr), (cia, cir) = sv(cr), sv(ci)
                    bcr = cra.unsqueeze(2).to_broadcast([128, 16, m]); bci = cia.unsqueeze(2).to_broadcast([128, 16, m])
                    tA, tBv = TA[:, :, 0:m], TB[:, :, 0:m]
                    src_c, src_s = Et[:, 0, :, 0:m], Et[:, 1, :, 0:m]
                    tt("dve", tA, src_c, bcr, ALU.mult, [rE, crr], rTA); tt("dve", tBv, src_s, bci, ALU.mult, [rE, cir], rTB)
                    tt("dve", Et[:, 0, :, m:2 * m], tA, tBv, ALU.subtract, rTA + rTB, [rE])
                    tt("dve", tA, src_c, bci, ALU.mult, [rE, cir], rTA); tt("dve", tBv, src_s, bcr, ALU.mult, [rE, crr], rTB)
                    tt("dve", Et[:, 1, :, m:2 * m], tA, tBv, ALU.add, rTA + rTB, [rE])
                    nr_, ni_ = ("pAr", "pAi") if cr != "pAr" else ("pBr", "pBi")
                    svcmul(nr_, ni_, cr, ci, cr, ci)
                    cr, ci = nr_, ni_
                    m *= 2
                return cr, ci
            w32r, w32i = table(E0, 32, "wr", "wi")
            table(E1, 16, w32r, w32i)

            def bc(name, m=16):
                a, r = sv(name)
                return a.unsqueeze(2).to_broadcast([128, 16, m]), r
            (cfr, rcfr), (cfi, rcfi) = bc("cfr"), bc("cfi")
            tt("dve", TA, Bre, cfr, ALU.mult, rB + [rcfr], rTA); tt("dve", TB, Bim, cfi, ALU.mult, rB + [rcfi], rTB)
            tt("dve", Bbr, TA, TB, ALU.subtract, rTA + rTB, rBb)
            tt("dve", TA, Bim, cfr, ALU.mult, rB + [rcfr], rTA); tt("dve", TB, Bre, cfi, ALU.mult, rB + [rcfi], rTB)
            tt("dve", Bbi, TA, TB, ALU.add, rTA + rTB, rBb)
            MRv = SPR[:, 6144:8192]; MIv = SPR[:, 8192:10240]
            for buf, rr in ((Xre, rXre), (Xim, rXim), (Yre, rYre), (Yim, rYim), (MRv, spr(6144, 8192)), (MIv, spr(8192, 10240))):
                S.op("dve", lambda e, buf=buf: e.memset(buf, 0.0), writes=rr)

            def place(dst, rdst, sr, si, rsrc, pwr, pwi, neg_im):
                dre, dim_ = dst
                rdre, rdim = rdst
                for sg in range(4):
                    (pr, rpr), (pi, rpi) = bc(pwr[sg]), bc(pwi[sg])
                    dr5 = dre.rearrange("p (j s g q) -> p j s g q", j=16, s=4, g=2)
                    di5 = dim_.rearrange("p (j s g q) -> p j s g q", j=16, s=4, g=2)
                    tt("dve", TA, sr, pr, ALU.mult, rsrc + [rpr], rTA); tt("dve", TB, si, pi, ALU.mult, rsrc + [rpi], rTB)
                    for g2 in range(2):
                        ps_ = slice(64 * g2, 64 * g2 + 64)
                        tt("dve", dr5[ps_, :, sg, g2, :], TA[ps_], TB[ps_], ALU.subtract, rTA + rTB, rdre)
                    tt("dve", TA, sr, pi, ALU.mult, rsrc + [rpi], rTA); tt("dve", TB, si, pr, ALU.mult, rsrc + [rpr], rTB)
                    for g2 in range(2):
                        ps_ = slice(64 * g2, 64 * g2 + 64)
                        if neg_im:
                            S.op("dve", lambda e, ps_=ps_, sg=sg, g2=g2, di5=di5: e.scalar_tensor_tensor(
                                out=di5[ps_, :, sg, g2, :], in0=TA[ps_], scalar=-1.0, in1=TB[ps_], op0=ALU.mult, op1=ALU.subtract),
                                reads=rTA + rTB, writes=rdim)
                        else:
                            tt("dve", di5[ps_, :, sg, g2, :], TA[ps_], TB[ps_], ALU.add, rTA + rTB, rdim)
            place((Xre, Xim), (rXre, rXim), Bbr, Bbi, rBb, ["a3r", "a2r", "a1r", "one"], ["a3i", "a2i", "a1i", "zero"], False)
            place((Yre, Yim), (rYre, rYim), Cre, Cim, rC, ["i3r", "i2r", "i1r", "one"], ["i3i", "i2i", "i1i", "zero"], True)
            place((MRv, MIv), (spr(6144, 8192), spr(8192, 10240)), Cre, Cim, rC, ["a1r", "a2r", "a3r", "a4r"], ["a1i", "a2i", "a3i", "a4i"], True)
            Tv = SPR[:, 0:2048].rearrange("p (j c) -> p j c", j=16)
            WRv = SPR[:, 2048:4096].rearrange("p (j c) -> p j c", j=16)
            WIv = SPR[:, 4096:6144].rearrange("p (j c) -> p j c", j=16)
            X3r = Xre.rearrange("p (j c) -> p j c", j=16); X3i = Xim.rearrange("p (j c) -> p j c", j=16)
            Y3r = Yre.rearrange("p (j c) -> p j c", j=16); Y3i = Yim.rearrange("p (j c) -> p j c", j=16)
            TMs = [(R1[:, 12544:12800].bitcast(F32), r1(12544, 12800)), (R1[:, 12800:13056].bitcast(F32), r1(12800, 13056))]
            dcol, rdcol = sv("dcol")
            for j in range(16):
                TM, rTM = TMs[j % 2]
                pT, pW = (7, 6) if j % 2 == 0 else (5, 4)
                S.op("pe", lambda e, j=j, pT=pT: e.matmul(PS[pT][:, 0:128], lhsT=X3r[:, j, :], rhs=Y3r[:, j, :], start=True, stop=False),
                     reads=rXre + rYre, writes=[PR(pT)], sig=False)
                S.op("pe", lambda e, j=j, pT=pT: e.matmul(PS[pT][:, 0:128], lhsT=X3i[:, j, :], rhs=Y3i[:, j, :], start=False, stop=True),
                     reads=rXim + rYim, writes=[PR(pT)])
                tt("dve", TM, PS[pT][:, 0:128], maskT[:], ALU.mult, [PR(pT), R("maskT")], rTM)
                S.op("dve", lambda e, j=j, TM=TM: e.scalar_tensor_tensor(out=Tv[:, j, :], in0=identF[:], scalar=dcol[:, j:j + 1], in1=TM,
                                                                         op0=ALU.mult, op1=ALU.add), reads=rTM + [R("identF"), rdcol], writes=spr(128 * j, 128 * j + 128))
                ptb = PS[pW][:, :].bitcast(BF16)
                S.op("pe", lambda e, j=j, ptb=ptb: e.transpose(ptb[:, 0:128], X3r[:, j, :], identB[:]), reads=rXre + [R("identB")], writes=[PR(pW)], sig=False)
                S.op("pe", lambda e, j=j, ptb=ptb: e.transpose(ptb[:, 128:256], X3i[:, j, :], identB[:]), reads=rXim + [R("identB")], writes=[PR(pW)])
                S.op("act", lambda e, j=j, ptb=ptb: e.activation(out=WRv[:, j, :], in_=ptb[:, 0:128], func=AF.Copy), reads=[PR(pW)], writes=spr(2048 + 128 * j, 2048 + 128 * j + 128))
                S.op("act", lambda e, j=j, ptb=ptb: e.activation(out=WIv[:, j, :], in_=ptb[:, 128:256], func=AF.Copy), reads=[PR(pW)], writes=spr(4096 + 128 * j, 4096 + 128 * j + 128))

        def ssm(l):
            ssm_params(l)
            Tv = SPR[:, 0:2048].rearrange("p (j c) -> p j c", j=16)
            WRv = SPR[:, 2048:4096].rearrange("p (j c) -> p j c", j=16)
            WIv = SPR[:, 4096:6144].rearrange("p (j c) -> p j c", j=16)
            MRv = SPR[:, 6144:8192].rearrange("p (j c) -> p j c", j=16)
            MIv = SPR[:, 8192:10240].rearrange("p (j c) -> p j c", j=16)
            rT, rWR, rWI, rMR, rMI = spr(0, 2048), spr(2048, 4096), spr(4096, 6144), spr(6144, 8192), spr(8192, 10240)
            NC_ = 544
            Ut = R1[:, 0:544]; rUt = r1(0, 544)
            def f32v(a, n):
                return R1[:, a:a + 2 * n].bitcast(F32), r1(a, a + 2 * n)
            ZtR, rZtR = f32v(544, 512); ZtI, rZtI = f32v(1568, 512)
            GR, rGR = f32v(2592, 512); GI, rGI = f32v(3616, 512)
            ECS = [f32v(4640, 512) + f32v(5664, 512),
                   (RG[:, 6528:7552].bitcast(F32), rg(6528, 7552), RG[:, 7552:8576].bitcast(F32), rg(7552, 8576))]
            rPT = [R("PTMP")]
            TA, rTA = f32v(6688, 512); TB, rTB = f32v(7712, 512)
            HsR = R1[:, 8736:8736 + 544]; rHsR = r1(8736, 9280)
            HsI = R1[:, 9280:9280 + 544]; rHsI = r1(9280, 9824)
            Y4 = R1[:, 9824:9824 + 2176].rearrange("p (j c) -> p j c", j=4); rY4 = r1(9824, 12000)
            SA, rSA = f32v(12000, 64)
            YG = RG[:, :].rearrange("p (c n) -> p c n", c=4)
            win = wview(l, "w_in")
            uslot, usr = wload([(lambda s: s.rearrange("p (c f) -> p c f", c=8), win[:, :, 0:512])])
            uw = uslot.rearrange("p (c f) -> p c f", c=8)
            r4, rr4 = sv("r4")
            a4r, ra4r = sv("a4r"); a4i, ra4i = sv("a4i"); na4i, rna4i = sv("na4i")
            hp_r, rhp_r = sv("hp_r"); hp_i, rhp_i = sv("hp_i")
            E0c, E0s, E1c, E1s = E0[:, 0], E0[:, 1], E1[:, 0], E1[:, 1]
            rE0, rE1 = R("E", id(E0)), R("E", id(E1))
            HTp = HT[:, :, 0:2048].rearrange("p c (n t) -> p c t n", t=4)
            HTs = HT[:, :, 2048:2176].rearrange("p c (n t) -> p c t n", t=4)
            Uts = [(R1[:, 0:544], r1(0, 544)), (R1[:, 12192:12736], r1(12192, 12736))]

            def stage_U(j):
                Ut, rUt = Uts[j % 2]
                for c in range(8):
                    for tau in range(4):
                        S.op("pe", lambda e, tau=tau, c=c, j=j: e.matmul(
                            PS[0][32 * tau:32 * tau + 32, 0:512], lhsT=uw[:, c, 32 * j:32 * j + 32], rhs=HTp[:, c, tau, :],
                            start=(c == 0), stop=(c == 7), tile_position=(0, 32 * tau)),
                            reads=[usr] + [HR(c, t) for t in range(4)], writes=[PR(0)], sig=(c == 7 and tau == 3))
                for c in range(8):
                    for tau in range(4):
                        S.op("pe", lambda e, tau=tau, c=c, j=j: e.matmul(
                            PS[7][32 * tau:32 * tau + 32, 0:32], lhsT=uw[:, c, 32 * j:32 * j + 32], rhs=HTs[:, c, tau, :],
                            start=(c == 0), stop=(c == 7), tile_position=(0, 32 * tau)),
                            reads=[usr, HR(c, 4)], writes=[PR(7)], sig=(c == 7 and tau == 3))
                S.op("act", lambda e, Ut=Ut: e.activation(out=Ut[:, 0:512], in_=PS[0][:, :], func=AF.Copy), reads=[PR(0)], writes=rUt)
                S.op("act", lambda e, Ut=Ut: e.activation(out=Ut[:, 512:544], in_=PS[7][:, 0:32], func=AF.Copy), reads=[PR(7)], writes=rUt)

            def stage_Z(j):
                Ut, rUt = Uts[j % 2]
                for (W_, rW, pb, so) in ((WRv, rWR, 1, 32), (WIv, rWI, 2, 64)):
                    S.op("pe", lambda e, W_=W_, pb=pb, j=j, Ut=Ut: e.matmul(PS[pb][:, 0:512], lhsT=W_[:, j, :], rhs=Ut[:, 0:512], start=True, stop=True),
                         reads=rW + rUt, writes=[PR(pb)])
                    S.op("pe", lambda e, W_=W_, so=so, j=j, Ut=Ut: e.matmul(PS[4][:, so:so + 32], lhsT=W_[:, j, :], rhs=Ut[:, 512:544], start=True, stop=True),
                         reads=rW + rUt, writes=[PR(4)])

            def stage_B(j):
                zs_r = PS[4][:, 32:64].rearrange("p (s h) -> p s h", h=2); zs_i = PS[4][:, 64:96].rearrange("p (s h) -> p s h", h=2)
                hsr3 = HsR[:, 512:544].rearrange("p (s h) -> p s h", h=2); hsi3 = HsI[:, 512:544].rearrange("p (s h) -> p s h", h=2)
                h0r, h0i = H0[:, 0, j, :], H0[:, 1, j, :]
                sa = [SA[:, 16 * k:16 * k + 16] for k in range(4)]
                car, cai, cnai = a4r[:, j:j + 1], a4i[:, j:j + 1], na4i[:, j:j + 1]
                rH0 = R("H0")

                def cstep(inr, ini, zr, zi, outr, outi, rin, rout, car=car, cai=cai, cnai=cnai):
                    S.op("dve", lambda e: e.tensor_scalar(out=sa[0], in0=inr, scalar1=car, scalar2=None, op0=ALU.mult), reads=rin + [ra4r], writes=rSA)
                    S.op("dve", lambda e: e.scalar_tensor_tensor(out=sa[1], in0=ini, scalar=cnai, in1=sa[0], op0=ALU.mult, op1=ALU.add), reads=rin + rSA + [rna4i], writes=rSA)
                    S.op("dve", lambda e: e.tensor_scalar(out=sa[2], in0=ini, scalar1=car, scalar2=None, op0=ALU.mult), reads=rin + [ra4r], writes=rSA)
                    S.op("dve", lambda e: e.scalar_tensor_tensor(out=sa[3], in0=inr, scalar=cai, in1=sa[2], op0=ALU.mult, op1=ALU.add), reads=rin + rSA + [ra4i], writes=rSA)
                    tt("dve", outr, sa[1], zr, ALU.add, rSA + [PR(4)], rout)
                    tt("dve", outi, sa[3], zi, ALU.add, rSA + [PR(4)], rout)
                S.op("act", lambda e, h0r=h0r: e.activation(out=hsr3[:, :, 0], in_=h0r, func=AF.Copy), reads=[rH0], writes=rHsR)
                S.op("act", lambda e, h0i=h0i: e.activation(out=hsi3[:, :, 0], in_=h0i, func=AF.Copy), reads=[rH0], writes=rHsI)
                HA, rHA = f32v(12128, 32)
                har, hai = HA[:, 0:16], HA[:, 16:32]
                cstep(h0r, h0i, zs_r[:, :, 0], zs_i[:, :, 0], har, hai, [rH0], rHA)
                S.op("act", lambda e: e.activation(out=hsr3[:, :, 1], in_=har, func=AF.Copy), reads=rHA, writes=rHsR)
                S.op("act", lambda e: e.activation(out=hsi3[:, :, 1], in_=hai, func=AF.Copy), reads=rHA, writes=rHsI)
                if not cfg.get('dbg_h0', 0):
                    cstep(har, hai, zs_r[:, :, 1], zs_i[:, :, 1], h0r, h0i, rHA, [rH0])
                EC, rEC, ES, rES = ECS[j % 2]
                tt("dve", TA, PS[1][:, 0:512], EC, ALU.mult, [PR(1)] + rEC, rTA); tt("dve", TB, PS[2][:, 0:512], ES, ALU.mult, [PR(2)] + rES, rTB)
                tt("dve", ZtR, TA, TB, ALU.add, rTA + rTB, rZtR)
                tt("dve", TA, PS[2][:, 0:512], EC, ALU.mult, [PR(2)] + rEC, rTA); tt("dve", TB, PS[1][:, 0:512], ES, ALU.mult, [PR(1)] + rES, rTB)
                tt("dve", ZtI, TA, TB, ALU.subtract, rTA + rTB, rZtI)
                r4b = r4[:, j:j + 1].to_broadcast([128, 512])
                S.op("dve", lambda e, r4b=r4b: e.tensor_tensor_scan(out=GR, data0=r4b, data1=ZtR, initial=0.0, op0=ALU.mult, op1=ALU.add),
                     reads=rZtR + [rr4], writes=rGR)
                S.op("dve", lambda e, r4b=r4b: e.tensor_tensor_scan(out=GI, data0=r4b, data1=ZtI, initial=0.0, op0=ALU.mult, op1=ALU.add),
                     reads=rZtI + [rr4], writes=rGI)
                tt("dve", TA, GR, EC, ALU.mult, rGR + rEC, rTA); tt("dve", TB, GI, ES, ALU.mult, rGI + rES, rTB)
                tt("dve", ZtR, TA, TB, ALU.subtract, rTA + rTB, rZtR)
                tt("pool", ZtI, GI, EC, ALU.mult, rGI + rEC, rZtI); tt("pool", PTMP[:, :], GR, ES, ALU.mult, rGR + rES, rPT)
                tt("pool", ZtI, ZtI, PTMP[:, :], ALU.add, rZtI + rPT, rZtI)
                S.op("act", lambda e: e.activation(out=HsR[:, 1:512], in_=ZtR[:, 0:511], func=AF.Copy), reads=rZtR, writes=rHsR)
                S.op("act", lambda e: e.activation(out=HsI[:, 1:512], in_=ZtI[:, 0:511], func=AF.Copy), reads=rZtI, writes=rHsI)
                S.op("dve", lambda e: e.memset(HsR[:, 0:1], 0.0), writes=rHsR)
                S.op("dve", lambda e: e.memset(HsI[:, 0:1], 0.0), writes=rHsI)
                S.op("act", lambda e, j=j: e.activation(out=hp_r[:, j:j + 1], in_=ZtR[:, 511:512], func=AF.Copy), reads=rZtR, writes=[rhp_r])
                S.op("act", lambda e, j=j: e.activation(out=hp_i[:, j:j + 1], in_=ZtI[:, 511:512], func=AF.Copy), reads=rZtI, writes=[rhp_i])

            def stage_E(j):
                EC, rEC, ES, rES = ECS[j % 2]
                e0c = E0c[:, j, :].unsqueeze(1).to_broadcast([128, 16, 32]); e0s = E0s[:, j, :].unsqueeze(1).to_broadcast([128, 16, 32])
                e1c = E1c[:, j, :].unsqueeze(2).to_broadcast([128, 16, 32]); e1s = E1s[:, j, :].unsqueeze(2).to_broadcast([128, 16, 32])
                v3 = lambda a: a.rearrange("p (a b) -> p a b", a=16)
                tt("pool", v3(EC), e0c, e1c, ALU.mult, [rE0, rE1], rEC); tt("pool", v3(PTMP[:, :]), e0s, e1s, ALU.mult, [rE0, rE1], rPT)
                tt("pool", EC, EC, PTMP[:, :], ALU.subtract, rEC + rPT, rEC)
                tt("pool", v3(ES), e0s, e1c, ALU.mult, [rE0, rE1], rES); tt("pool", v3(PTMP[:, :]), e0c, e1s, ALU.mult, [rE0, rE1], rPT)
                tt("pool", ES, ES, PTMP[:, :], ALU.add, rES + rPT, rES)

            def stage_C(j):
                jj = j % 4
                Ut, rUt = Uts[j % 2]
                for (lo_, n_, ob) in ((0, 512, PS[3][:, 0:512]), (512, 32, PS[7][:, 64:96])):
                    pr_ = PR(3) if lo_ == 0 else PR(7)
                    S.op("pe", lambda e, lo_=lo_, n_=n_, ob=ob, j=j, Ut=Ut: e.matmul(ob, lhsT=Tv[:, j, :], rhs=Ut[:, lo_:lo_ + n_], start=True, stop=False),
                         reads=rT + rUt, writes=[pr_], sig=False)
                    S.op("pe", lambda e, lo_=lo_, n_=n_, ob=ob, j=j: e.matmul(ob, lhsT=MRv[:, j, :], rhs=HsR[:, lo_:lo_ + n_], start=False, stop=False),
                         reads=rMR + rHsR, writes=[pr_], sig=False)
                    S.op("pe", lambda e, lo_=lo_, n_=n_, ob=ob, j=j: e.matmul(ob, lhsT=MIv[:, j, :], rhs=HsI[:, lo_:lo_ + n_], start=False, stop=True),
                         reads=rMI + rHsI, writes=[pr_])
                S.op("act", lambda e, jj=jj: e.activation(out=Y4[:, jj, 0:512], in_=PS[3][:, 0:512], func=AF.Copy), reads=[PR(3)], writes=rY4)
                S.op("act", lambda e, jj=jj: e.activation(out=Y4[:, jj, 512:544], in_=PS[7][:, 64:96], func=AF.Copy), reads=[PR(7)], writes=rY4)
                if jj == 3:
                    oc = j // 4
                    for (lo_, n_, tok0) in ((0, 512, 0), (512, 32, 2048)):
                        for tau in range(4):
                            pb = 5 + tau % 2
                            for q in range(4):
                                S.op("pe", lambda e, tau=tau, q=q, pb=pb, lo_=lo_, n_=n_: e.matmul(
                                    PS[pb][:, 0:n_], lhsT=SelC[32 * tau:32 * tau + 32, q, :], rhs=Y4[32 * tau:32 * tau + 32, q, lo_:lo_ + n_],
                                    start=(q == 0), stop=(q == 3), tile_position=(32 * tau, 0)),
                                    reads=rY4 + [R("SelC")], writes=[PR(pb)], sig=(q == 3))
                            yv, tv = TA[:, 0:n_], TB[:, 0:n_]
                            S.op("act", lambda e, pb=pb, n_=n_, yv=yv: e.activation(out=yv, in_=PS[pb][:, 0:n_], func=AF.Copy), reads=[PR(pb)], writes=rTA)
                            S.op("act", lambda e, pb=pb, n_=n_, tv=tv: e.activation(out=tv, in_=PS[pb][:, 0:n_], func=AF.Square), reads=[PR(pb)], writes=rTB)
                            S.op("act", lambda e, tv=tv: e.activation(out=tv, in_=tv, func=AF.Copy, scale=0.044715, bias=1.0), reads=rTB, writes=rTB)
                            tt("dve", tv, tv, yv, ALU.mult, rTA + rTB, rTB)
                            S.op("act", lambda e, tv=tv: e.activation(out=tv, in_=tv, func=AF.Sigmoid, scale=1.5957691216057308), reads=rTB, writes=rTB)
                            nn = n_ * 4
                            dst = YG[:, oc, tok0:tok0 + nn].rearrange("p (n t) -> p t n", t=4)[:, tau, :]
                            a_ = oc * NTOK + tok0
                            tt("dve", dst, yv, tv, ALU.mult, rTA + rTB, rg(a_, a_ + nn))

            stage_U(0)
            stage_Z(0)
            stage_E(0)
            for j in range(16):
                if j + 1 < 16:
                    stage_U(j + 1)
                    stage_E(j + 1)
                stage_B(j)
                if j + 1 < 16:
                    stage_Z(j + 1)
                stage_C(j)
            STo = R1[:, 0:4096].bitcast(F32)
            rSTo = r1(0, 4096)
            for ri, nm in enumerate(("hs_re", "hs_im")):
                for j4 in range(4):
                    for q in range(4):
                        j = j4 * 4 + q
                        S.op("pe", lambda e, ri=ri, j=j, q=q: e.transpose(PS[6][0:16, 128 * q:128 * q + 128], H0[:, ri, j, :], identF[:]),
                             reads=[R("H0"), R("identF")], writes=[PR(6)], sig=(q == 3))
                    S.op("dve", lambda e, j4=j4: e.tensor_copy(out=STo[0:16, 512 * j4:512 * j4 + 512], in_=PS[6][0:16, :]), reads=[PR(6)], writes=rSTo)
                S.dma("sp", lambda e, nm=nm: e.dma_start(out=dr[nm][l], in_=STo[0:16, :]), "d_out", reads=rSTo)
            for (nm, (hp, rhp)) in (("hp_re", (hp_r, rhp_r)), ("hp_im", (hp_i, rhp_i))):
                S.op("pe", lambda e, hp=hp: e.transpose(PS[6][0:16, 0:128], hp, identF[:]), reads=[rhp, R("identF")], writes=[PR(6)])
                S.op("dve", lambda e: e.tensor_copy(out=STo[0:16, 0:128], in_=PS[6][0:16, 0:128]), reads=[PR(6)], writes=rSTo)
                S.dma("sp", lambda e, nm=nm: e.dma_start(out=dr[nm][l].rearrange("(j c) -> j c", j=16), in_=STo[0:16, 0:128]), "d_out", reads=rSTo)

        def glu(l):
            YG = RG[:, :].rearrange("p (c n) -> p c n", c=4)
            SGt = SPR[:, 3200:4224].bitcast(F32)
            rSGt = spr(3200, 4224)
            gslot, gsr = wload([(lambda s: s[:, 0:2048].rearrange("p (c f) -> p c f", c=4),
                                 dr["ssm_w_glu"][l].rearrange("(c p) f -> p c f", p=128))])
            gv = gslot[:, 0:2048].rearrange("p (c f) -> p c f", c=4)
            k = 0
            for oc in range(4):
                for t in range(5):
                    lo, n = TT[t]
                    pb = k % 2
                    for c in range(4):
                        S.op("pe", lambda e, c=c, oc=oc, pb=pb, lo=lo, n=n: e.matmul(
                            PS[pb][:, 0:n], lhsT=gv[:, c, oc * 128:oc * 128 + 128], rhs=YG[:, c, lo:lo + n], start=(c == 0), stop=(c == 3)),
                            reads=[gsr] + rg(c * NTOK + lo, c * NTOK + lo + n), writes=[PR(pb)], sig=(c == 3))
                    S.op("act", lambda e, pb=pb, n=n: e.activation(out=SGt[:, 0:n], in_=PS[pb][:, 0:n], func=AF.Sigmoid), reads=[PR(pb)], writes=rSGt)
                    S.op("dve", lambda e, oc=oc, lo=lo, n=n: e.tensor_tensor(out=MO[:, oc, lo:lo + n], in0=YG[:, oc, lo:lo + n], in1=SGt[:, 0:n], op=ALU.mult),
                         reads=rSGt + rg(oc * NTOK + lo, oc * NTOK + lo + n), writes=[MR(oc, t)])
                    k += 1
            rmsnorm(lambda c, lo, n: MO[:, c, lo:lo + n], lambda c, t: MR(c, t), lambda c, lo, n: MO[:, c, lo:lo + n], lambda c, t: MR(c, t), 4,
                    lambda c: GN2[:, l * 2 + 0, c:c + 1])

        def final_out():
            yo = [RG[:, 0:2048].bitcast(F32), RG[:, 2048:4096].bitcast(F32)]
            SQ = [RG[:, 4096:4608], RG[:, 4608:5120]]
            RS = RG[:, 5120:6144].bitcast(F32)
            k = 0
            for t in range(5):
                lo, n = TT[t]
                for c in range(8):
                    b = c % 2
                    S.op("act", lambda e, c=c, b=b, lo=lo, n=n: e.activation(out=SQ[b][:, 0:n], in_=X[:, c, lo:lo + n], func=AF.Square),
                         reads=[XR(c, t)], writes=rg(4096 + 512 * b, 4608 + 512 * b))
                    S.op("pe", lambda e, c=c, b=b, n=n: e.matmul(PS[5][:, 0:n], lhsT=onesB[:], rhs=SQ[b][:, 0:n], start=(c == 0), stop=(c == 7)),
                         reads=rg(4096 + 512 * b, 4608 + 512 * b) + [R("onesB")], writes=[PR(5)])
                S.op("act", lambda e, n=n: e.activation(out=RS[:, 0:n], in_=PS[5][:, 0:n], func=AF.Sqrt, scale=1.0 / D, bias=EPSB[:, 0:1]),
                     reads=[PR(5), R("EPSB")], writes=rg(5120, 6144))
                S.op("dve", lambda e, n=n: e.reciprocal(out=RS[:, 0:n], in_=RS[:, 0:n]), reads=rg(5120, 6144), writes=rg(5120, 6144))
                for c in range(8):
                    S.op("dve", lambda e, c=c, lo=lo, n=n: e.scalar_tensor_tensor(
                        out=X[:, c, lo:lo + n], in0=X[:, c, lo:lo + n], scalar=GN[:, 8, c:c + 1], in1=RS[:, 0:n], op0=ALU.mult, op1=ALU.mult),
                        reads=[XR(c, t), R("GN")] + rg(5120, 6144), writes=[XR(c, t)])
                for sub in range(n // 128):
                    tok0 = lo + 128 * sub
                    b = k % 2
                    for h in range(2):
                        pb = 6 + h
                        for cc in range(4):
                            c = h * 4 + cc
                            S.op("pe", lambda e, c=c, cc=cc, pb=pb, tok0=tok0: e.transpose(
                                PS[pb][:, cc * 128:cc * 128 + 128], X[:, c, tok0:tok0 + 128], identF[:]),
                                reads=[XR(c, t), R("identF")], writes=[PR(pb)], sig=(cc == 3))
                        if h == 0:
                            S.op("dve", lambda e, b=b, pb=pb: e.tensor_copy(out=yo[b][:, 0:512], in_=PS[pb][:, :]), reads=[PR(pb)], writes=rg(2048 * b, 2048 * b + 2048))
                        else:
                            S.op("act", lambda e, b=b, pb=pb: e.activation(out=yo[b][:, 512:1024], in_=PS[pb][:, :], func=AF.Copy), reads=[PR(pb)], writes=rg(2048 * b, 2048 * b + 2048))
                    S.dma("sp", lambda e, b=b, tok0=tok0: e.dma_start(out=dr["y"][tok0:tok0 + 128, :], in_=yo[b]), ("d_yo", b), reads=rg(2048 * b, 2048 * b + 2048))
                    k += 1

        load_consts()
        load_x()
        if cfg.get("ssm", 1) and cfg.get("mix", 1):
            load_ssm_vecs()
        for l in range(2):
            if cfg.get("ssm", 1) and cfg.get("mix", 1):
                ssm_prefetch(l)
            if cfg.get("ffn", 1):
                ffn(l, "ffn1")
            if cfg.get("mix", 1):
                mixer(l)
            if cfg.get("ffn", 1):
                ffn(l, "ffn2", (lambda l=l: ple_load(l)) if cfg.get("ple", 1) else None)
            elif cfg.get("ple", 1):
                ple_load(l)
            if cfg.get("ple", 1):
                ple(l)
        final_out()
        S.wait_all("sp")
        S.emit(st)
        info = {k: len(v) for k, v in S.ops.items()}
        print("instructions per engine:", info, "sems:", len(S.cnt), flush=True)
    return nc


def make_consts():
    c = {}
    c["c_ident"] = np.eye(128, dtype=np.float32)
    i = np.arange(128)[:, None]
    j = np.arange(256)[None, :]
    valid = ((j < 128) & (j > i)) | ((j >= 128) & (j - 128 <= i))
    c["c_amask"] = np.where(valid, 0.0, NEG).astype(np.float32)
    r = np.arange(128)[:, None] % 32
    j = np.arange(136)[None, :]
    valid = (r < 8) & (j > r) & (j <= r + 128)
    c["c_smask"] = np.where(valid | (r >= 8), 0.0, NEG).astype(np.float32)
    row_tau = (np.arange(128) // 32)[:, None]
    col_tau = (np.arange(128) // 32)[None, :]
    c["c_maskT"] = (col_tau >= row_tau).astype(np.float32)
    sel = np.zeros((128, 4, 128), np.float32)
    for p in range(128):
        for jj in range(4):
            sel[p, jj, 32 * jj + (p % 32)] = 1.0
    c["c_sel"] = sel.reshape(128, 512)
    return c


CFG = dict(ffn=1, mix=1, ple=1, ssm=1, attn=1)
_NC_CACHE = {}


def kernel(**inputs):
    cfg = dict(CFG)
    key = tuple(sorted(cfg.items()))
    if key not in _NC_CACHE:
        _NC_CACHE[key] = build(cfg)
    nc = _NC_CACHE[key]
    f = lambda a: np.ascontiguousarray(np.asarray(a, dtype=np.float32))
    consts = make_consts()
    shared = {n: f(inputs[n]) for n in list(WSHAPES) + list(VNAMES)}
    shared.update(consts)
    in_maps = []
    for b in range(NCORES):
        m = dict(shared)
        sl = slice(16 * b, 16 * b + 16)
        m["x_p"] = f(inputs["x_prompt"][b])
        m["x_s"] = f(inputs["x_sample"][sl]).reshape(128, D)
        m["p_p"] = f(inputs["p_prompt"][:, b])
        m["p_s"] = f(inputs["p_sample"][:, sl]).reshape(2, 128, 256)
        m["ck"] = f(inputs["cache_k"][:, sl]).reshape(2, 16, 128, 128)
        m["cv"] = f(inputs["cache_v"][:, sl]).reshape(2, 16, 128, 128)
        m["st_re"] = f(inputs["state_ssm_re"][:, sl]).reshape(2, 16, 2048)
        m["st_im"] = f(inputs["state_ssm_im"][:, sl]).reshape(2, 16, 2048)
        in_maps.append(m)
    res = run_bass_kernel_spmd(nc, in_maps, core_ids=list(range(NCORES)))
    rs = res.results
    y = np.stack([r["y"] for r in rs])
    y_prompt = np.ascontiguousarray(y[:, :2048, :])
    y_sample = np.ascontiguousarray(y[:, 2048:, :].reshape(128, 8, D))
    kp = np.stack([r["kp"] for r in rs], axis=1).reshape(2, 8, 128, 2, 64)
    vp = np.stack([r["vp"] for r in rs], axis=1).reshape(2, 8, 128, 2, 64)
    hp_re = np.stack([r["hp_re"] for r in rs], axis=1).reshape(2, 8, 32, 64)
    hp_im = np.stack([r["hp_im"] for r in rs], axis=1).reshape(2, 8, 32, 64)
    ks = np.concatenate([r["ks"] for r in rs], axis=1).reshape(2, 128, 128, 2, 64)
    vs = np.concatenate([r["vs"] for r in rs], axis=1).reshape(2, 128, 128, 2, 64)
    hs_re = np.concatenate([r["hs_re"] for r in rs], axis=1).reshape(2, 128, 32, 64)
    hs_im = np.concatenate([r["hs_im"] for r in rs], axis=1).reshape(2, 128, 32, 64)
    return (y_prompt, y_sample, kp, vp, hp_re, hp_im, ks, vs, hs_re, hs_im)
```

```python
import math
from contextlib import ExitStack
import numpy as np
import concourse.bass as bass
import concourse.mybir as mybir
from concourse.bass_utils import run_bass_kernel_spmd

F32 = mybir.dt.float32
BF16 = mybir.dt.bfloat16
AF = mybir.ActivationFunctionType
ALU = mybir.AluOpType
AX = mybir.AxisListType

ENGS = ("pe", "dve", "act", "pool", "sp")
NCORES = 8
D = 1024
DFF = 2816
NTOK = 2176
TT = [(0, 512), (512, 512), (1024, 512), (1536, 512), (2048, 128)]
EPS = 1e-6
NEG = -1e30


class Res:
    __slots__ = ("name", "w", "r", "excl")

    def __init__(self, name):
        self.name = name
        self.w = None
        self.r = {}
        self.excl = name[0] == "ps"


class Sched:
    def __init__(self, nc):
        self.nc = nc
        self.ops = {e: [] for e in ENGS}
        self.trace = {e: [] for e in ENGS}
        self.cnt = {}
        self.sems = {}
        self.waited = {e: {} for e in ENGS}
        self.res = {}

    def sem(self, key):
        if key not in self.cnt:
            self.cnt[key] = 0
        return key

    def R(self, *key):
        r = self.res.get(key)
        if r is None:
            r = Res(key)
            self.res[key] = r
        return r

    def _deps(self, reads, writes):
        deps = {}

        def add(k, v):
            if deps.get(k, 0) < v:
                deps[k] = v
        for r in reads:
            if r.w:
                add(*r.w)
        for r in writes:
            if r.w:
                add(*r.w)
            for k, v in r.r.items():
                add(k, v)
        return deps

    def _emit_waits(self, eng, deps):
        wd = self.waited[eng]
        for k, v in deps.items():
            if wd.get(k, 0) >= v:
                continue
            if k == eng and (eng == "pe" or v > self.cnt.get(eng, 0)):
                continue
            wd[k] = v
            self.sem(k)
            self.ops[eng].append(lambda e, k=k, v=v: e.wait_ge(self.sems[k], v))
            self.trace[eng].append(("w", k, v))

    def _commit(self, tok, reads, writes):
        k, v = tok
        for r in reads:
            if r.r.get(k, 0) < v:
                r.r[k] = v
        for r in writes:
            r.w = tok
            r.r = {}

    @staticmethod
    def _flat(xs):
        out = []
        for x in xs:
            if isinstance(x, (list, tuple)):
                out.extend(Sched._flat(x))
            else:
                out.append(x)
        return out

    def op(self, eng, fn, reads=(), writes=(), sig=True):
        reads, writes = self._flat(reads), self._flat(writes)
        ex = [r for r in reads if r.excl]
        if ex:
            writes = list(writes) + ex
        self._emit_waits(eng, self._deps(reads, writes))
        self.sem(eng)
        if sig:
            self.cnt[eng] += 1
            self.ops[eng].append(lambda e, fn=fn, k=eng: fn(e).then_inc(self.sems[k], 1))
            self.trace[eng].append(("i", eng, 1))
            self._commit((eng, self.cnt[eng]), reads, writes)
        else:
            self.ops[eng].append(lambda e, fn=fn: fn(e))
            self._commit((eng, self.cnt[eng] + 1), reads, writes)

    def dma(self, q, fn, semkey, reads=(), writes=()):
        reads, writes = self._flat(reads), self._flat(writes)
        semkey = ("d",) + tuple((writes[0] if writes else reads[0]).name)
        self._emit_waits(q, self._deps(reads, writes))
        self.sem(semkey)
        self.cnt[semkey] += 16
        self.ops[q].append(lambda e, fn=fn, k=semkey: fn(e).then_inc(self.sems[k], 16))
        self.trace[q].append(("i", semkey, 16))
        self._commit((semkey, self.cnt[semkey]), reads, writes)

    def wait_all(self, eng):
        deps = {}
        for r in self.res.values():
            toks = list(r.r.items()) + ([r.w] if r.w else [])
            for k, v in toks:
                if deps.get(k, 0) < v:
                    deps[k] = v
        self._emit_waits(eng, deps)

    def check_deadlock(self):
        val = {k: 0 for k in self.cnt}
        pos = {e: 0 for e in ENGS}
        progress = True
        while progress:
            progress = False
            for e in ENGS:
                tr = self.trace[e]
                while pos[e] < len(tr):
                    kind, k, v = tr[pos[e]]
                    if kind == "w":
                        if val[k] < v:
                            break
                    else:
                        val[k] += v
                    pos[e] += 1
                    progress = True
        for e in ENGS:
            if pos[e] < len(self.trace[e]):
                raise RuntimeError("DEADLOCK: %s stuck at %s" % (e, self.trace[e][pos[e]],))

    def emit(self, stack):
        ops = self.ops
        self.check_deadlock()
        for i, key in enumerate(self.cnt):
            self.sems[key] = stack.enter_context(self.nc.semaphore("sem%d" % i))
        block = stack.enter_context(self.nc.Block())

        @block.tensor
        def _(e):
            for f in ops["pe"]:
                f(e)

        @block.vector
        def _(e):
            for f in ops["dve"]:
                f(e)

        @block.scalar
        def _(e):
            for f in ops["act"]:
                f(e)

        @block.gpsimd
        def _(e):
            for f in ops["pool"]:
                f(e)

        @block.sync
        def _(e):
            for f in ops["sp"]:
                f(e)


WNAMES = ["ffn1_w_gate", "ffn1_w_up", "ffn1_w_down", "w_in", "ssm_w_glu", "w_out",
          "ffn2_w_gate", "ffn2_w_up", "ffn2_w_down", "ple_w_gate", "ple_w_proj"]
WSHAPES = {"ffn1_w_gate": [2, D, DFF], "ffn1_w_up": [2, D, DFF], "ffn1_w_down": [2, DFF, D], "w_in": [2, D, 1280],
           "ssm_w_glu": [2, 512, 512], "w_out": [2, D, D], "ffn2_w_gate": [2, D, DFF], "ffn2_w_up": [2, D, DFF],
           "ffn2_w_down": [2, DFF, D], "ple_w_gate": [2, D, D], "ple_w_proj": [2, 256, D]}
VNAMES = {"ffn1_norm": [2, D], "mix_norm": [2, D], "ffn2_norm": [2, D], "ple_norm": [2, D], "final_norm": [D],
          "ssm_out_norm": [2, 512], "attn_out_norm": [2, 512], "attn_sinks": [2, 8],
          "ssm_lam_re": [2, 32, 64], "ssm_lam_im": [2, 32, 64], "ssm_log_dt": [2, 32],
          "ssm_b_re": [2, 32, 64, 16], "ssm_b_im": [2, 32, 64, 16], "ssm_c_re": [2, 32, 16, 64],
          "ssm_c_im": [2, 32, 16, 64], "ssm_d": [2, 512]}
CNAMES = {"c_ident": [128, 128], "c_amask": [128, 256], "c_smask": [128, 136], "c_maskT": [128, 128],
          "c_sel": [128, 512]}
INAMES = {"x_p": [2048, D], "x_s": [128, D], "p_p": [2, 2048, 256], "p_s": [2, 128, 256],
          "ck": [2, 16, 128, 128], "cv": [2, 16, 128, 128], "st_re": [2, 16, 2048], "st_im": [2, 16, 2048]}
ONAMES = {"y": [NTOK, D], "kp": [2, 128, 128], "vp": [2, 128, 128], "hp_re": [2, 2048], "hp_im": [2, 2048],
          "ks": [2, 16, 128, 128], "vs": [2, 16, 128, 128], "hs_re": [2, 16, 2048], "hs_im": [2, 16, 2048]}


def build(cfg):
    nc = bass.Bass("TRN2", target_bir_lowering=False)
    dr = {}
    for n, s in list(WSHAPES.items()) + list(VNAMES.items()) + list(CNAMES.items()) + list(INAMES.items()):
        dr[n] = nc.dram_tensor(n, s, F32, kind="ExternalInput").ap()
    for n, s in ONAMES.items():
        dr[n] = nc.dram_tensor(n, s, F32, kind="ExternalOutput").ap()
    st = ExitStack()
    with st:
        S = Sched(nc)
        R = S.R

        def T(name, shape, dt):
            return st.enter_context(nc.sbuf_tensor(name, shape, dt))

        X = T("X", [128, 8, NTOK], F32)
        HT = T("HT", [128, 8, NTOK], BF16)
        R1 = T("R1", [128, 6 * NTOK], BF16)
        WP = T("WP", [128, 3, 4096], BF16)
        RG = T("RG", [128, 4 * NTOK], BF16)
        SPR = T("SPR", [128, 10240], BF16)
        identF = T("identF", [128, 128], F32)
        identB = T("identB", [128, 128], BF16)
        onesB = T("onesB", [128, 128], BF16)
        amask = T("amask", [128, 256], F32)
        amaskB = T("amaskB", [128, 256], BF16)
        smask = T("smask", [128, 136], F32)
        GN = T("GN", [128, 9, 8], F32)
        GN2 = T("GN2", [128, 4, 4], F32)
        SK = T("SK", [128, 16], F32)
        SM = T("SM", [128, 64], F32)
        SV = T("SV", [128, 48, 16], F32)
        E0 = T("E0", [128, 2, 16, 32], F32)
        E1 = T("E1", [128, 2, 16, 16], F32)
        H0 = T("H0", [128, 2, 16, 16], F32)
        SelC = T("SelC", [128, 4, 128], BF16)
        PTMP = T("PTMP", [128, 512], F32)
        SVL = T("SVL", [128, 8, 16], F32)
        maskT = T("maskT", [128, 128], F32)
        PS = [st.enter_context(nc.psum_tensor("ps%d" % i, [128, 512], F32)) for i in range(8)]

        def PR(i):
            return R("ps", i)

        def r1(a, b):
            return [R("R1b", k) for k in range(a // 32, (b - 1) // 32 + 1)]

        def R1r(c, t):
            return r1(c * NTOK + TT[t][0], c * NTOK + TT[t][0] + TT[t][1])

        def rg(a, b):
            return [R("RG", k) for k in range(a // 64, (b - 1) // 64 + 1)]

        def spr(a, b):
            return [R("SPR", k) for k in range(a // 64, (b - 1) // 64 + 1)]

        def XR(c, t):
            return R("X", c, t)

        def HR(c, t):
            return R("HT", c, t)

        wq = {"n": 0}

        def wload(dmas):
            s = wq["n"] % 3
            wq["n"] += 1
            slot = WP[:, s, :]
            for ov, ia in dmas:
                S.dma("pool", lambda e, o=ov(slot), i=ia: e.dma_start(out=o, in_=i), ("dw", s), writes=[R("wp", s)])
            return slot, R("wp", s)

        def wview(l, name):
            return dr[name][l].rearrange("(c p) f -> p c f", p=128)

        def load_consts():
            S.dma("sp", lambda e: e.dma_start(out=identF[:], in_=dr["c_ident"]), "d_c", writes=[R("identF")])
            S.dma("sp", lambda e: e.dma_start(out=amask[:], in_=dr["c_amask"]), "d_c", writes=[R("amask")])
            S.dma("sp", lambda e: e.dma_start(out=smask[:], in_=dr["c_smask"]), "d_c", writes=[R("smask")])
            S.dma("sp", lambda e: e.dma_start(out=maskT[:], in_=dr["c_maskT"]), "d_c", writes=[R("maskT")])
            S.dma("pool", lambda e: e.dma_start(out=SelC[:].rearrange("p a b -> p (a b)"), in_=dr["c_sel"]), "d_sel", writes=[R("SelC")])
            S.op("dve", lambda e: e.tensor_copy(out=identB[:], in_=identF[:]), reads=[R("identF")], writes=[R("identB")])
            S.op("dve", lambda e: e.memset(onesB[:], 1.0), writes=[R("onesB")])
            S.op("dve", lambda e: e.tensor_copy(out=amaskB[:], in_=amask[:]), reads=[R("amask")], writes=[R("amaskB")])
            for l in range(2):
                for k, n in enumerate(["ffn1_norm", "mix_norm", "ffn2_norm", "ple_norm"]):
                    S.dma("sp", lambda e, l=l, k=k, n=n: e.dma_start(
                        out=GN[:, l * 4 + k, :], in_=dr[n][l].rearrange("(c p) -> p c", p=128),
                        allow_slow_non_contiguous=True), "d_c", writes=[R("GN")])
                for k, n in enumerate(["ssm_out_norm", "attn_out_norm"]):
                    S.dma("sp", lambda e, l=l, k=k, n=n: e.dma_start(
                        out=GN2[:, l * 2 + k, :], in_=dr[n][l].rearrange("(c p) -> p c", p=128),
                        allow_slow_non_contiguous=True), "d_c", writes=[R("GN")])
            S.dma("sp", lambda e: e.dma_start(out=GN[:, 8, :], in_=dr["final_norm"].rearrange("(c p) -> p c", p=128),
                                              allow_slow_non_contiguous=True), "d_c", writes=[R("GN")])
            S.dma("sp", lambda e: e.dma_start(out=SK[:], in_=dr["attn_sinks"].rearrange("l h -> (l h)").partition_broadcast(128)),
                  "d_c", writes=[R("SK")])

        def load_x():
            xin = [RG[:, 2048 * b:2048 * b + 2048].bitcast(F32) for b in range(4)]
            for i in range(17):
                b = i % 4
                src = dr["x_p"][128 * i:128 * i + 128, :] if i < 16 else dr["x_s"]
                S.dma("sp", lambda e, b=b, src=src: e.dma_start(out=xin[b], in_=src), ("d_xin", b), writes=rg(2048 * b, 2048 * b + 2048))
                t = i // 4 if i < 16 else 4
                for h in range(2):
                    pb = 6 + h
                    for cc in range(4):
                        c = h * 4 + cc
                        S.op("pe", lambda e, b=b, c=c, cc=cc, pb=pb: e.transpose(PS[pb][:, cc * 128:cc * 128 + 128],
                                                                               xin[b][:, c * 128:c * 128 + 128], identF[:]),
                             reads=rg(2048 * b, 2048 * b + 2048) + [R("identF")], writes=[PR(pb)], sig=(cc == 3))
                    eng = "dve" if h == 0 else "act"
                    dst = X[:, h * 4:h * 4 + 4, 128 * i:128 * i + 128]
                    srcp = PS[pb][:, :].rearrange("p (c n) -> p c n", c=4)
                    if eng == "dve":
                        S.op("dve", lambda e, dst=dst, srcp=srcp: e.tensor_copy(out=dst, in_=srcp),
                             reads=[PR(pb)], writes=[XR(h * 4 + cc, t) for cc in range(4)])
                    else:
                        S.op("act", lambda e, dst=dst, srcp=srcp: e.activation(out=dst, in_=srcp, func=AF.Copy),
                             reads=[PR(pb)], writes=[XR(h * 4 + cc, t) for cc in range(4)])

        def rmsnorm(src, srcR, dst, dstR, nch, gain, tiles=range(5), psb=5):
            SQ = [RG[:, 4096:4608], RG[:, 4608:5120]]
            RS = RG[:, 5120:6144].bitcast(F32)
            for t in tiles:
                lo, n = TT[t]
                for c in range(nch):
                    b = c % 2
                    S.op("act", lambda e, c=c, b=b, lo=lo, n=n: e.activation(out=SQ[b][:, 0:n], in_=src(c, lo, n), func=AF.Square),
                         reads=[srcR(c, t)], writes=rg(4096 + 512 * b, 4608 + 512 * b))
                    S.op("pe", lambda e, c=c, b=b, n=n: e.matmul(PS[psb][:, 0:n], lhsT=onesB[:], rhs=SQ[b][:, 0:n],
                                                                 start=(c == 0), stop=(c == nch - 1)),
                         reads=rg(4096 + 512 * b, 4608 + 512 * b) + [R("onesB")], writes=[PR(psb)], sig=True)
                S.op("act", lambda e, n=n: e.activation(out=RS[:, 0:n], in_=PS[psb][:, 0:n], func=AF.Sqrt,
                                                         scale=1.0 / (128 * nch), bias=EPSB[:, 0:1]),
                     reads=[PR(psb), R("EPSB")], writes=rg(5120, 6144))
                S.op("dve", lambda e, n=n: e.reciprocal(out=RS[:, 0:n], in_=RS[:, 0:n]), reads=rg(5120, 6144), writes=rg(5120, 6144))
                for c in range(nch):
                    S.op("dve", lambda e, c=c, lo=lo, n=n: e.scalar_tensor_tensor(
                        out=dst(c, lo, n), in0=src(c, lo, n), scalar=gain(c), in1=RS[:, 0:n], op0=ALU.mult, op1=ALU.mult),
                        reads=[srcR(c, t), R("GN")] + rg(5120, 6144), writes=[dstR(c, t)])

        EPSB = T("EPSB", [128, 1], F32)
        S.op("dve", lambda e: e.memset(EPSB[:], EPS), writes=[R("EPSB")])

        def norm_X(gi):
            rmsnorm(lambda c, lo, n: X[:, c, lo:lo + n], XR, lambda c, lo, n: HT[:, c, lo:lo + n], HR, 8,
                    lambda c: GN[:, gi, c:c + 1])

        def ffn(l, pre, mid_hook=None):
            norm_X(l * 4 + (0 if pre == "ffn1" else 2))
            A = R1[:, :].rearrange("p (c n) -> p c n", c=6)
            SG = [RG[:, 6144:7168].bitcast(F32), RG[:, 7168:8192].bitcast(F32)]
            quarters = [[0, 1, 2], [3, 4, 5], [6, 7, 8], [9, 10]]
            wg, wu, wd = wview(l, pre + "_w_gate"), wview(l, pre + "_w_up"), dr[pre + "_w_down"][l]
            k = 0
            for q in quarters:
                if q is quarters[1] and mid_hook is not None:
                    mid_hook()
                for pi, jp in enumerate(q):
                    f0 = jp * 256
                    slot, sr = wload([
                        (lambda s: s[:, 0:2048].rearrange("p (c f) -> p c f", c=8), wg[:, :, f0:f0 + 256]),
                        (lambda s: s[:, 2048:4096].rearrange("p (c f) -> p c f", c=8), wu[:, :, f0:f0 + 256])])
                    sv = slot.rearrange("p (g c f) -> p g c f", g=2, c=8)
                    for fc in range(2):
                        ai = pi * 2 + fc
                        for t in range(5):
                            lo, n = TT[t]
                            pg, pu = k % 2, 2 + k % 2
                            for c in range(8):
                                S.op("pe", lambda e, c=c, fc=fc, pg=pg, lo=lo, n=n, sv=sv: e.matmul(
                                    PS[pg][:, 0:n], lhsT=sv[:, 0, c, fc * 128:fc * 128 + 128], rhs=HT[:, c, lo:lo + n],
                                    start=(c == 0), stop=(c == 7)), reads=[sr, HR(c, t)], writes=[PR(pg)], sig=(c == 7))
                            for c in range(8):
                                S.op("pe", lambda e, c=c, fc=fc, pu=pu, lo=lo, n=n, sv=sv: e.matmul(
                                    PS[pu][:, 0:n], lhsT=sv[:, 1, c, fc * 128:fc * 128 + 128], rhs=HT[:, c, lo:lo + n],
                                    start=(c == 0), stop=(c == 7)), reads=[sr, HR(c, t)], writes=[PR(pu)], sig=(c == 7))
                            S.op("act", lambda e, pg=pg, n=n, b=k % 2: e.activation(out=SG[b][:, 0:n], in_=PS[pg][:, 0:n], func=AF.Silu),
                                 reads=[PR(pg)], writes=rg(6144 + 1024 * (k % 2), 7168 + 1024 * (k % 2)))
                            S.op("dve", lambda e, pu=pu, n=n, lo=lo, ai=ai, b=k % 2: e.tensor_tensor(
                                out=A[:, ai, lo:lo + n], in0=PS[pu][:, 0:n], in1=SG[b][:, 0:n], op=ALU.mult),
                                reads=[PR(pu)] + rg(6144 + 1024 * (k % 2), 7168 + 1024 * (k % 2)), writes=[R1r(ai, t)])
                            k += 1
                nj = 2 * len(q)
                r0 = q[0] * 256
                for dh in range(2):
                    slot, sr = wload([(lambda s, nj=nj: s[:, 0:nj * 512].rearrange("p (j d) -> p j d", j=nj),
                                       wd[r0:r0 + nj * 128, dh * 512:dh * 512 + 512].rearrange("(j p) d -> p j d", p=128))])
                    sv = slot[:, 0:nj * 512].rearrange("p (j d) -> p j d", j=nj)
                    for i in range(4):
                        dc = dh * 4 + i
                        for t in range(5):
                            lo, n = TT[t]
                            pd = 4 + k % 2
                            for jj in range(nj):
                                S.op("pe", lambda e, jj=jj, i=i, pd=pd, lo=lo, n=n, sv=sv, nj=nj: e.matmul(
                                    PS[pd][:, 0:n], lhsT=sv[:, jj, i * 128:i * 128 + 128], rhs=A[:, jj, lo:lo + n],
                                    start=(jj == 0), stop=(jj == nj - 1)), reads=[sr, R1r(jj, t)], writes=[PR(pd)], sig=(jj == nj - 1))
                            S.op("dve", lambda e, pd=pd, dc=dc, lo=lo, n=n: e.scalar_tensor_tensor(
                                out=X[:, dc, lo:lo + n], in0=PS[pd][:, 0:n], scalar=0.5, in1=X[:, dc, lo:lo + n],
                                op0=ALU.mult, op1=ALU.add), reads=[PR(pd), XR(dc, t)], writes=[XR(dc, t)])
                            k += 1

        def ple_load(l):
            PT = SPR[:, 0:2 * NTOK].rearrange("p (c n) -> p c n", c=2)
            pin = [RG[:, 512 * b:512 * b + 512].bitcast(F32) for b in range(8)]
            def pdma(i):
                b = i % 8
                src = dr["p_p"][l, 128 * i:128 * i + 128, :] if i < 16 else dr["p_s"][l]
                S.dma("sp", lambda e, b=b, src=src: e.dma_start(out=pin[b], in_=src), None, writes=rg(512 * b, 512 * b + 512))
            for i in range(8):
                pdma(i)
            for i in range(17):
                b = i % 8
                for c in range(2):
                    S.op("pe", lambda e, b=b, c=c: e.transpose(PS[7][:, c * 128:c * 128 + 128], pin[b][:, c * 128:c * 128 + 128], identF[:]),
                         reads=rg(512 * b, 512 * b + 512) + [R("identF")], writes=[PR(7)], sig=(c == 1))
                S.op("act", lambda e, i=i: e.activation(out=PT[:, :, 128 * i:128 * i + 128],
                                                         in_=PS[7][:, 0:256].rearrange("p (c n) -> p c n", c=2), func=AF.Copy),
                     reads=[PR(7)], writes=spr(128 * i, 128 * i + 128) + spr(NTOK + 128 * i, NTOK + 128 * i + 128))
                if i + 8 < 17:
                    pdma(i + 8)

        def ple(l):
            norm_X(l * 4 + 3)
            PT = SPR[:, 0:2 * NTOK].rearrange("p (c n) -> p c n", c=2)
            SG = [RG[:, 6144:7168].bitcast(F32), RG[:, 7168:8192].bitcast(F32)]
            wp_ = dr["ple_w_proj"][l].rearrange("(c p) f -> p c f", p=128)
            wg = wview(l, "ple_w_gate")
            k = 0
            for dh in range(2):
                pslot, psr = wload([(lambda s: s[:, 0:1024].rearrange("p (c f) -> p c f", c=2), wp_[:, :, dh * 512:dh * 512 + 512])])
                pv = pslot[:, 0:1024].rearrange("p (c f) -> p c f", c=2)
                gslot, gsr = wload([(lambda s: s[:, 0:4096].rearrange("p (c f) -> p c f", c=8), wg[:, :, dh * 512:dh * 512 + 512])])
                gv = gslot.rearrange("p (c f) -> p c f", c=8)
                for i in range(4):
                    dc = dh * 4 + i
                    for t in range(5):
                        lo, n = TT[t]
                        pg, pp = k % 2, 2 + k % 2
                        for c in range(8):
                            S.op("pe", lambda e, c=c, i=i, pg=pg, lo=lo, n=n, gv=gv: e.matmul(
                                PS[pg][:, 0:n], lhsT=gv[:, c, i * 128:i * 128 + 128], rhs=HT[:, c, lo:lo + n],
                                start=(c == 0), stop=(c == 7)), reads=[gsr, HR(c, t)], writes=[PR(pg)], sig=(c == 7))
                        for c in range(2):
                            S.op("pe", lambda e, c=c, i=i, pp=pp, lo=lo, n=n, pv=pv: e.matmul(
                                PS[pp][:, 0:n], lhsT=pv[:, c, i * 128:i * 128 + 128], rhs=PT[:, c, lo:lo + n],
                                start=(c == 0), stop=(c == 1)), reads=[psr] + spr(c * NTOK + lo, c * NTOK + lo + n), writes=[PR(pp)], sig=(c == 1))
                        b = k % 2
                        S.op("act", lambda e, pg=pg, n=n, b=b: e.activation(out=SG[b][:, 0:n], in_=PS[pg][:, 0:n], func=AF.Sigmoid),
                             reads=[PR(pg)], writes=rg(6144 + 1024 * b, 7168 + 1024 * b))
                        S.op("dve", lambda e, pp=pp, n=n, b=b: e.tensor_tensor(out=SG[b][:, 0:n], in0=PS[pp][:, 0:n], in1=SG[b][:, 0:n], op=ALU.mult),
                             reads=[PR(pp)] + rg(6144 + 1024 * b, 7168 + 1024 * b), writes=rg(6144 + 1024 * b, 7168 + 1024 * b))
                        S.op("dve", lambda e, dc=dc, lo=lo, n=n, b=b: e.tensor_tensor(out=X[:, dc, lo:lo + n], in0=X[:, dc, lo:lo + n], in1=SG[b][:, 0:n], op=ALU.add),
                             reads=[XR(dc, t)] + rg(6144 + 1024 * b, 7168 + 1024 * b), writes=[XR(dc, t)])
                        k += 1

        MO = HT

        def MR(c, t):
            return R("HT", c, t)

        def mixer(l):
            norm_X(l * 4 + 1)
            QT = R1[:, 0:4 * NTOK].rearrange("p (c n) -> p c n", c=4)
            KD = R1[:, 4 * NTOK:6 * NTOK].rearrange("p (c n) -> p c n", c=2)
            VT = SPR[:, 0:17 * 128].rearrange("p (i f) -> p i f", i=17)
            KVO = SPR[:, 2176:2176 + 1024].bitcast(F32).rearrange("p (i f) -> p i f", i=2)
            win = wview(l, "w_in")
            if cfg.get("ssm", 1):
                ssm(l)
            qslot, qsr = wload([(lambda s: s.rearrange("p (c f) -> p c f", c=8), win[:, :, 512:1024])])
            qv = qslot.rearrange("p (c f) -> p c f", c=8)

            def kvd(s, a, b):
                return s.rearrange("p (c f) -> p c f", c=8)[:, :, a:b]
            kslot, ksr = wload([
                (lambda s: kvd(s, 0, 64), win[:, :, 1024:1088]), (lambda s: kvd(s, 64, 128), win[:, :, 1024:1088]),
                (lambda s: kvd(s, 128, 192), win[:, :, 1088:1152]), (lambda s: kvd(s, 192, 256), win[:, :, 1088:1152]),
                (lambda s: kvd(s, 256, 384), win[:, :, 1152:1280]), (lambda s: kvd(s, 384, 512), win[:, :, 1024:1152])])
            kv_ = kslot.rearrange("p (c f) -> p c f", c=8)
            k = 0
            for qc in range(4):
                for t in range(5):
                    lo, n = TT[t]
                    pb = k % 2
                    for c in range(8):
                        S.op("pe", lambda e, c=c, qc=qc, pb=pb, lo=lo, n=n: e.matmul(
                            PS[pb][:, 0:n], lhsT=qv[:, c, qc * 128:qc * 128 + 128], rhs=HT[:, c, lo:lo + n],
                            start=(c == 0), stop=(c == 7)), reads=[qsr, HR(c, t)], writes=[PR(pb)], sig=(c == 7))
                    S.op("act", lambda e, qc=qc, pb=pb, lo=lo, n=n: e.activation(out=QT[:, qc, lo:lo + n], in_=PS[pb][:, 0:n], func=AF.Copy),
                         reads=[PR(pb)], writes=[R1r(qc, t)])
                    k += 1
            for kc in (range(2) if cfg.get("mixlevel", 9) >= 2 else []):
                for t in range(5):
                    lo, n = TT[t]
                    pb = k % 2
                    for c in range(8):
                        S.op("pe", lambda e, c=c, kc=kc, pb=pb, lo=lo, n=n: e.matmul(
                            PS[pb][:, 0:n], lhsT=kv_[:, c, kc * 128:kc * 128 + 128], rhs=HT[:, c, lo:lo + n],
                            start=(c == 0), stop=(c == 7)), reads=[ksr, HR(c, t)], writes=[PR(pb)], sig=(c == 7))
                    S.op("act", lambda e, kc=kc, pb=pb, lo=lo, n=n: e.activation(out=KD[:, kc, lo:lo + n], in_=PS[pb][:, 0:n], func=AF.Copy),
                         reads=[PR(pb)], writes=[R1r(4 + kc, t)])
                    k += 1
            for i in (range(17) if cfg.get("mixlevel", 9) >= 3 else []):
                t = i // 4 if i < 16 else 4
                pb = k % 2
                for c in range(8):
                    S.op("pe", lambda e, c=c, i=i, pb=pb: e.matmul(
                        PS[pb][:, 0:256], lhsT=HT[:, c, 128 * i:128 * i + 128], rhs=kv_[:, c, 256:512],
                        start=(c == 0), stop=(c == 7)), reads=[ksr, HR(c, t)], writes=[PR(pb)], sig=(c == 7))
                S.op("act", lambda e, i=i, pb=pb: e.activation(out=VT[:, i, :], in_=PS[pb][:, 0:128], func=AF.Copy),
                     reads=[PR(pb)], writes=spr(128 * i, 128 * i + 128))
                if i >= 15:
                    S.op("dve", lambda e, i=i, pb=pb: e.tensor_copy(out=KVO[:, i - 15, :], in_=PS[pb][:, 0:256]),
                         reads=[PR(pb)], writes=spr(2176 + 512 * (i - 15), 2176 + 512 * (i - 15) + 512))
                k += 1
            VNa = [SPR[:, 7296:8320], SPR[:, 9024:10048]]
            rVN = spr(7296, 8320) + spr(9024, 10048)
            for s4 in (range(4) if cfg.get("mixlevel", 9) >= 3 else []):
                for sl in range(4):
                    s = s4 * 4 + sl
                    for c in range(8):
                        S.op("pe", lambda e, c=c, s=s, sl=sl: e.matmul(
                            PS[6][0:8, sl * 128:sl * 128 + 128], lhsT=HT[:, c, 2048 + 8 * s:2048 + 8 * s + 8], rhs=kv_[:, c, 256:384],
                            start=(c == 0), stop=(c == 7)), reads=[ksr, HR(c, 4)], writes=[PR(6)], sig=(c == 7 and sl == 3))
                S.op("act", lambda e, s4=s4: e.activation(out=VNa[s4 // 2][0:8, (s4 % 2) * 512:(s4 % 2) * 512 + 512],
                                                           in_=PS[6][0:8, :], func=AF.Copy),
                     reads=[PR(6)], writes=rVN)
            if cfg.get("mixlevel", 9) >= 4:
                S.dma("sp", lambda e: e.dma_start(out=dr["vp"][l], in_=KVO[:, 0, 0:128]), "d_out", reads=spr(2176, 2688))
                S.dma("sp", lambda e: e.dma_start(out=dr["kp"][l], in_=KVO[:, 0, 128:256]), "d_out", reads=spr(2176, 2688))
            if cfg.get("ssm", 1):
                glu(l)
            else:
                for c in range(4):
                    S.op("dve", lambda e, c=c: e.memset(MO[:, c, :], 0.0), reads=[], writes=[MR(c, t) for t in range(5)])
            if cfg.get("attn", 1):
                if cfg.get("attn_p", 1):
                    attn_prompt(l, QT, KD, VT)
                if cfg.get("attn_s", 1):
                    attn_sample(l, QT, KD, VT, KVO)
                rmsnorm(lambda c, lo, n: MO[:, 4 + c, lo:lo + n], lambda c, t: MR(4 + c, t),
                        lambda c, lo, n: MO[:, 4 + c, lo:lo + n], lambda c, t: MR(4 + c, t), 4,
                        lambda c: GN2[:, l * 2 + 1, c:c + 1])
            else:
                for c in range(4, 8):
                    S.op("dve", lambda e, c=c: e.memset(MO[:, c, :], 0.0), reads=[], writes=[MR(c, t) for t in range(5)])
                zero_kv_sample(l, KVO)
            wo = wview(l, "w_out")
            k = 0
            for dh in range(2):
                slot, sr = wload([(lambda s: s.rearrange("p (c f) -> p c f", c=8), wo[:, :, dh * 512:dh * 512 + 512])])
                sv = slot.rearrange("p (c f) -> p c f", c=8)
                for i in range(4):
                    dc = dh * 4 + i
                    for t in range(5):
                        lo, n = TT[t]
                        pb = 2 + k % 2
                        for c in range(8):
                            S.op("pe", lambda e, c=c, i=i, pb=pb, lo=lo, n=n, sv=sv: e.matmul(
                                PS[pb][:, 0:n], lhsT=sv[:, c, i * 128:i * 128 + 128], rhs=MO[:, c, lo:lo + n],
                                start=(c == 0), stop=(c == 7)), reads=[sr, MR(c, t)], writes=[PR(pb)], sig=(c == 7))
                        S.op("dve", lambda e, pb=pb, dc=dc, lo=lo, n=n: e.tensor_tensor(
                            out=X[:, dc, lo:lo + n], in0=PS[pb][:, 0:n], in1=X[:, dc, lo:lo + n], op=ALU.add),
                            reads=[PR(pb), XR(dc, t)], writes=[XR(dc, t)])
                        k += 1

        def zero_kv_sample(l, KVO):
            pass

        def softmax_block(l, kvh, psA, psB, nk, maskap, Sf, rSf, Pb, rPb, so=0, part="AB", pemask=False):
            mx, ng, sm, es = SM[:, so:so + 4], SM[:, so + 4:so + 8], SM[:, so + 8:so + 12], SM[:, so + 12:so + 16]
            rmx, rng_, rsm, res_ = R("SMx", so), R("SMn", so), R("SMs", so), R("SMe", so)
            for h2, pp in (enumerate((psA, psB)) if ("A" in part and not pemask) else []):
                S.op("dve", lambda e, h2=h2, pp=pp: e.tensor_tensor(
                    out=Sf[:, h2:4:2, 0:nk], in0=PS[pp][:, :].rearrange("p (g k) -> p g k", g=2)[:, :, 0:nk],
                    in1=maskap.unsqueeze(1).to_broadcast([128, 2, nk]), op=ALU.add),
                    reads=[PR(pp), R("amask"), R("smask")], writes=rSf)
            if "A" not in part:
                S.op("dve", lambda e: e.tensor_tensor(out=sm, in0=sm, in1=es, op=ALU.add), reads=[rsm, res_], writes=[rsm])
                S.op("dve", lambda e: e.reciprocal(out=sm, in_=sm), reads=[rsm], writes=[rsm])
                S.op("dve", lambda e: e.tensor_tensor(out=Pb[:, :, 0:nk], in0=Sf[:, :, 0:nk], in1=sm.unsqueeze(2).to_broadcast([128, 4, nk]), op=ALU.mult),
                     reads=rSf + [rsm], writes=rPb)
                return
            if pemask:
                for h2, pp in enumerate((psA, psB)):
                    S.op("dve", lambda e, h2=h2, pp=pp: e.tensor_reduce(
                        out=mx[:, h2:4:2], in_=PS[pp][:, :].rearrange("p (g k) -> p g k", g=2)[:, :, 0:nk], axis=AX.X, op=ALU.max),
                        reads=[PR(pp)], writes=[rmx])
            else:
                S.op("dve", lambda e: e.tensor_reduce(out=mx, in_=Sf[:, :, 0:nk], axis=AX.X, op=ALU.max), reads=rSf, writes=[rmx])
            S.op("dve", lambda e: e.scalar_tensor_tensor(out=mx, in0=mx, scalar=0.125, in1=SK[:, l * 8 + kvh * 4:l * 8 + kvh * 4 + 4],
                                                          op0=ALU.mult, op1=ALU.max), reads=[rmx, R("SK")], writes=[rmx])
            S.op("dve", lambda e: e.tensor_scalar(out=ng, in0=mx, scalar1=-1.0, scalar2=None, op0=ALU.mult), reads=[rmx], writes=[rng_])
            S.op("dve", lambda e: e.tensor_tensor(out=es, in0=SK[:, l * 8 + kvh * 4:l * 8 + kvh * 4 + 4], in1=ng, op=ALU.add),
                 reads=[rng_, R("SK")], writes=[res_])
            S.op("act", lambda e: e.activation(out=es, in_=es, func=AF.Exp), reads=[res_], writes=[res_])
            for g in range(4):
                if pemask:
                    pp = (psA, psB)[g % 2]
                    S.op("act", lambda e, g=g, pp=pp: e.activation(out=Sf[:, g, 0:nk], in_=PS[pp][:, (g // 2) * 256:(g // 2) * 256 + nk], func=AF.Exp,
                                                                    scale=0.125, bias=ng[:, g:g + 1], accum_out=sm[:, g:g + 1]),
                         reads=[PR(pp), rng_], writes=rSf + [rsm])
                else:
                    S.op("act", lambda e, g=g: e.activation(out=Sf[:, g, 0:nk], in_=Sf[:, g, 0:nk], func=AF.Exp, scale=0.125,
                                                             bias=ng[:, g:g + 1], accum_out=sm[:, g:g + 1]),
                         reads=rSf + [rng_], writes=rSf + [rsm])
            if "B" not in part:
                return
            S.op("dve", lambda e: e.tensor_tensor(out=sm, in0=sm, in1=es, op=ALU.add), reads=[rsm, res_], writes=[rsm])
            S.op("dve", lambda e: e.reciprocal(out=sm, in_=sm), reads=[rsm], writes=[rsm])
            S.op("dve", lambda e: e.tensor_tensor(out=Pb[:, :, 0:nk], in0=Sf[:, :, 0:nk], in1=sm.unsqueeze(2).to_broadcast([128, 4, nk]), op=ALU.mult),
                 reads=rSf + [rsm], writes=rPb)

        def attn_prompt(l, QT, KD, VT):
            sets = [
                dict(Sf=SPR[:, 3200:5248].bitcast(F32).rearrange("p (g k) -> p g k", g=4), rSf=spr(3200, 5248),
                     Pb=SPR[:, 5248:6272].rearrange("p (g k) -> p g k", g=4), rPb=spr(5248, 6272),
                     PTs=SPR[:, 6272:7296].rearrange("p (g b q) -> p g b q", g=4, b=2), rPT=spr(6272, 7296), ps=(0, 1, 2, 3), so=0),
                dict(Sf=RG[:, 6144:8192].bitcast(F32).rearrange("p (g k) -> p g k", g=4), rSf=rg(6144, 8192),
                     Pb=RG[:, 0:1024].rearrange("p (g k) -> p g k", g=4), rPb=rg(0, 1024),
                     PTs=RG[:, 1024:2048].rearrange("p (g b q) -> p g b q", g=4, b=2), rPT=rg(1024, 2048), ps=(4, 5, 6, 7), so=16),
            ]

            def geo(it):
                nb, kvh = it // 2, it % 2
                return dict(nb=nb, kvh=kvh, t=nb // 4, tk0=(nb - 1) // 4 if nb > 0 else 0, k0=128 * (nb - 1) if nb > 0 else 0,
                            nk=256 if nb > 0 else 128, mcol=0 if nb > 0 else 128)

            def st_scores(it):
                G, B = geo(it), sets[it % 2]
                nb, kvh, t, tk0, k0, nk = G["nb"], G["kvh"], G["t"], G["tk0"], G["k0"], G["nk"]
                for g in range(4):
                    h = 4 * kvh + g
                    ch, hf = h // 2, h % 2
                    pp = B["ps"][hf]
                    S.op("pe", lambda e, ch=ch, hf=hf, pp=pp, g=g, nb=nb, k0=k0, nk=nk, kvh=kvh: e.matmul(
                        PS[pp][:, (g // 2) * 256:(g // 2) * 256 + nk], lhsT=QT[64 * hf:64 * hf + 64, ch, 128 * nb:128 * nb + 128],
                        rhs=KD[64 * hf:64 * hf + 64, kvh, k0:k0 + nk], start=True, stop=False),
                        reads=[R1r(ch, t), R1r(4 + kvh, t), R1r(4 + kvh, tk0)], writes=[PR(pp)], sig=False)
                    mc = G["mcol"]
                    S.op("pe", lambda e, pp=pp, g=g, nk=nk, mc=mc: e.matmul(
                        PS[pp][:, (g // 2) * 256:(g // 2) * 256 + nk], lhsT=identB[:], rhs=amaskB[:, mc:mc + nk], start=False, stop=True),
                        reads=[R("identB"), R("amaskB")], writes=[PR(pp)], sig=(g >= 2))

            def st_softmax(it, part):
                G, B = geo(it), sets[it % 2]
                softmax_block(l, G["kvh"], B["ps"][0], B["ps"][1], G["nk"], amask[:, G["mcol"]:G["mcol"] + G["nk"]],
                              B["Sf"], B["rSf"], B["Pb"], B["rPb"], B["so"], part, pemask=True)

            def st_pv(it):
                G, B = geo(it), sets[it % 2]
                nb, kvh, t, nk = G["nb"], G["kvh"], G["t"], G["nk"]
                Pb, PTs, pT, pO = B["Pb"], B["PTs"], B["ps"][2], B["ps"][3]
                nkb = nk // 128
                ptp = PS[pT][:, :].bitcast(BF16).rearrange("p (g b q) -> p g b q", g=4, b=2)
                for g in range(4):
                    for kb in range(nkb):
                        S.op("pe", lambda e, g=g, kb=kb, ptp=ptp, Pb=Pb: e.transpose(ptp[:, g, kb, :], Pb[:, g, kb * 128:kb * 128 + 128], identB[:]),
                             reads=B["rPb"] + [R("identB")], writes=[PR(pT)], sig=(g == 3 and kb == nkb - 1))
                S.op("act", lambda e, nkb=nkb, ptp=ptp, PTs=PTs: e.activation(out=PTs[:, :, 0:nkb, :], in_=ptp[:, :, 0:nkb, :], func=AF.Copy),
                     reads=[PR(pT)], writes=B["rPT"])
                for g in range(4):
                    h = 4 * kvh + g
                    hf = h % 2
                    for kb in range(nkb):
                        vi = nb - 1 + kb if nb > 0 else 0
                        S.op("pe", lambda e, g=g, kb=kb, hf=hf, vi=vi, kvh=kvh, nkb=nkb, PTs=PTs, pO=pO: e.matmul(
                            PS[pO][64 * hf:64 * hf + 64, (g // 2) * 128:(g // 2) * 128 + 128],
                            lhsT=VT[:, vi, kvh * 64:kvh * 64 + 64], rhs=PTs[:, g, kb, :],
                            start=(kb == 0), stop=(kb == nkb - 1), tile_position=(0, 64 * hf)),
                            reads=spr(128 * vi, 128 * vi + 128) + B["rPT"], writes=[PR(pO)], sig=(g == 3 and kb == nkb - 1))
                S.op("act", lambda e, kvh=kvh, nb=nb, pO=pO: e.activation(
                    out=MO[:, 4 + 2 * kvh:4 + 2 * kvh + 2, 128 * nb:128 * nb + 128],
                    in_=PS[pO][:, 0:256].rearrange("p (c q) -> p c q", c=2), func=AF.Copy),
                    reads=[PR(pO)], writes=[MR(4 + 2 * kvh, t), MR(5 + 2 * kvh, t)])

            st_scores(0)
            st_scores(1)
            st_softmax(0, "A")
            for it in range(32):
                if it + 1 < 32:
                    st_softmax(it + 1, "A")
                if it + 2 < 32:
                    st_scores(it + 2)
                st_softmax(it, "B")
                st_pv(it)

        def attn_sample(l, QT, KD, VT, KVO):
            Sf = SPR[:, 3200:3200 + 2048].bitcast(F32).rearrange("p (g k) -> p g k", g=4)
            Pb = SPR[:, 5248:5248 + 1024].rearrange("p (g k) -> p g k", g=4)
            PTs = SPR[:, 6272:6272 + 512].rearrange("p (g q) -> p g q", g=4)
            PTn = SPR[:, 6784:6784 + 512].rearrange("p (g q) -> p g q", g=4)
            KTs = SPR[:, 8320:8320 + 4 * 136].rearrange("p (s k) -> p s k", s=4)
            KCb = SPR[:, 8896:8896 + 128]
            VNa = [SPR[:, 7296:8320], SPR[:, 9024:10048]]
            rVN = spr(7296, 8320) + spr(9024, 10048)
            for s in range(16):
                S.dma("sp", lambda e, s=s: e.dma_start(out=dr["vs"][l, s, 120:128, :], in_=KVO[8 * s:8 * s + 8, 1, 0:128]), "d_out", reads=spr(2688, 3200))
                S.dma("sp", lambda e, s=s: e.dma_start(out=dr["ks"][l, s, 120:128, :], in_=KVO[8 * s:8 * s + 8, 1, 128:256]), "d_out", reads=spr(2688, 3200))
            for s4 in range(4):
                klo = 2048 if s4 % 2 == 0 else 4096
                vlo = klo + 1024
                slo = 8192 if s4 % 2 == 0 else 6144
                KC4 = RG[:, klo:klo + 1024].bitcast(F32).rearrange("p (s f) -> p s f", s=4)
                VC4 = RG[:, vlo:vlo + 1024].bitcast(F32).rearrange("p (s f) -> p s f", s=4)
                VCs = RG[:, slo:slo + 512].rearrange("p (s f) -> p s f", s=4)
                rKC, rVC, rVS = rg(klo, klo + 1024), rg(vlo, vlo + 1024), rg(slo, slo + 512)
                S.dma("sp", lambda e, s4=s4, KC4=KC4: e.dma_start(out=KC4, in_=dr["ck"][l, 4 * s4:4 * s4 + 4].rearrange("s k f -> k s f")), "d_kc", writes=rKC)
                S.dma("sp", lambda e, s4=s4, VC4=VC4: e.dma_start(out=VC4, in_=dr["cv"][l, 4 * s4:4 * s4 + 4].rearrange("s k f -> k s f")), "d_vc", writes=rVC)
                S.op("dve", lambda e, VCs=VCs, VC4=VC4: e.tensor_copy(out=VCs, in_=VC4), reads=rVC, writes=rVS)
                S.dma("sp", lambda e, s4=s4, KC4=KC4: e.dma_start(out=dr["ks"][l, 4 * s4:4 * s4 + 4, 0:120, :].rearrange("s k f -> k s f"), in_=KC4[8:128, :, :]), "d_out", reads=rKC)
                S.dma("sp", lambda e, s4=s4, VC4=VC4: e.dma_start(out=dr["vs"][l, 4 * s4:4 * s4 + 4, 0:120, :].rearrange("s k f -> k s f"), in_=VC4[8:128, :, :]), "d_out", reads=rVC)
                for kvh in range(2):
                    po = 3 + kvh
                    for sl in range(4):
                        s = s4 * 4 + sl
                        klo_ = 8896 if sl % 2 == 0 else 10048
                        KCb2 = SPR[:, klo_:klo_ + 128]
                        rKb = spr(klo_, klo_ + 128)
                        pk = 5 if sl % 2 == 0 else 6
                        S.op("dve", lambda e, sl=sl, kvh=kvh, KC4=KC4, KCb2=KCb2: e.tensor_copy(
                            out=KCb2[:, :].rearrange("p (d f) -> p d f", d=2),
                            in_=KC4[:, sl, kvh * 64:kvh * 64 + 64].unsqueeze(1).to_broadcast([128, 2, 64])),
                            reads=rKC, writes=rKb)
                        ptk = PS[pk][:, :].bitcast(BF16)
                        S.op("pe", lambda e, ptk=ptk, KCb2=KCb2: e.transpose(ptk[:, 0:128], KCb2[:, 0:128], identB[:]),
                             reads=rKb + [R("identB")], writes=[PR(pk)])
                        S.op("act", lambda e, sl=sl, ptk=ptk: e.activation(out=KTs[:, sl, 0:128], in_=ptk[:, 0:128], func=AF.Copy),
                             reads=[PR(pk)], writes=spr(8320 + 136 * sl, 8320 + 136 * sl + 128))
                        S.op("act", lambda e, sl=sl, s=s, kvh=kvh: e.activation(out=KTs[:, sl, 128:136], in_=KD[:, kvh, 2048 + 8 * s:2048 + 8 * s + 8], func=AF.Copy),
                             reads=[R1r(4 + kvh, 4)], writes=spr(8320, 8864))
                    for sl in range(4):
                        s = s4 * 4 + sl
                        for g in range(4):
                            h = 4 * kvh + g
                            ch, hf = h // 2, h % 2
                            pp = hf
                            S.op("pe", lambda e, ch=ch, hf=hf, pp=pp, g=g, s=s, sl=sl: e.matmul(
                                PS[pp][32 * sl:32 * sl + 8, (g // 2) * 256:(g // 2) * 256 + 136],
                                lhsT=QT[64 * hf:64 * hf + 64, ch, 2048 + 8 * s:2048 + 8 * s + 8],
                                rhs=KTs[64 * hf:64 * hf + 64, sl, :], start=True, stop=True, tile_position=(64 * hf, 32 * sl)),
                                reads=[R1r(ch, 4)] + spr(8320, 8864), writes=[PR(pp)], sig=(sl == 3 and g >= 2))
                    softmax_block(l, kvh, 0, 1, 136, smask[:, :], Sf, spr(3200, 5248), Pb, spr(5248, 6272), 0)
                    ptp = PS[2][:, :].bitcast(BF16).rearrange("p (g q) -> p g q", g=8)
                    for g in range(4):
                        S.op("pe", lambda e, g=g, ptp=ptp: e.transpose(ptp[:, g, :], Pb[:, g, 0:128], identB[:]),
                             reads=spr(5248, 6272) + [R("identB")], writes=[PR(2)], sig=False)
                        S.op("pe", lambda e, g=g, ptp=ptp: e.transpose(ptp[0:8, 4 + g, :], Pb[:, g, 128:136], identB[:]),
                             reads=spr(5248, 6272) + [R("identB")], writes=[PR(2)], sig=(g == 3))
                    S.op("act", lambda e, ptp=ptp: e.activation(out=PTs, in_=ptp[:, 0:4, :], func=AF.Copy), reads=[PR(2)], writes=spr(6272, 7296))
                    S.op("act", lambda e, ptp=ptp: e.activation(out=PTn[0:8, :, :], in_=ptp[0:8, 4:8, :], func=AF.Copy), reads=[PR(2)], writes=spr(6272, 7296))
                    for sl in range(4):
                        s = s4 * 4 + sl
                        for g in range(4):
                            h = 4 * kvh + g
                            hf = h % 2
                            dst = PS[po][64 * hf:64 * hf + 64, (g // 2) * 128 + 8 * s:(g // 2) * 128 + 8 * s + 8]
                            S.op("pe", lambda e, dst=dst, s=s, sl=sl, g=g, kvh=kvh, hf=hf, VCs=VCs: e.matmul(
                                dst, lhsT=VCs[:, sl, kvh * 64:kvh * 64 + 64], rhs=PTs[:, g, 32 * sl:32 * sl + 8],
                                start=True, stop=False, tile_position=(0, 64 * hf)),
                                reads=rVS + spr(6272, 7296), writes=[PR(po)], sig=False)
                            S.op("pe", lambda e, dst=dst, s=s, sl=sl, g=g, kvh=kvh, hf=hf: e.matmul(
                                dst, lhsT=VNa[s // 8][0:8, (s % 8) * 128 + kvh * 64:(s % 8) * 128 + kvh * 64 + 64], rhs=PTn[0:8, g, 32 * sl:32 * sl + 8],
                                start=False, stop=True, tile_position=(0, 64 * hf)),
                                reads=rVN + spr(6272, 7296), writes=[PR(po)], sig=(sl == 3 and g == 3))
            for kvh in range(2):
                S.op("act", lambda e, kvh=kvh: e.activation(
                    out=MO[:, 4 + 2 * kvh:4 + 2 * kvh + 2, 2048:2176],
                    in_=PS[3 + kvh][:, 0:256].rearrange("p (c q) -> p c q", c=2), func=AF.Copy),
                    reads=[PR(3 + kvh)], writes=[MR(4 + 2 * kvh, 4), MR(5 + 2 * kvh, 4)])

        NSV = {}

        def sv(name):
            if name not in NSV:
                NSV[name] = len(NSV)
                assert len(NSV) <= 48, "SV overflow"
            i = NSV[name]
            return SV[:, i, :], R("sv", i)

        def tt(eng, out, a, b, op, reads, writes):
            S.op(eng, lambda e: e.tensor_tensor(out=out, in0=a, in1=b, op=op), reads=reads, writes=writes)

        def svop(out, a, b, op):
            (oa, orr), (aa, ar_), (ba, br_) = sv(out), sv(a), sv(b)
            tt("dve", oa, aa, ba, op, [ar_, br_], [orr])

        def svts(out, a, s1, op0, s2=None, op1=None):
            (oa, orr), (aa, ar_) = sv(out), sv(a)
            if op1 is None:
                S.op("dve", lambda e: e.tensor_scalar(out=oa, in0=aa, scalar1=s1, scalar2=None, op0=op0), reads=[ar_], writes=[orr])
            else:
                S.op("dve", lambda e: e.tensor_scalar(out=oa, in0=aa, scalar1=s1, scalar2=s2, op0=op0, op1=op1), reads=[ar_], writes=[orr])

        def svact(out, a, func, scale=1.0):
            (oa, orr), (aa, ar_) = sv(out), sv(a)
            S.op("act", lambda e: e.activation(out=oa, in_=aa, func=func, scale=scale), reads=[ar_], writes=[orr])

        def svcmul(outr, outi, ar_, ai_, br_, bi_):
            svop("t0", ar_, br_, ALU.mult); svop("t1", ai_, bi_, ALU.mult); svop(outr, "t0", "t1", ALU.subtract)
            svop("t0", ar_, bi_, ALU.mult); svop("t1", ai_, br_, ALU.mult); svop(outi, "t0", "t1", ALU.add)

        def load_ssm_vecs():
            for l in range(2):
                for k, src in enumerate(("ssm_lam_re", "ssm_lam_im")):
                    S.dma("sp", lambda e, l=l, k=k, src=src: e.dma_start(out=SVL[:, l * 4 + k, :], in_=dr[src][l].rearrange("(j g) n -> (g n) j", g=2),
                                                                      allow_slow_non_contiguous=True), None, writes=[R("SVL", l * 4 + k)])
                for g2 in range(2):
                    S.dma("sp", lambda e, l=l, g2=g2: e.dma_start(
                        out=SVL[64 * g2:64 * g2 + 64, l * 4 + 2, :], in_=dr["ssm_log_dt"][l].rearrange("(j g) -> g j", g=2)[g2].partition_broadcast(64),
                        allow_slow_non_contiguous=True), None, writes=[R("SVL", l * 4 + 2)])
                for sg in range(4):
                    S.dma("sp", lambda e, l=l, sg=sg: e.dma_start(out=SVL[32 * sg:32 * sg + 32, l * 4 + 3, :], in_=dr["ssm_d"][l].rearrange("(j r) -> r j", r=32),
                                                                  allow_slow_non_contiguous=True), None, writes=[R("SVL", l * 4 + 3)])

        def ssm_prefetch(l):
            Bre = RG[:, 0:512].bitcast(F32).rearrange("p (j q) -> p j q", j=16)
            Bim = RG[:, 512:1024].bitcast(F32).rearrange("p (j q) -> p j q", j=16)
            S.dma("sp", lambda e: e.dma_start(out=Bre, in_=dr["ssm_b_re"][l].rearrange("(j g) n q -> (g n) j q", g=2)), None, writes=rg(0, 512))
            S.dma("sp", lambda e: e.dma_start(out=Bim, in_=dr["ssm_b_im"][l].rearrange("(j g) n q -> (g n) j q", g=2)), None, writes=rg(512, 1024))
            for ci, src in enumerate(("ssm_c_re", "ssm_c_im")):
                for hh in range(2):
                    kk = ci * 2 + hh
                    cin = RG[:, 1024 + 256 * kk:1024 + 256 * kk + 256].bitcast(F32)
                    for j8 in range(8):
                        j = hh * 8 + j8
                        S.dma("sp", lambda e, src=src, j=j, j8=j8, cin=cin: e.dma_start(
                            out=cin[16 * j8:16 * j8 + 16, :].rearrange("p (g n) -> p g n", g=2),
                            in_=dr[src][l, 2 * j:2 * j + 2].rearrange("g p n -> p g n")), None, writes=rg(1024 + 256 * kk, 1280 + 256 * kk))

        def ssm_params(l):
            for k, nm in enumerate(("lamr", "lami", "ldt", "dcol")):
                ap, rr = sv(nm)
                S.op("dve", lambda e, ap=ap, k=k: e.tensor_copy(out=ap, in_=SVL[:, l * 4 + k, :]), reads=[R("SVL", l * 4 + k)], writes=[rr])
            Xre = R1[:, 0:2048]; Xim = R1[:, 2048:4096]; Yre = R1[:, 4096:6144]; Yim = R1[:, 6144:8192]
            Bre = R1[:, 8192:8704].bitcast(F32).rearrange("p (j q) -> p j q", j=16)
            Bim = R1[:, 8704:9216].bitcast(F32).rearrange("p (j q) -> p j q", j=16)
            Cre = R1[:, 9216:9728].bitcast(F32).rearrange("p (j q) -> p j q", j=16)
            Cim = R1[:, 9728:10240].bitcast(F32).rearrange("p (j q) -> p j q", j=16)
            Bbr = R1[:, 10240:10752].bitcast(F32).rearrange("p (j q) -> p j q", j=16)
            Bbi = R1[:, 10752:11264].bitcast(F32).rearrange("p (j q) -> p j q", j=16)
            TA = R1[:, 11264:11776].bitcast(F32).rearrange("p (j q) -> p j q", j=16)
            TB = R1[:, 11776:12288].bitcast(F32).rearrange("p (j q) -> p j q", j=16)
            CIN = R1[:, 12288:12544].bitcast(F32)
            rXre, rXim, rYre, rYim = r1(0, 2048), r1(2048, 4096), r1(4096, 6144), r1(6144, 8192)
            rB, rC, rBb, rTA, rTB, rCIN = r1(8192, 9216), r1(9216, 10240), r1(10240, 11264), r1(11264, 11776), r1(11776, 12288), r1(12288, 12544)
            STi = RG[:, 2048:6144].bitcast(F32)
            rSTi = rg(2048, 6144)
            Bre = RG[:, 0:512].bitcast(F32).rearrange("p (j q) -> p j q", j=16)
            Bim = RG[:, 512:1024].bitcast(F32).rearrange("p (j q) -> p j q", j=16)
            rB = rg(0, 1024)
            for ci, dst in enumerate((Cre, Cim)):
                for hh in range(2):
                    kk = ci * 2 + hh
                    cin = RG[:, 1024 + 256 * kk:1024 + 256 * kk + 256].bitcast(F32)
                    S.op("pe", lambda e, cin=cin: e.transpose(PS[7][:, 0:128], cin, identF[:]), reads=rg(1024 + 256 * kk, 1280 + 256 * kk) + [R("identF")], writes=[PR(7)])
                    S.op("dve", lambda e, dst=dst, hh=hh: e.tensor_copy(out=dst[:, 8 * hh:8 * hh + 8, :], in_=PS[7][:, 0:128].rearrange("p (j q) -> p j q", j=8)),
                         reads=[PR(7)], writes=rC)
            for ri, src in enumerate(("st_re", "st_im")):
                S.dma("sp", lambda e, src=src: e.dma_start(out=STi[0:16, :], in_=dr[src][l]), "d_sti", writes=rSTi)
                for j in range(16):
                    S.op("pe", lambda e, j=j: e.transpose(PS[6][:, 16 * j:16 * j + 16], STi[0:16, 128 * j:128 * j + 128], identF[0:16, 0:16]),
                         reads=rSTi + [R("identF")], writes=[PR(6)], sig=(j == 15))
                S.op("dve", lambda e, ri=ri: e.tensor_copy(out=H0[:, ri, :, :], in_=PS[6][:, 0:256].rearrange("p (j s) -> p j s", j=16)),
                     reads=[PR(6)], writes=[R("H0")])
            svact("dt", "ldt", AF.Exp)
            svop("x", "lamr", "dt", ALU.mult)
            svop("phi", "lami", "dt", ALU.mult)
            svact("mag", "x", AF.Exp)
            for _ in range(5):
                svts("t0", "phi", math.pi, ALU.is_gt, 2 * math.pi, ALU.mult)
                svop("phi", "phi", "t0", ALU.subtract)
            svts("phc", "phi", math.pi / 2, ALU.add)
            svts("t0", "phc", math.pi, ALU.is_gt, 2 * math.pi, ALU.mult)
            svop("phc", "phc", "t0", ALU.subtract)
            svact("s1", "phi", AF.Sin)
            svact("c1", "phc", AF.Sin)
            svop("a1r", "mag", "c1", ALU.mult)
            svop("a1i", "mag", "s1", ALU.mult)
            svcmul("a2r", "a2i", "a1r", "a1i", "a1r", "a1i")
            svcmul("a3r", "a3i", "a2r", "a2i", "a1r", "a1i")
            svcmul("a4r", "a4i", "a2r", "a2i", "a2r", "a2i")
            svts("na4i", "a4i", -1.0, ALU.mult)
            for k in (1, 2, 3):
                svact("t2", "x", AF.Exp, scale=-2.0 * k)
                svop("i%dr" % k, "a%dr" % k, "t2", ALU.mult)
                svop("t3", "a%di" % k, "t2", ALU.mult)
                svts("i%di" % k, "t3", -1.0, ALU.mult)
            S.op("dve", lambda e: e.memset(sv("one")[0], 1.0), writes=[sv("one")[1]])
            S.op("dve", lambda e: e.memset(sv("zero")[0], 0.0), writes=[sv("zero")[1]])
            svts("nr", "a1r", -1.0, ALU.add)
            svop("t0", "lamr", "lamr", ALU.mult); svop("t1", "lami", "lami", ALU.mult); svop("den", "t0", "t1", ALU.add)
            S.op("dve", lambda e: e.reciprocal(out=sv("den")[0], in_=sv("den")[0]), reads=[sv("den")[1]], writes=[sv("den")[1]])
            svop("t0", "nr", "lamr", ALU.mult); svop("t1", "a1i", "lami", ALU.mult); svop("t2", "t0", "t1", ALU.add); svop("cfr", "t2", "den", ALU.mult)
            svop("t0", "a1i", "lamr", ALU.mult); svop("t1", "nr", "lami", ALU.mult); svop("t2", "t0", "t1", ALU.subtract); svop("cfi", "t2", "den", ALU.mult)
            svact("r4", "x", AF.Exp, scale=4.0)
            svact("t2", "x", AF.Exp, scale=-4.0)
            svop("wr", "a4r", "t2", ALU.mult); svop("wi", "a4i", "t2", ALU.mult)

            def table(Et, n, wr, wi):
                rE = R("E", id(Et))
                S.op("dve", lambda e: e.memset(Et[:, 0, :, 0:1], 1.0), writes=[rE])
                S.op("dve", lambda e: e.memset(Et[:, 1, :, 0:1], 0.0), writes=[rE])
                m = 1
                cr, ci = wr, wi
                while m < n:
                    (cra, crr), (cia, cir) = sv(cr), sv(ci)
                    bcr = cra.unsqueeze(2).to_broadcast([128, 16, m]); bci = cia.unsqueeze(2).to_broadcast([128, 16, m])
                    tA, tBv = TA[:, :, 0:m], TB[:, :, 0:m]
                    src_c, src_s = Et[:, 0, :, 0:m], Et[:, 1, :, 0:m]
                    tt("dve", tA, src_c, bcr, ALU.mult, [rE, crr], rTA); tt("dve", tBv, src_s, bci, ALU.mult, [rE, cir], rTB)
                    tt("dve", Et[:, 0, :, m:2 * m], tA, tBv, ALU.subtract, rTA + rTB, [rE])
                    tt("dve", tA, src_c, bci, ALU.mult, [rE, cir], rTA); tt("dve", tBv, src_s, bcr, ALU.mult, [rE, crr], rTB)
                    tt("dve", Et[:, 1, :, m:2 * m], tA, tBv, ALU.add, rTA + rTB, [rE])
                    nr_, ni_ = ("pAr", "pAi") if cr != "pAr" else ("pBr", "pBi")
                    svcmul(nr_, ni_, cr, ci, cr, ci)
                    cr, ci = nr_, ni_
                    m *= 2
                return cr, ci
            w32r, w32i = table(E0, 32, "wr", "wi")
            table(E1, 16, w32r, w32i)

            def bc(name, m=16):
                a, r = sv(name)
                return a.unsqueeze(2).to_broadcast([128, 16, m]), r
            (cfr, rcfr), (cfi, rcfi) = bc("cfr"), bc("cfi")
            tt("dve", TA, Bre, cfr, ALU.mult, rB + [rcfr], rTA); tt("dve", TB, Bim, cfi, ALU.mult, rB + [rcfi], rTB)
            tt("dve", Bbr, TA, TB, ALU.subtract, rTA + rTB, rBb)
            tt("dve", TA, Bim, cfr, ALU.mult, rB + [rcfr], rTA); tt("dve", TB, Bre, cfi, ALU.mult, rB + [rcfi], rTB)
            tt("dve", Bbi, TA, TB, ALU.add, rTA + rTB, rBb)
            MRv = SPR[:, 6144:8192]; MIv = SPR[:, 8192:10240]
            for buf, rr in ((Xre, rXre), (Xim, rXim), (Yre, rYre), (Yim, rYim), (MRv, spr(6144, 8192)), (MIv, spr(8192, 10240))):
                S.op("dve", lambda e, buf=buf: e.memset(buf, 0.0), writes=rr)

            def place(dst, rdst, sr, si, rsrc, pwr, pwi, neg_im):
                dre, dim_ = dst
                rdre, rdim = rdst
                for sg in range(4):
                    (pr, rpr), (pi, rpi) = bc(pwr[sg]), bc(pwi[sg])
                    dr5 = dre.rearrange("p (j s g q) -> p j s g q", j=16, s=4, g=2)
                    di5 = dim_.rearrange("p (j s g q) -> p j s g q", j=16, s=4, g=2)
                    tt("dve", TA, sr, pr, ALU.mult, rsrc + [rpr], rTA); tt("dve", TB, si, pi, ALU.mult, rsrc + [rpi], rTB)
                    for g2 in range(2):
                        ps_ = slice(64 * g2, 64 * g2 + 64)
                        tt("dve", dr5[ps_, :, sg, g2, :], TA[ps_], TB[ps_], ALU.subtract, rTA + rTB, rdre)
                    tt("dve", TA, sr, pi, ALU.mult, rsrc + [rpi], rTA); tt("dve", TB, si, pr, ALU.mult, rsrc + [rpr], rTB)
                    for g2 in range(2):
                        ps_ = slice(64 * g2, 64 * g2 + 64)
                        if neg_im:
                            S.op("dve", lambda e, ps_=ps_, sg=sg, g2=g2, di5=di5: e.scalar_tensor_tensor(
                                out=di5[ps_, :, sg, g2, :], in0=TA[ps_], scalar=-1.0, in1=TB[ps_], op0=ALU.mult, op1=ALU.subtract),
                                reads=rTA + rTB, writes=rdim)
                        else:
                            tt("dve", di5[ps_, :, sg, g2, :], TA[ps_], TB[ps_], ALU.add, rTA + rTB, rdim)
            place((Xre, Xim), (rXre, rXim), Bbr, Bbi, rBb, ["a3r", "a2r", "a1r", "one"], ["a3i", "a2i", "a1i", "zero"], False)
            place((Yre, Yim), (rYre, rYim), Cre, Cim, rC, ["i3r", "i2r", "i1r", "one"], ["i3i", "i2i", "i1i", "zero"], True)
            place((MRv, MIv), (spr(6144, 8192), spr(8192, 10240)), Cre, Cim, rC, ["a1r", "a2r", "a3r", "a4r"], ["a1i", "a2i", "a3i", "a4i"], True)
            Tv = SPR[:, 0:2048].rearrange("p (j c) -> p j c", j=16)
            WRv = SPR[:, 2048:4096].rearrange("p (j c) -> p j c", j=16)
            WIv = SPR[:, 4096:6144].rearrange("p (j c) -> p j c", j=16)
            X3r = Xre.rearrange("p (j c) -> p j c", j=16); X3i = Xim.rearrange("p (j c) -> p j c", j=16)
            Y3r = Yre.rearrange("p (j c) -> p j c", j=16); Y3i = Yim.rearrange("p (j c) -> p j c", j=16)
            TMs = [(R1[:, 12544:12800].bitcast(F32), r1(12544, 12800)), (R1[:, 12800:13056].bitcast(F32), r1(12800, 13056))]
            dcol, rdcol = sv("dcol")
            for j in range(16):
                TM, rTM = TMs[j % 2]
                pT, pW = (7, 6) if j % 2 == 0 else (5, 4)
                S.op("pe", lambda e, j=j, pT=pT: e.matmul(PS[pT][:, 0:128], lhsT=X3r[:, j, :], rhs=Y3r[:, j, :], start=True, stop=False),
                     reads=rXre + rYre, writes=[PR(pT)], sig=False)
                S.op("pe", lambda e, j=j, pT=pT: e.matmul(PS[pT][:, 0:128], lhsT=X3i[:, j, :], rhs=Y3i[:, j, :], start=False, stop=True),
                     reads=rXim + rYim, writes=[PR(pT)])
                tt("dve", TM, PS[pT][:, 0:128], maskT[:], ALU.mult, [PR(pT), R("maskT")], rTM)
                S.op("dve", lambda e, j=j, TM=TM: e.scalar_tensor_tensor(out=Tv[:, j, :], in0=identF[:], scalar=dcol[:, j:j + 1], in1=TM,
                                                                         op0=ALU.mult, op1=ALU.add), reads=rTM + [R("identF"), rdcol], writes=spr(128 * j, 128 * j + 128))
                ptb = PS[pW][:, :].bitcast(BF16)
                S.op("pe", lambda e, j=j, ptb=ptb: e.transpose(ptb[:, 0:128], X3r[:, j, :], identB[:]), reads=rXre + [R("identB")], writes=[PR(pW)], sig=False)
                S.op("pe", lambda e, j=j, ptb=ptb: e.transpose(ptb[:, 128:256], X3i[:, j, :], identB[:]), reads=rXim + [R("identB")], writes=[PR(pW)])
                S.op("act", lambda e, j=j, ptb=ptb: e.activation(out=WRv[:, j, :], in_=ptb[:, 0:128], func=AF.Copy), reads=[PR(pW)], writes=spr(2048 + 128 * j, 2048 + 128 * j + 128))
                S.op("act", lambda e, j=j, ptb=ptb: e.activation(out=WIv[:, j, :], in_=ptb[:, 128:256], func=AF.Copy), reads=[PR(pW)], writes=spr(4096 + 128 * j, 4096 + 128 * j + 128))

        def ssm(l):
            ssm_params(l)
            Tv = SPR[:, 0:2048].rearrange("p (j c) -> p j c", j=16)
            WRv = SPR[:, 2048:4096].rearrange("p (j c) -> p j c", j=16)
            WIv = SPR[:, 4096:6144].rearrange("p (j c) -> p j c", j=16)
            MRv = SPR[:, 6144:8192].rearrange("p (j c) -> p j c", j=16)
            MIv = SPR[:, 8192:10240].rearrange("p (j c) -> p j c", j=16)
            rT, rWR, rWI, rMR, rMI = spr(0, 2048), spr(2048, 4096), spr(4096, 6144), spr(6144, 8192), spr(8192, 10240)
            NC_ = 544
            Ut = R1[:, 0:544]; rUt = r1(0, 544)
            def f32v(a, n):
                return R1[:, a:a + 2 * n].bitcast(F32), r1(a, a + 2 * n)
            ZtR, rZtR = f32v(544, 512); ZtI, rZtI = f32v(1568, 512)
            GR, rGR = f32v(2592, 512); GI, rGI = f32v(3616, 512)
            ECS = [f32v(4640, 512) + f32v(5664, 512),
                   (RG[:, 6528:7552].bitcast(F32), rg(6528, 7552), RG[:, 7552:8576].bitcast(F32), rg(7552, 8576))]
            rPT = [R("PTMP")]
            TA, rTA = f32v(6688, 512); TB, rTB = f32v(7712, 512)
            HsR = R1[:, 8736:8736 + 544]; rHsR = r1(8736, 9280)
            HsI = R1[:, 9280:9280 + 544]; rHsI = r1(9280, 9824)
            Y4 = R1[:, 9824:9824 + 2176].rearrange("p (j c) -> p j c", j=4); rY4 = r1(9824, 12000)
            SA, rSA = f32v(12000, 64)
            YG = RG[:, :].rearrange("p (c n) -> p c n", c=4)
            win = wview(l, "w_in")
            uslot, usr = wload([(lambda s: s.rearrange("p (c f) -> p c f", c=8), win[:, :, 0:512])])
            uw = uslot.rearrange("p (c f) -> p c f", c=8)
            r4, rr4 = sv("r4")
            a4r, ra4r = sv("a4r"); a4i, ra4i = sv("a4i"); na4i, rna4i = sv("na4i")
            hp_r, rhp_r = sv("hp_r"); hp_i, rhp_i = sv("hp_i")
            E0c, E0s, E1c, E1s = E0[:, 0], E0[:, 1], E1[:, 0], E1[:, 1]
            rE0, rE1 = R("E", id(E0)), R("E", id(E1))
            HTp = HT[:, :, 0:2048].rearrange("p c (n t) -> p c t n", t=4)
            HTs = HT[:, :, 2048:2176].rearrange("p c (n t) -> p c t n", t=4)
            Uts = [(R1[:, 0:544], r1(0, 544)), (R1[:, 12192:12736], r1(12192, 12736))]

            def stage_U(j):
                Ut, rUt = Uts[j % 2]
                for c in range(8):
                    for tau in range(4):
                        S.op("pe", lambda e, tau=tau, c=c, j=j: e.matmul(
                            PS[0][32 * tau:32 * tau + 32, 0:512], lhsT=uw[:, c, 32 * j:32 * j + 32], rhs=HTp[:, c, tau, :],
                            start=(c == 0), stop=(c == 7), tile_position=(0, 32 * tau)),
                            reads=[usr] + [HR(c, t) for t in range(4)], writes=[PR(0)], sig=(c == 7 and tau == 3))
                for c in range(8):
                    for tau in range(4):
                        S.op("pe", lambda e, tau=tau, c=c, j=j: e.matmul(
                            PS[7][32 * tau:32 * tau + 32, 0:32], lhsT=uw[:, c, 32 * j:32 * j + 32], rhs=HTs[:, c, tau, :],
                            start=(c == 0), stop=(c == 7), tile_position=(0, 32 * tau)),
                            reads=[usr, HR(c, 4)], writes=[PR(7)], sig=(c == 7 and tau == 3))
                S.op("act", lambda e, Ut=Ut: e.activation(out=Ut[:, 0:512], in_=PS[0][:, :], func=AF.Copy), reads=[PR(0)], writes=rUt)
                S.op("act", lambda e, Ut=Ut: e.activation(out=Ut[:, 512:544], in_=PS[7][:, 0:32], func=AF.Copy), reads=[PR(7)], writes=rUt)

            def stage_Z(j):
                Ut, rUt = Uts[j % 2]
                for (W_, rW, pb, so) in ((WRv, rWR, 1, 32), (WIv, rWI, 2, 64)):
                    S.op("pe", lambda e, W_=W_, pb=pb, j=j, Ut=Ut: e.matmul(PS[pb][:, 0:512], lhsT=W_[:, j, :], rhs=Ut[:, 0:512], start=True, stop=True),
                         reads=rW + rUt, writes=[PR(pb)])
                    S.op("pe", lambda e, W_=W_, so=so, j=j, Ut=Ut: e.matmul(PS[4][:, so:so + 32], lhsT=W_[:, j, :], rhs=Ut[:, 512:544], start=True, stop=True),
                         reads=rW + rUt, writes=[PR(4)])

            def stage_B(j):
                zs_r = PS[4][:, 32:64].rearrange("p (s h) -> p s h", h=2); zs_i = PS[4][:, 64:96].rearrange("p (s h) -> p s h", h=2)
                hsr3 = HsR[:, 512:544].rearrange("p (s h) -> p s h", h=2); hsi3 = HsI[:, 512:544].rearrange("p (s h) -> p s h", h=2)
                h0r, h0i = H0[:, 0, j, :], H0[:, 1, j, :]
                sa = [SA[:, 16 * k:16 * k + 16] for k in range(4)]
                car, cai, cnai = a4r[:, j:j + 1], a4i[:, j:j + 1], na4i[:, j:j + 1]
                rH0 = R("H0")

                def cstep(inr, ini, zr, zi, outr, outi, rin, rout, car=car, cai=cai, cnai=cnai):
                    S.op("dve", lambda e: e.tensor_scalar(out=sa[0], in0=inr, scalar1=car, scalar2=None, op0=ALU.mult), reads=rin + [ra4r], writes=rSA)
                    S.op("dve", lambda e: e.scalar_tensor_tensor(out=sa[1], in0=ini, scalar=cnai, in1=sa[0], op0=ALU.mult, op1=ALU.add), reads=rin + rSA + [rna4i], writes=rSA)
                    S.op("dve", lambda e: e.tensor_scalar(out=sa[2], in0=ini, scalar1=car, scalar2=None, op0=ALU.mult), reads=rin + [ra4r], writes=rSA)
                    S.op("dve", lambda e: e.scalar_tensor_tensor(out=sa[3], in0=inr, scalar=cai, in1=sa[2], op0=ALU.mult, op1=ALU.add), reads=rin + rSA + [ra4i], writes=rSA)
                    tt("dve", outr, sa[1], zr, ALU.add, rSA + [PR(4)], rout)
                    tt("dve", outi, sa[3], zi, ALU.add, rSA + [PR(4)], rout)
                S.op("act", lambda e, h0r=h0r: e.activation(out=hsr3[:, :, 0], in_=h0r, func=AF.Copy), reads=[rH0], writes=rHsR)
                S.op("act", lambda e, h0i=h0i: e.activation(out=hsi3[:, :, 0], in_=h0i, func=AF.Copy), reads=[rH0], writes=rHsI)
                HA, rHA = f32v(12128, 32)
                har, hai = HA[:, 0:16], HA[:, 16:32]
                cstep(h0r, h0i, zs_r[:, :, 0], zs_i[:, :, 0], har, hai, [rH0], rHA)
                S.op("act", lambda e: e.activation(out=hsr3[:, :, 1], in_=har, func=AF.Copy), reads=rHA, writes=rHsR)
                S.op("act", lambda e: e.activation(out=hsi3[:, :, 1], in_=hai, func=AF.Copy), reads=rHA, writes=rHsI)
                if not cfg.get('dbg_h0', 0):
                    cstep(har, hai, zs_r[:, :, 1], zs_i[:, :, 1], h0r, h0i, rHA, [rH0])
                EC, rEC, ES, rES = ECS[j % 2]
                tt("dve", TA, PS[1][:, 0:512], EC, ALU.mult, [PR(1)] + rEC, rTA); tt("dve", TB, PS[2][:, 0:512], ES, ALU.mult, [PR(2)] + rES, rTB)
                tt("dve", ZtR, TA, TB, ALU.add, rTA + rTB, rZtR)
                tt("dve", TA, PS[2][:, 0:512], EC, ALU.mult, [PR(2)] + rEC, rTA); tt("dve", TB, PS[1][:, 0:512], ES, ALU.mult, [PR(1)] + rES, rTB)
                tt("dve", ZtI, TA, TB, ALU.subtract, rTA + rTB, rZtI)
                r4b = r4[:, j:j + 1].to_broadcast([128, 512])
                S.op("dve", lambda e, r4b=r4b: e.tensor_tensor_scan(out=GR, data0=r4b, data1=ZtR, initial=0.0, op0=ALU.mult, op1=ALU.add),
                     reads=rZtR + [rr4], writes=rGR)
                S.op("dve", lambda e, r4b=r4b: e.tensor_tensor_scan(out=GI, data0=r4b, data1=ZtI, initial=0.0, op0=ALU.mult, op1=ALU.add),
                     reads=rZtI + [rr4], writes=rGI)
                tt("dve", TA, GR, EC, ALU.mult, rGR + rEC, rTA); tt("dve", TB, GI, ES, ALU.mult, rGI + rES, rTB)
                tt("dve", ZtR, TA, TB, ALU.subtract, rTA + rTB, rZtR)
                tt("pool", ZtI, GI, EC, ALU.mult, rGI + rEC, rZtI); tt("pool", PTMP[:, :], GR, ES, ALU.mult, rGR + rES, rPT)
                tt("pool", ZtI, ZtI, PTMP[:, :], ALU.add, rZtI + rPT, rZtI)
                S.op("act", lambda e: e.activation(out=HsR[:, 1:512], in_=ZtR[:, 0:511], func=AF.Copy), reads=rZtR, writes=rHsR)
                S.op("act", lambda e: e.activation(out=HsI[:, 1:512], in_=ZtI[:, 0:511], func=AF.Copy), reads=rZtI, writes=rHsI)
                S.op("dve", lambda e: e.memset(HsR[:, 0:1], 0.0), writes=rHsR)
                S.op("dve", lambda e: e.memset(HsI[:, 0:1], 0.0), writes=rHsI)
                S.op("act", lambda e, j=j: e.activation(out=hp_r[:, j:j + 1], in_=ZtR[:, 511:512], func=AF.Copy), reads=rZtR, writes=[rhp_r])
                S.op("act", lambda e, j=j: e.activation(out=hp_i[:, j:j + 1], in_=ZtI[:, 511:512], func=AF.Copy), reads=rZtI, writes=[rhp_i])

            def stage_E(j):
                EC, rEC, ES, rES = ECS[j % 2]
                e0c = E0c[:, j, :].unsqueeze(1).to_broadcast([128, 16, 32]); e0s = E0s[:, j, :].unsqueeze(1).to_broadcast([128, 16, 32])
                e1c = E1c[:, j, :].unsqueeze(2).to_broadcast([128, 16, 32]); e1s = E1s[:, j, :].unsqueeze(2).to_broadcast([128, 16, 32])
                v3 = lambda a: a.rearrange("p (a b) -> p a b", a=16)
                tt("pool", v3(EC), e0c, e1c, ALU.mult, [rE0, rE1], rEC); tt("pool", v3(PTMP[:, :]), e0s, e1s, ALU.mult, [rE0, rE1], rPT)
                tt("pool", EC, EC, PTMP[:, :], ALU.subtract, rEC + rPT, rEC)
                tt("pool", v3(ES), e0s, e1c, ALU.mult, [rE0, rE1], rES); tt("pool", v3(PTMP[:, :]), e0c, e1s, ALU.mult, [rE0, rE1], rPT)
                tt("pool", ES, ES, PTMP[:, :], ALU.add, rES + rPT, rES)

            def stage_C(j):
                jj = j % 4
                Ut, rUt = Uts[j % 2]
                for (lo_, n_, ob) in ((0, 512, PS[3][:, 0:512]), (512, 32, PS[7][:, 64:96])):
                    pr_ = PR(3) if lo_ == 0 else PR(7)
                    S.op("pe", lambda e, lo_=lo_, n_=n_, ob=ob, j=j, Ut=Ut: e.matmul(ob, lhsT=Tv[:, j, :], rhs=Ut[:, lo_:lo_ + n_], start=True, stop=False),
                         reads=rT + rUt, writes=[pr_], sig=False)
                    S.op("pe", lambda e, lo_=lo_, n_=n_, ob=ob, j=j: e.matmul(ob, lhsT=MRv[:, j, :], rhs=HsR[:, lo_:lo_ + n_], start=False, stop=False),
                         reads=rMR + rHsR, writes=[pr_], sig=False)
                    S.op("pe", lambda e, lo_=lo_, n_=n_, ob=ob, j=j: e.matmul(ob, lhsT=MIv[:, j, :], rhs=HsI[:, lo_:lo_ + n_], start=False, stop=True),
                         reads=rMI + rHsI, writes=[pr_])
                S.op("act", lambda e, jj=jj: e.activation(out=Y4[:, jj, 0:512], in_=PS[3][:, 0:512], func=AF.Copy), reads=[PR(3)], writes=rY4)
                S.op("act", lambda e, jj=jj: e.activation(out=Y4[:, jj, 512:544], in_=PS[7][:, 64:96], func=AF.Copy), reads=[PR(7)], writes=rY4)
                if jj == 3:
                    oc = j // 4
                    for (lo_, n_, tok0) in ((0, 512, 0), (512, 32, 2048)):
                        for tau in range(4):
                            pb = 5 + tau % 2
                            for q in range(4):
                                S.op("pe", lambda e, tau=tau, q=q, pb=pb, lo_=lo_, n_=n_: e.matmul(
                                    PS[pb][:, 0:n_], lhsT=SelC[32 * tau:32 * tau + 32, q, :], rhs=Y4[32 * tau:32 * tau + 32, q, lo_:lo_ + n_],
                                    start=(q == 0), stop=(q == 3), tile_position=(32 * tau, 0)),
                                    reads=rY4 + [R("SelC")], writes=[PR(pb)], sig=(q == 3))
                            yv, tv = TA[:, 0:n_], TB[:, 0:n_]
                            S.op("act", lambda e, pb=pb, n_=n_, yv=yv: e.activation(out=yv, in_=PS[pb][:, 0:n_], func=AF.Copy), reads=[PR(pb)], writes=rTA)
                            S.op("act", lambda e, pb=pb, n_=n_, tv=tv: e.activation(out=tv, in_=PS[pb][:, 0:n_], func=AF.Square), reads=[PR(pb)], writes=rTB)
                            S.op("act", lambda e, tv=tv: e.activation(out=tv, in_=tv, func=AF.Copy, scale=0.044715, bias=1.0), reads=rTB, writes=rTB)
                            tt("dve", tv, tv, yv, ALU.mult, rTA + rTB, rTB)
                            S.op("act", lambda e, tv=tv: e.activation(out=tv, in_=tv, func=AF.Sigmoid, scale=1.5957691216057308), reads=rTB, writes=rTB)
                            nn = n_ * 4
                            dst = YG[:, oc, tok0:tok0 + nn].rearrange("p (n t) -> p t n", t=4)[:, tau, :]
                            a_ = oc * NTOK + tok0
                            tt("dve", dst, yv, tv, ALU.mult, rTA + rTB, rg(a_, a_ + nn))

            stage_U(0)
            stage_Z(0)
            stage_E(0)
            for j in range(16):
                if j + 1 < 16:
                    stage_U(j + 1)
                    stage_E(j + 1)
                stage_B(j)
                if j + 1 < 16:
                    stage_Z(j + 1)
                stage_C(j)
            STo = R1[:, 0:4096].bitcast(F32)
            rSTo = r1(0, 4096)
            for ri, nm in enumerate(("hs_re", "hs_im")):
                for j4 in range(4):
                    for q in range(4):
                        j = j4 * 4 + q
                        S.op("pe", lambda e, ri=ri, j=j, q=q: e.transpose(PS[6][0:16, 128 * q:128 * q + 128], H0[:, ri, j, :], identF[:]),
                             reads=[R("H0"), R("identF")], writes=[PR(6)], sig=(q == 3))
                    S.op("dve", lambda e, j4=j4: e.tensor_copy(out=STo[0:16, 512 * j4:512 * j4 + 512], in_=PS[6][0:16, :]), reads=[PR(6)], writes=rSTo)
                S.dma("sp", lambda e, nm=nm: e.dma_start(out=dr[nm][l], in_=STo[0:16, :]), "d_out", reads=rSTo)
            for (nm, (hp, rhp)) in (("hp_re", (hp_r, rhp_r)), ("hp_im", (hp_i, rhp_i))):
                S.op("pe", lambda e, hp=hp: e.transpose(PS[6][0:16, 0:128], hp, identF[:]), reads=[rhp, R("identF")], writes=[PR(6)])
                S.op("dve", lambda e: e.tensor_copy(out=STo[0:16, 0:128], in_=PS[6][0:16, 0:128]), reads=[PR(6)], writes=rSTo)
                S.dma("sp", lambda e, nm=nm: e.dma_start(out=dr[nm][l].rearrange("(j c) -> j c", j=16), in_=STo[0:16, 0:128]), "d_out", reads=rSTo)

        def glu(l):
            YG = RG[:, :].rearrange("p (c n) -> p c n", c=4)
            SGt = SPR[:, 3200:4224].bitcast(F32)
            rSGt = spr(3200, 4224)
            gslot, gsr = wload([(lambda s: s[:, 0:2048].rearrange("p (c f) -> p c f", c=4),
                                 dr["ssm_w_glu"][l].rearrange("(c p) f -> p c f", p=128))])
            gv = gslot[:, 0:2048].rearrange("p (c f) -> p c f", c=4)
            k = 0
            for oc in range(4):
                for t in range(5):
                    lo, n = TT[t]
                    pb = k % 2
                    for c in range(4):
                        S.op("pe", lambda e, c=c, oc=oc, pb=pb, lo=lo, n=n: e.matmul(
                            PS[pb][:, 0:n], lhsT=gv[:, c, oc * 128:oc * 128 + 128], rhs=YG[:, c, lo:lo + n], start=(c == 0), stop=(c == 3)),
                            reads=[gsr] + rg(c * NTOK + lo, c * NTOK + lo + n), writes=[PR(pb)], sig=(c == 3))
                    S.op("act", lambda e, pb=pb, n=n: e.activation(out=SGt[:, 0:n], in_=PS[pb][:, 0:n], func=AF.Sigmoid), reads=[PR(pb)], writes=rSGt)
                    S.op("dve", lambda e, oc=oc, lo=lo, n=n: e.tensor_tensor(out=MO[:, oc, lo:lo + n], in0=YG[:, oc, lo:lo + n], in1=SGt[:, 0:n], op=ALU.mult),
                         reads=rSGt + rg(oc * NTOK + lo, oc * NTOK + lo + n), writes=[MR(oc, t)])
                    k += 1
            rmsnorm(lambda c, lo, n: MO[:, c, lo:lo + n], lambda c, t: MR(c, t), lambda c, lo, n: MO[:, c, lo:lo + n], lambda c, t: MR(c, t), 4,
                    lambda c: GN2[:, l * 2 + 0, c:c + 1])

        def final_out():
            yo = [RG[:, 0:2048].bitcast(F32), RG[:, 2048:4096].bitcast(F32)]
            SQ = [RG[:, 4096:4608], RG[:, 4608:5120]]
            RS = RG[:, 5120:6144].bitcast(F32)
            k = 0
            for t in range(5):
                lo, n = TT[t]
                for c in range(8):
                    b = c % 2
                    S.op("act", lambda e, c=c, b=b, lo=lo, n=n: e.activation(out=SQ[b][:, 0:n], in_=X[:, c, lo:lo + n], func=AF.Square),
                         reads=[XR(c, t)], writes=rg(4096 + 512 * b, 4608 + 512 * b))
                    S.op("pe", lambda e, c=c, b=b, n=n: e.matmul(PS[5][:, 0:n], lhsT=onesB[:], rhs=SQ[b][:, 0:n], start=(c == 0), stop=(c == 7)),
                         reads=rg(4096 + 512 * b, 4608 + 512 * b) + [R("onesB")], writes=[PR(5)])
                S.op("act", lambda e, n=n: e.activation(out=RS[:, 0:n], in_=PS[5][:, 0:n], func=AF.Sqrt, scale=1.0 / D, bias=EPSB[:, 0:1]),
                     reads=[PR(5), R("EPSB")], writes=rg(5120, 6144))
                S.op("dve", lambda e, n=n: e.reciprocal(out=RS[:, 0:n], in_=RS[:, 0:n]), reads=rg(5120, 6144), writes=rg(5120, 6144))
                for c in range(8):
                    S.op("dve", lambda e, c=c, lo=lo, n=n: e.scalar_tensor_tensor(
                        out=X[:, c, lo:lo + n], in0=X[:, c, lo:lo + n], scalar=GN[:, 8, c:c + 1], in1=RS[:, 0:n], op0=ALU.mult, op1=ALU.mult),
                        reads=[XR(c, t), R("GN")] + rg(5120, 6144), writes=[XR(c, t)])
                for sub in range(n // 128):
                    tok0 = lo + 128 * sub
                    b = k % 2
                    for h in range(2):
                        pb = 6 + h
                        for cc in range(4):
                            c = h * 4 + cc
                            S.op("pe", lambda e, c=c, cc=cc, pb=pb, tok0=tok0: e.transpose(
                                PS[pb][:, cc * 128:cc * 128 + 128], X[:, c, tok0:tok0 + 128], identF[:]),
                                reads=[XR(c, t), R("identF")], writes=[PR(pb)], sig=(cc == 3))
                        if h == 0:
                            S.op("dve", lambda e, b=b, pb=pb: e.tensor_copy(out=yo[b][:, 0:512], in_=PS[pb][:, :]), reads=[PR(pb)], writes=rg(2048 * b, 2048 * b + 2048))
                        else:
                            S.op("act", lambda e, b=b, pb=pb: e.activation(out=yo[b][:, 512:1024], in_=PS[pb][:, :], func=AF.Copy), reads=[PR(pb)], writes=rg(2048 * b, 2048 * b + 2048))
                    S.dma("sp", lambda e, b=b, tok0=tok0: e.dma_start(out=dr["y"][tok0:tok0 + 128, :], in_=yo[b]), ("d_yo", b), reads=rg(2048 * b, 2048 * b + 2048))
                    k += 1

        load_consts()
        load_x()
        if cfg.get("ssm", 1) and cfg.get("mix", 1):
            load_ssm_vecs()
        for l in range(2):
            if cfg.get("ssm", 1) and cfg.get("mix", 1):
                ssm_prefetch(l)
            if cfg.get("ffn", 1):
                ffn(l, "ffn1")
            if cfg.get("mix", 1):
                mixer(l)
            if cfg.get("ffn", 1):
                ffn(l, "ffn2", (lambda l=l: ple_load(l)) if cfg.get("ple", 1) else None)
            elif cfg.get("ple", 1):
                ple_load(l)
            if cfg.get("ple", 1):
                ple(l)
        final_out()
        S.wait_all("sp")
        S.emit(st)
        info = {k: len(v) for k, v in S.ops.items()}
        print("instructions per engine:", info, "sems:", len(S.cnt), flush=True)
    return nc


def make_consts():
    c = {}
    c["c_ident"] = np.eye(128, dtype=np.float32)
    i = np.arange(128)[:, None]
    j = np.arange(256)[None, :]
    valid = ((j < 128) & (j > i)) | ((j >= 128) & (j - 128 <= i))
    c["c_amask"] = np.where(valid, 0.0, NEG).astype(np.float32)
    r = np.arange(128)[:, None] % 32
    j = np.arange(136)[None, :]
    valid = (r < 8) & (j > r) & (j <= r + 128)
    c["c_smask"] = np.where(valid | (r >= 8), 0.0, NEG).astype(np.float32)
    row_tau = (np.arange(128) // 32)[:, None]
    col_tau = (np.arange(128) // 32)[None, :]
    c["c_maskT"] = (col_tau >= row_tau).astype(np.float32)
    sel = np.zeros((128, 4, 128), np.float32)
    for p in range(128):
        for jj in range(4):
            sel[p, jj, 32 * jj + (p % 32)] = 1.0
    c["c_sel"] = sel.reshape(128, 512)
    return c


CFG = dict(ffn=1, mix=1, ple=1, ssm=1, attn=1)
_NC_CACHE = {}


def kernel(**inputs):
    cfg = dict(CFG)
    key = tuple(sorted(cfg.items()))
    if key not in _NC_CACHE:
        _NC_CACHE[key] = build(cfg)
    nc = _NC_CACHE[key]
    f = lambda a: np.ascontiguousarray(np.asarray(a, dtype=np.float32))
    consts = make_consts()
    shared = {n: f(inputs[n]) for n in list(WSHAPES) + list(VNAMES)}
    shared.update(consts)
    in_maps = []
    for b in range(NCORES):
        m = dict(shared)
        sl = slice(16 * b, 16 * b + 16)
        m["x_p"] = f(inputs["x_prompt"][b])
        m["x_s"] = f(inputs["x_sample"][sl]).reshape(128, D)
        m["p_p"] = f(inputs["p_prompt"][:, b])
        m["p_s"] = f(inputs["p_sample"][:, sl]).reshape(2, 128, 256)
        m["ck"] = f(inputs["cache_k"][:, sl]).reshape(2, 16, 128, 128)
        m["cv"] = f(inputs["cache_v"][:, sl]).reshape(2, 16, 128, 128)
        m["st_re"] = f(inputs["state_ssm_re"][:, sl]).reshape(2, 16, 2048)
        m["st_im"] = f(inputs["state_ssm_im"][:, sl]).reshape(2, 16, 2048)
        in_maps.append(m)
    res = run_bass_kernel_spmd(nc, in_maps, core_ids=list(range(NCORES)))
    rs = res.results
    y = np.stack([r["y"] for r in rs])
    y_prompt = np.ascontiguousarray(y[:, :2048, :])
    y_sample = np.ascontiguousarray(y[:, 2048:, :].reshape(128, 8, D))
    kp = np.stack([r["kp"] for r in rs], axis=1).reshape(2, 8, 128, 2, 64)
    vp = np.stack([r["vp"] for r in rs], axis=1).reshape(2, 8, 128, 2, 64)
    hp_re = np.stack([r["hp_re"] for r in rs], axis=1).reshape(2, 8, 32, 64)
    hp_im = np.stack([r["hp_im"] for r in rs], axis=1).reshape(2, 8, 32, 64)
    ks = np.concatenate([r["ks"] for r in rs], axis=1).reshape(2, 128, 128, 2, 64)
    vs = np.concatenate([r["vs"] for r in rs], axis=1).reshape(2, 128, 128, 2, 64)
    hs_re = np.concatenate([r["hs_re"] for r in rs], axis=1).reshape(2, 128, 32, 64)
    hs_im = np.concatenate([r["hs_im"] for r in rs], axis=1).reshape(2, 128, 32, 64)
    return (y_prompt, y_sample, kp, vp, hp_re, hp_im, ks, vs, hs_re, hs_im)
```
